# Optimizing a Trainium2 kernel written in Bass

```python
import math
import jax, jax.numpy as jnp
from jax import lax
import numpy as np

D_MODEL = 2048
BATCH = 8
SEQ = 2048
DEPTH = 4
DEC_BATCH = 4
DEC_SEQ = 4096
PAST_LEN = 128

GRID_W = 64
D_FF = 5632
NA_HEADS = 8
NA_HEAD_DIM = 64
NA_WIN_H = 8
NA_WIN_W = 16
NA_WIDTH = NA_HEADS * NA_HEAD_DIM
HG_HEADS = 4
HG_DK = 128
HG_DV = 128
HG_WIDTH = HG_HEADS * HG_DV
HG_CHUNK = 16
S5_GROUPS = 64
S5_GROUP_CH = 16
S5_STATE = 64
S5_WIDTH = S5_GROUPS * S5_GROUP_CH
MIX_WIDTH = NA_WIDTH + HG_WIDTH + S5_WIDTH
IN_SPLITS = [NA_WIDTH, NA_WIDTH, NA_WIDTH,
             HG_HEADS * HG_DK, HG_HEADS * HG_DK, HG_HEADS * HG_DK, HG_WIDTH, HG_WIDTH,
             S5_WIDTH]
IN_COLS = sum(IN_SPLITS)
EPS = 1e-6

kernel_name = "hybrid_bidir_encoder_na_hgrn2_s5"


def rms_norm(x, gain):
    xf = x.astype(jnp.float32)
    y = xf * lax.rsqrt(jnp.mean(xf * xf, axis=-1, keepdims=True) + EPS)
    return (y * gain.astype(jnp.float32)).astype(x.dtype)


def swiglu(h, w_gate, w_up, w_down):
    return (jax.nn.silu(h @ w_gate) * (h @ w_up)) @ w_down


def neighborhood_attention(q, k, v, rel_bias):
    bsz, t = q.shape[0], q.shape[1]
    rows = t // GRID_W
    kh = min(NA_WIN_H, rows)
    kw = NA_WIN_W
    grid = lambda a: a.reshape(bsz, rows, GRID_W, NA_HEADS, NA_HEAD_DIM)
    qg, kg, vg = grid(q), grid(k), grid(v)
    col = jnp.arange(GRID_W)
    col_start = jnp.clip(col - kw // 2, 0, GRID_W - kw)
    col_idx = col_start[:, None] + jnp.arange(kw)[None, :]
    col_bias = rel_bias[:, :, col_idx - col[:, None] + (NA_WIN_W - 1)]
    scale = NA_HEAD_DIM ** -0.5

    def one_row(r):
        rs = jnp.clip(r - kh // 2, 0, rows - kh)
        k_win = lax.dynamic_slice_in_dim(kg, rs, kh, axis=1)[:, :, col_idx]
        v_win = lax.dynamic_slice_in_dim(vg, rs, kh, axis=1)[:, :, col_idx]
        q_r = lax.dynamic_index_in_dim(qg, r, axis=1, keepdims=False)
        s = jnp.einsum('bwhd,bawkhd->bhwak', q_r, k_win).astype(jnp.float32) * scale
        dr = rs + jnp.arange(kh) - r
        bias = jnp.transpose(col_bias[:, dr + (NA_WIN_H - 1)], (0, 2, 1, 3))
        s = s + bias.astype(jnp.float32)[None]
        p = jax.nn.softmax(s.reshape(bsz, NA_HEADS, GRID_W, kh * kw), axis=-1)
        p = p.reshape(bsz, NA_HEADS, GRID_W, kh, kw).astype(v.dtype)
        return jnp.einsum('bhwak,bawkhd->bwhd', p, v_win)

    out = lax.map(one_row, jnp.arange(rows))
    return jnp.transpose(out, (1, 0, 2, 3, 4)).reshape(bsz, t, NA_WIDTH)


def hgrn2_gates(z, lb):
    bsz, t = z.shape[0], z.shape[1]
    zf = z.astype(jnp.float32).reshape(bsz, t, HG_HEADS, HG_DK)
    lb = lb.astype(jnp.float32).reshape(HG_HEADS, HG_DK)
    log_f = jnp.logaddexp(jnp.log(lb), jnp.log1p(-lb) + jax.nn.log_sigmoid(zf))
    k = (1.0 - lb) * jax.nn.sigmoid(-zf)
    return log_f, k


def hgrn2_chunk_scan(q, log_f, k, v):
    bsz, t = q.shape[0], q.shape[1]
    n = t // HG_CHUNK

    def chunks(a):
        return jnp.transpose(a.reshape(bsz, n, HG_CHUNK, HG_HEADS, a.shape[-1]), (0, 3, 1, 2, 4))

    q, log_f, k, v = chunks(q), chunks(log_f), chunks(k), chunks(v)
    b = jnp.cumsum(log_f, axis=3)
    lower = jnp.tril(jnp.ones((HG_CHUNK, HG_CHUNK), dtype=bool))
    diff = b[:, :, :, :, None, :] - b[:, :, :, None, :, :]
    decay = jnp.exp(jnp.where(lower[:, :, None], diff, -jnp.inf))
    scores = jnp.einsum('bhntd,bhntsd,bhnsd->bhnts', q, decay, k)
    o_intra = jnp.einsum('bhnts,bhnsv->bhntv', scores, v)
    b_last = b[:, :, :, -1:, :]
    q_dec = q * jnp.exp(b)
    kv = jnp.einsum('bhnsd,bhnsv->bhndv', k * jnp.exp(b_last - b), v)
    chunk_decay = jnp.exp(b_last[:, :, :, 0, :])

    def step(state, inp):
        qd, dec, kvc = inp
        o = jnp.einsum('bhtd,bhdv->bhtv', qd, state)
        return dec[..., None] * state + kvc, o

    s0 = jnp.zeros((bsz, HG_HEADS, HG_DK, v.shape[-1]), jnp.float32)
    _, o_inter = lax.scan(step, s0, (jnp.moveaxis(q_dec, 2, 0), jnp.moveaxis(chunk_decay, 2, 0),
                                     jnp.moveaxis(kv, 2, 0)))
    o = o_intra + jnp.moveaxis(o_inter, 0, 2)
    return jnp.transpose(o, (0, 2, 3, 1, 4)).reshape(bsz, t, HG_HEADS, v.shape[-1])


def _complex_affine_combine(e1, e2):
    a1r, a1i, b1r, b1i = e1
    a2r, a2i, b2r, b2i = e2
    ar = a1r * a2r - a1i * a2i
    ai = a1r * a2i + a1i * a2r
    br = a2r * b1r - a2i * b1i + b2r
    bi = a2r * b1i + a2i * b1r + b2i
    return ar, ai, br, bi


def s5_direction(u_t, a_re, a_im, log_dt, b_re, b_im, c_re, c_im, reverse):
    t = u_t.shape[0]
    dt = jnp.exp(log_dt)[:, None]
    mag = jnp.exp(dt * a_re)
    ang = dt * a_im
    ab_re = mag * jnp.cos(ang)
    ab_im = mag * jnp.sin(ang)
    den = a_re * a_re + a_im * a_im
    nr = ab_re - 1.0
    f_re = (nr * a_re + ab_im * a_im) / den
    f_im = (ab_im * a_re - nr * a_im) / den
    bb_re = f_re[..., None] * b_re - f_im[..., None] * b_im
    bb_im = f_re[..., None] * b_im + f_im[..., None] * b_re
    bu_re = jnp.einsum('tbgc,gpc->tbgp', u_t, bb_re)
    bu_im = jnp.einsum('tbgc,gpc->tbgp', u_t, bb_im)
    shape = (t, 1) + ab_re.shape
    a_r = jnp.broadcast_to(ab_re, shape)
    a_i = jnp.broadcast_to(ab_im, shape)
    _, _, x_re, x_im = lax.associative_scan(_complex_affine_combine, (a_r, a_i, bu_re, bu_im),
                                            reverse=reverse, axis=0)
    return jnp.einsum('tbgp,gcp->btgc', x_re, c_re) - jnp.einsum('tbgp,gcp->btgc', x_im, c_im)


def encoder_layer(x, p, l, lb_l):
    f32 = jnp.float32
    bsz, t = x.shape[0], x.shape[1]
    h = rms_norm(x, p['ffn1_norm'][l])
    x = x + 0.5 * swiglu(h, p['ffn1_w_gate'][l], p['ffn1_w_up'][l], p['ffn1_w_down'][l])
    h = rms_norm(x, p['mix_norm'][l])
    z = h @ p['w_in'][l]
    offs = [int(o) for o in np.cumsum(IN_SPLITS)[:-1]]
    zq, zk, zv, hq, hf_f, hf_b, hi, hg, su = jnp.split(z, offs, axis=-1)
    qa = rms_norm(zq.reshape(bsz, t, NA_HEADS, NA_HEAD_DIM), p['q_norm'][l])
    ka = rms_norm(zk.reshape(bsz, t, NA_HEADS, NA_HEAD_DIM), p['k_norm'][l])
    va = zv.reshape(bsz, t, NA_HEADS, NA_HEAD_DIM)
    a_out = rms_norm(neighborhood_attention(qa, ka, va, p['rel_bias'][l]), p['attn_out_norm'][l]).astype(x.dtype)
    q_h = jax.nn.silu(hq.astype(f32)).reshape(bsz, t, HG_HEADS, HG_DK)
    v_h = hi.astype(f32).reshape(bsz, t, HG_HEADS, HG_DV)
    logf_f, k_f = hgrn2_gates(hf_f, lb_l[0])
    logf_b, k_b = hgrn2_gates(hf_b, lb_l[1])
    flip = lambda a: jnp.flip(a, axis=1)
    o_h = hgrn2_chunk_scan(q_h, logf_f, k_f, v_h) + flip(
        hgrn2_chunk_scan(flip(q_h), flip(logf_b), flip(k_b), flip(v_h)))
    o_h = rms_norm(o_h, p['hg_out_norm'][l].reshape(HG_HEADS, HG_DV)).reshape(bsz, t, HG_WIDTH)
    b_out = (o_h * jax.nn.silu(hg.astype(f32))).astype(x.dtype)
    u = su.astype(f32).reshape(bsz, t, S5_GROUPS, S5_GROUP_CH)
    u_t = jnp.transpose(u, (1, 0, 2, 3))
    dirs = []
    for d in range(2):
        dirs.append(s5_direction(u_t, p['s5_a_re'][l, d].astype(f32), p['s5_a_im'][l, d].astype(f32),
                                 p['s5_log_dt'][l, d].astype(f32), p['s5_b_re'][l, d].astype(f32),
                                 p['s5_b_im'][l, d].astype(f32), p['s5_c_re'][l, d].astype(f32),
                                 p['s5_c_im'][l, d].astype(f32), reverse=(d == 1)))
    y = dirs[0] + dirs[1] + u * p['s5_d'][l].astype(f32).reshape(S5_GROUPS, S5_GROUP_CH)
    y = jax.nn.gelu(y.reshape(bsz, t, S5_WIDTH))
    y = y * jax.nn.sigmoid(y @ p['s5_w_glu'][l].astype(f32) + p['s5_b_glu'][l].astype(f32))
    c_out = rms_norm(y, p['s5_out_norm'][l]).astype(x.dtype)
    mix = jnp.concatenate([a_out, b_out, c_out], axis=-1) @ p['w_out'][l]
    x = x + mix
    h = rms_norm(x, p['ffn2_norm'][l])
    x = x + 0.5 * swiglu(h, p['ffn2_w_gate'][l], p['ffn2_w_up'][l], p['ffn2_w_down'][l])
    return rms_norm(x, p['final_norm'][l])


def encoder_trunk(x, p, lb):
    for l in range(DEPTH):
        x = encoder_layer(x, p, l, lb[l])
    return x


def setup_inputs(seed: int = 0) -> dict:
    key = jax.random.key(seed)
    ks = jax.random.split(key, 32)
    f32 = jnp.float32
    L, D, F = DEPTH, D_MODEL, D_FF
    G, P, Gc = S5_GROUPS, S5_STATE, S5_GROUP_CH

    def nrm(k, shape, scale):
        return jax.random.normal(k, shape, f32) * scale

    def gain(k, shape):
        return 1.0 + 0.02 * jax.random.normal(k, shape, f32)

    n_idx = jnp.arange(P, dtype=f32)
    return {
        'x_prompt': nrm(ks[0], (BATCH, SEQ, D), 1.0),
        'x_sample': nrm(ks[1], (DEC_BATCH, DEC_SEQ, D), 1.0),
        'ffn1_norm': gain(ks[2], (L, D)),
        'ffn1_w_gate': nrm(ks[3], (L, D, F), D ** -0.5),
        'ffn1_w_up': nrm(ks[4], (L, D, F), D ** -0.5),
        'ffn1_w_down': nrm(ks[5], (L, F, D), F ** -0.5),
        'mix_norm': gain(ks[6], (L, D)),
        'w_in': nrm(ks[7], (L, D, IN_COLS), D ** -0.5),
        'q_norm': gain(ks[8], (L, NA_HEAD_DIM)),
        'k_norm': gain(ks[9], (L, NA_HEAD_DIM)),
        'rel_bias': nrm(ks[10], (L, NA_HEADS, 2 * NA_WIN_H - 1, 2 * NA_WIN_W - 1), 0.02),
        'attn_out_norm': gain(ks[11], (L, NA_WIDTH)),
        'hg_lb_logits': nrm(ks[12], (L, 2, HG_HEADS * HG_DK), 1.0),
        'hg_out_norm': gain(ks[13], (L, HG_WIDTH)),
        's5_a_re': -0.5 * jnp.exp(nrm(ks[14], (L, 2, G, P), 0.05)),
        's5_a_im': jnp.pi * n_idx + nrm(ks[15], (L, 2, G, P), 0.01),
        's5_log_dt': jax.random.uniform(ks[16], (L, 2, G), f32, math.log(1e-3), math.log(1e-1)),
        's5_b_re': nrm(ks[17], (L, 2, G, P, Gc), (2 * Gc) ** -0.5),
        's5_b_im': nrm(ks[18], (L, 2, G, P, Gc), (2 * Gc) ** -0.5),
        's5_c_re': nrm(ks[19], (L, 2, G, Gc, P), (2 * P) ** -0.5),
        's5_c_im': nrm(ks[20], (L, 2, G, Gc, P), (2 * P) ** -0.5),
        's5_d': nrm(ks[21], (L, S5_WIDTH), 1.0),
        's5_w_glu': nrm(ks[22], (L, S5_WIDTH, S5_WIDTH), S5_WIDTH ** -0.5),
        's5_b_glu': nrm(ks[23], (L, S5_WIDTH), 0.01),
        's5_out_norm': gain(ks[24], (L, S5_WIDTH)),
        'w_out': nrm(ks[25], (L, MIX_WIDTH, D), MIX_WIDTH ** -0.5),
        'ffn2_norm': gain(ks[26], (L, D)),
        'ffn2_w_gate': nrm(ks[27], (L, D, F), D ** -0.5),
        'ffn2_w_up': nrm(ks[28], (L, D, F), D ** -0.5),
        'ffn2_w_down': nrm(ks[29], (L, F, D), F ** -0.5),
        'final_norm': gain(ks[30], (L, D)),
    }


def reference(x_prompt, x_sample, ffn1_norm, ffn1_w_gate, ffn1_w_up, ffn1_w_down, mix_norm, w_in,
              q_norm, k_norm, rel_bias, attn_out_norm, hg_lb_logits, hg_out_norm,
              s5_a_re, s5_a_im, s5_log_dt, s5_b_re, s5_b_im, s5_c_re, s5_c_im, s5_d, s5_w_glu, s5_b_glu,
              s5_out_norm, w_out, ffn2_norm, ffn2_w_gate, ffn2_w_up, ffn2_w_down, final_norm):
    p = dict(ffn1_norm=ffn1_norm, ffn1_w_gate=ffn1_w_gate, ffn1_w_up=ffn1_w_up, ffn1_w_down=ffn1_w_down,
             mix_norm=mix_norm, w_in=w_in, q_norm=q_norm, k_norm=k_norm, rel_bias=rel_bias,
             attn_out_norm=attn_out_norm, hg_out_norm=hg_out_norm,
             s5_a_re=s5_a_re, s5_a_im=s5_a_im, s5_log_dt=s5_log_dt, s5_b_re=s5_b_re, s5_b_im=s5_b_im,
             s5_c_re=s5_c_re, s5_c_im=s5_c_im, s5_d=s5_d, s5_w_glu=s5_w_glu, s5_b_glu=s5_b_glu,
             s5_out_norm=s5_out_norm, w_out=w_out, ffn2_norm=ffn2_norm, ffn2_w_gate=ffn2_w_gate,
             ffn2_w_up=ffn2_w_up, ffn2_w_down=ffn2_w_down, final_norm=final_norm)
    lb = jnp.cumsum(jax.nn.softmax(hg_lb_logits.astype(jnp.float32), axis=0), axis=0)
    lb = lb - lb[0:1]
    y_prompt = encoder_trunk(x_prompt, p, lb)
    y_sample = encoder_trunk(x_sample, p, lb)
    return (y_prompt, y_sample)
```

```python
import contextlib
import numpy as np
import concourse.bass as bass
import concourse.mybir as mybir
from concourse.bass_utils import run_bass_kernel_spmd

F32 = mybir.dt.float32
BF16 = mybir.dt.bfloat16
I32 = mybir.dt.int32
AF = mybir.ActivationFunctionType
ALU = mybir.AluOpType
AX = mybir.AxisListType

D = 2048
FF = 5632
NKC = 16
NFC = 44
TT = 512
INC = 5120
EPS = 1e-6
NEG = -30000.0
STOP = None
NBLIM = None
PI = float(np.pi)


class Res:
    __slots__ = ("w", "r")

    def __init__(self):
        self.w = None
        self.r = {}


class Sched:
    def __init__(self, nc, es, n_dma=14):
        self.nc = nc
        self.eng = {"pe": nc.tensor, "act": nc.scalar, "dve": nc.vector, "pool": nc.gpsimd, "sp": nc.sync}
        self.sem = {}
        self.cnt = {}
        for e in ("pe", "act", "dve", "pool"):
            self.sem[e] = es.enter_context(nc.semaphore("s_" + e))
            self.cnt[e] = 0
        self.dq = {"sp": [], "pool": []}
        self.dq_next = {"sp": 0, "pool": 0}
        for q in ("sp", "pool"):
            for i in range(n_dma):
                pid = "d_%s_%d" % (q, i)
                self.sem[pid] = es.enter_context(nc.semaphore(pid))
                self.cnt[pid] = 0
                self.dq[q].append(pid)
        self.seen = {e: {} for e in self.eng}
        self.n_ins = 0

    def _wait(self, e, pid, val):
        if val > 0 and self.seen[e].get(pid, 0) < val:
            self.eng[e].wait_ge(self.sem[pid], val)
            self.seen[e][pid] = val

    def _deps(self, e, reads, writes):
        deps = {}
        for b in reads:
            if b.w is not None:
                p, v = b.w
                if deps.get(p, 0) < v:
                    deps[p] = v
        for b in writes:
            if b.w is not None:
                p, v = b.w
                if deps.get(p, 0) < v:
                    deps[p] = v
            for p, v in b.r.items():
                if deps.get(p, 0) < v:
                    deps[p] = v
        for p, v in deps.items():
            if p == e and e == "pe":
                continue
            self._wait(e, p, v)

    def _record(self, pid, val, reads, writes):
        for b in reads:
            if b.r.get(pid, 0) < val:
                b.r[pid] = val
        for b in writes:
            b.w = (pid, val)
            b.r = {}

    def op(self, e, fn, reads=(), writes=(), inc=True):
        self._deps(e, reads, writes)
        ins = fn()
        tick = self.cnt[e] + 1
        if inc:
            ins.then_inc(self.sem[e], 1)
            self.cnt[e] = tick
        self._record(e, tick, reads, writes)
        self.n_ins += 1
        return ins

    def dma(self, q, out, in_, reads=(), writes=(), **kw):
        self._deps(q, reads, writes)
        i = self.dq_next[q]
        self.dq_next[q] = (i + 1) % len(self.dq[q])
        pid = self.dq[q][i]
        prev = self.cnt[pid]
        self._wait(q, pid, prev)
        ins = self.eng[q].dma_start(out=out, in_=in_, **kw)
        ins.then_inc(self.sem[pid], 16)
        self.cnt[pid] = prev + 16
        self._record(pid, prev + 16, reads, writes)
        self.n_ins += 1
        return ins

    def barrier(self):
        for e in self.eng:
            for pid, v in self.cnt.items():
                if pid == e and e == "pe":
                    continue
                self._wait(e, pid, v)


class Rot:
    def __init__(self, items):
        self.items = items
        self.i = 0

    def next(self):
        it = self.items[self.i]
        self.i = (self.i + 1) % len(self.items)
        return it


WSPECS = [
    ("ffn1_norm", (D,)), ("ffn1_w_gate", (D, FF)), ("ffn1_w_up", (D, FF)), ("ffn1_w_down", (FF, D)),
    ("mix_norm", (D,)), ("w_in", (D, INC)), ("q_norm", (64,)), ("k_norm", (64,)),
    ("attn_out_norm", (512,)), ("hg_lb_logits", (2, 512)), ("hg_out_norm", (512,)),
    ("s5_a_re", (2, 64, 64)), ("s5_a_im", (2, 64, 64)), ("s5_log_dt", (2, 64)),
    ("s5_b_re", (2, 64, 64, 16)), ("s5_b_im", (2, 64, 64, 16)), ("s5_c_re", (2, 64, 16, 64)),
    ("s5_c_im", (2, 64, 16, 64)), ("s5_d", (1024,)), ("s5_w_glu", (1024, 1024)), ("s5_b_glu", (1024,)),
    ("s5_out_norm", (1024,)), ("w_out", (D, D)), ("ffn2_norm", (D,)), ("ffn2_w_gate", (D, FF)),
    ("ffn2_w_up", (D, FF)), ("ffn2_w_down", (FF, D)), ("final_norm", (D,)),
]


def host_consts(T, L, rel_bias, link):
    c = {}
    c["c_ident"] = np.eye(128, dtype=np.float32)
    c["c_anti"] = np.eye(128, dtype=np.float32)[::-1].copy()
    bo = np.zeros((128, 128), np.float32)
    bo[:64, :64] = 1.0
    bo[64:, 64:] = 1.0
    c["c_blockones"] = bo
    s = np.arange(128)
    same = (s[:, None] // 16) == (s[None, :] // 16)
    hm = np.zeros((2, 128, 128), np.float32)
    hm[0] = (same & (s[:, None] <= s[None, :])).astype(np.float32)
    hm[1] = (same & (s[:, None] >= s[None, :])).astype(np.float32)
    c["c_hgmask"] = hm
    c["c_chunkmask"] = ((s[:, None] // 16) == np.arange(8)[None, :]).astype(np.float32)
    rm = np.ones((128, TT), np.float32)
    rm[:, ::16] = 0.0
    c["c_resetmask"] = rm
    sp = s // 16
    sm = np.zeros((2, 128, 128), np.float32)
    sm[0] = (sp[None, :] >= sp[:, None]).astype(np.float32)
    sm[1] = (sp[:, None] >= sp[None, :]).astype(np.float32)
    c["c_s5mask"] = sm
    c["link"] = np.full((128, 1), float(link), np.float32)
    ex = np.zeros((2, 26), np.float32)
    k8 = np.arange(8)
    ex[0, 0:8] = 7 - k8; ex[1, 0:8] = k8
    ex[0, 8:16] = k8 + 1; ex[1, 8:16] = 8 - k8
    ex[0, 16:24] = k8 - 7; ex[1, 16:24] = -k8
    ex[:, 24] = 8; ex[:, 25] = 1
    c["c_s5exp"] = ex
    q = np.arange(64)
    cs = np.clip(q - 8, 0, 48)
    j = np.arange(64)
    inwin = (j[None, :] >= cs[:, None]) & (j[None, :] < cs[:, None] + 16)
    jj = np.clip(j[None, :] - q[:, None] + 15, 0, 30)
    nb = np.full((L, 8, 64, 8, 8, 64), NEG, np.float32)
    for dl in range(8):
        for a in range(8):
            rb = rel_bias[:L, :, a - dl + 7, :]
            g = rb[:, :, jj]
            g = np.where(inwin[None, None], g, np.float32(NEG))
            nb[:, dl, :, :, a, :] = np.transpose(g, (0, 2, 1, 3))
    c["nabias"] = nb.reshape(L, 8, 64, 8 * 512)
    return c


CONST_SHAPES = lambda L: {
    "c_ident": (128, 128), "c_anti": (128, 128), "c_blockones": (128, 128), "c_hgmask": (2, 128, 128),
    "c_chunkmask": (128, 8), "c_resetmask": (128, TT), "c_s5mask": (2, 128, 128), "link": (128, 1),
    "c_s5exp": (2, 26),
    "nabias": (L, 8, 64, 8 * 512),
}


def build(T, L, dbg=False, phases=("p1", "p2", "p3", "p4")):
    NT = T // TT
    NROW = T // 64
    HALF_T = T // 2
    nc = bass.Bass("TRN2", target_bir_lowering=False)

    def din(name, shape, dt=F32):
        return nc.dram_tensor(name, list(shape), dt, kind="ExternalInput").ap()

    def dscr(name, shape, dt):
        return nc.dram_tensor(name, list(shape), dt, kind=("ExternalOutput" if dbg else "Internal")).ap()

    x_in = din("x", [T, D])
    W = {n: din(n, (L,) + tuple(s)) for n, s in WSPECS}
    C = {n: din(n, s) for n, s in CONST_SHAPES(L).items()}
    y_out = nc.dram_tensor("y", [T, D], F32, kind="ExternalOutput").ap()

    xbuf = dscr("xbuf", [D, T], F32)
    qT_d = dscr("qT", [512, T], BF16)
    kT_d = dscr("kT", [512, T], BF16)
    v_d = dscr("vtok", [T, 512], BF16)
    hqT_d = dscr("hqT", [512, T], BF16)
    zf_d = dscr("zf", [2, 512, T], F32)
    vh_d = dscr("vh", [T, 512], BF16)
    hgT_d = dscr("hgT", [512, T], F32)
    su_d = dscr("sutok", [T, 1024], F32)
    ohf_d = dscr("ohf", [512, T], F32)
    yfb_d = dscr("yfb", [2, T, 1024], F32)
    mixT_d = dscr("mixT", [D, T], BF16)

    es = contextlib.ExitStack()
    with es:
        S = Sched(nc, es)

        _uid = [0]

        def uname(n):
            _uid[0] += 1
            return "t%d_%s" % (_uid[0], n)

        def gsb(name, shape, dt):
            return es.enter_context(nc.sbuf_tensor(uname(name), shape, dt))

        ident = gsb("ident", [128, 128], F32)
        ident_b = gsb("ident_b", [128, 128], BF16)
        anti_b = gsb("anti_b", [128, 128], BF16)
        ones_b = gsb("ones_b", [128, 128], BF16)
        bones_b = gsb("bones_b", [128, 128], BF16)
        epsb = gsb("epsb", [128, 1], F32)
        linkt = gsb("linkt", [128, 1], F32)
        onest = gsb("onest", [128, 1], F32)
        rconst = Res()
        tmpc = gsb("tmpc", [128, 128], F32)
        rtmpc = Res()
        S.dma("sp", ident[:], C["c_ident"], writes=[rconst])
        S.op("dve", lambda: nc.vector.tensor_copy(ident_b[:], ident[:]), reads=[rconst], writes=[rconst])
        S.dma("sp", tmpc[:], C["c_anti"], writes=[rtmpc])
        S.op("dve", lambda: nc.vector.tensor_copy(anti_b[:], tmpc[:]), reads=[rtmpc], writes=[rconst])
        S.dma("sp", tmpc[:], C["c_blockones"], reads=[], writes=[rtmpc])
        S.op("dve", lambda: nc.vector.tensor_copy(bones_b[:], tmpc[:]), reads=[rtmpc], writes=[rconst])
        S.op("dve", lambda: nc.vector.memset(ones_b[:], 1.0), writes=[rconst])
        S.op("dve", lambda: nc.vector.memset(epsb[:], EPS), writes=[rconst])
        S.op("dve", lambda: nc.vector.memset(onest[:], 1.0), writes=[rconst])
        S.dma("sp", linkt[:], C["link"], writes=[rconst])
        S.barrier()

        def load_pc(dst, src1d, res):
            S.dma("sp", dst, src1d.rearrange("(c p) -> p c", p=128), writes=[res],
                  allow_slow_non_contiguous=True)

        class TL:
            def __init__(self, ph, with_ffn=True):
                sbt = lambda n, s, d: ph.enter_context(nc.sbuf_tensor(uname(n), s, d))
                pst = lambda n, s, d: ph.enter_context(nc.psum_tensor(uname(n), s, d))
                self.sbt, self.pst = sbt, pst
                self.x = sbt("x", [128, NKC, TT], F32)
                self.rx = [Res() for _ in range(NKC)]
                self.h = sbt("h", [128, NKC, TT], BF16)
                self.rh = Res()
                self.sq = Rot([(sbt("sq%d" % i, [128, TT], BF16), Res()) for i in range(2)])
                self.rstd = sbt("rstd", [128, TT], F32)
                self.rrstd = Res()
                self.tmp = Rot([(sbt("tmp%d" % i, [128, TT], F32), Res()) for i in range(2)])
                self.wgu = [(sbt("wgu%d" % i, [128, NKC, 256], BF16), Res()) for i in range(4)]
                self.wgu_rot = Rot(self.wgu)
                self.pbank = [(pst("pb%d" % i, [128, TT], F32), Res()) for i in range(4)]
                self.pb_rot = Rot(self.pbank)
                self.pstat = pst("pstat", [128, TT], F32)
                self.rpstat = Res()
                if with_ffn:
                    self.g = sbt("g", [128, NFC, TT], BF16)
                    self.rg = [Res() for _ in range(NFC)]
                    self.wd = Rot([(sbt("wd%d" % i, [128, NFC // 2, 256], BF16), Res()) for i in range(3)])
                    self.pd = Rot([(pst("pd%d" % i, [128, TT], F32), Res()) for i in range(2)])
                self.gains = sbt("gains", [128, 4, NKC], F32)
                self.rgains = Res()

            def rmsnorm(self, gi, inplace=False):
                x, rx = self.x, self.rx
                for c in range(NKC):
                    sq, rsq = self.sq.next()
                    S.op("act", lambda: nc.scalar.activation(out=sq[:], in_=x[:, c, :], func=AF.Square),
                         reads=[rx[c]], writes=[rsq])
                    S.op("pe", lambda: nc.tensor.matmul(self.pstat[:], ones_b[:], sq[:], start=(c == 0),
                                                        stop=(c == NKC - 1)),
                         reads=[rsq, rconst], writes=[self.rpstat])
                S.op("act", lambda: nc.scalar.activation(out=self.rstd[:], in_=self.pstat[:], func=AF.Ln,
                                                         bias=epsb[:], scale=1.0 / D),
                     reads=[self.rpstat, rconst], writes=[self.rrstd])
                S.op("act", lambda: nc.scalar.activation(out=self.rstd[:], in_=self.rstd[:], func=AF.Exp,
                                                         scale=-0.5),
                     reads=[self.rrstd], writes=[self.rrstd])
                for c in range(NKC):
                    if inplace:
                        S.op("dve", lambda: nc.vector.scalar_tensor_tensor(
                            out=x[:, c, :], in0=x[:, c, :], scalar=self.gains[:, gi, c:c + 1], in1=self.rstd[:],
                            op0=ALU.mult, op1=ALU.mult),
                            reads=[rx[c], self.rrstd, self.rgains], writes=[rx[c]])
                    else:
                        S.op("dve", lambda: nc.vector.scalar_tensor_tensor(
                            out=self.h[:, c, :], in0=x[:, c, :], scalar=self.gains[:, gi, c:c + 1],
                            in1=self.rstd[:], op0=ALU.mult, op1=ALU.mult),
                            reads=[rx[c], self.rrstd, self.rgains], writes=[self.rh])

            def ffn(self, wg_ap, wu_ap, wd_ap):
                wgv = wg_ap.rearrange("(k p) f -> p k f", p=128)
                wuv = wu_ap.rearrange("(k p) f -> p k f", p=128)
                wdv = wd_ap.rearrange("(j p) d -> p j d", p=128)
                x, rx, h, rh, g, rg = self.x, self.rx, self.h, self.rh, self.g, self.rg

                def load_gu(jb):
                    tg, rtg = self.wgu_rot.next()
                    tu, rtu = self.wgu_rot.next()
                    S.dma("pool", tg[:], wgv[:, :, jb * 256:(jb + 1) * 256], writes=[rtg])
                    S.dma("pool", tu[:], wuv[:, :, jb * 256:(jb + 1) * 256], writes=[rtu])
                    return tg, rtg, tu, rtu

                def load_d(q):
                    db, jh = q // 2, q % 2
                    td, rtd = self.wd.next()
                    S.dma("pool", td[:], wdv[:, jh * 22:(jh + 1) * 22, db * 256:(db + 1) * 256], writes=[rtd])
                    return td, rtd

                NJB = FF // 256
                nxt = load_gu(0)
                dq = [load_d(0)]
                for jb in range(NJB):
                    tg, rtg, tu, rtu = nxt
                    if jb + 1 < NJB:
                        nxt = load_gu(jb + 1)
                    else:
                        dq.append(load_d(1))
                    for jj in range(2):
                        j = 2 * jb + jj
                        pg, rpg = self.pb_rot.next()
                        pu, rpu = self.pb_rot.next()
                        for k in range(NKC):
                            S.op("pe", lambda: nc.tensor.matmul(pg[:], tg[:, k, jj * 128:(jj + 1) * 128], h[:, k, :],
                                                                start=(k == 0), stop=(k == NKC - 1)),
                                 reads=[rtg, rh], writes=[rpg], inc=(k == NKC - 1))
                        for k in range(NKC):
                            S.op("pe", lambda: nc.tensor.matmul(pu[:], tu[:, k, jj * 128:(jj + 1) * 128], h[:, k, :],
                                                                start=(k == 0), stop=(k == NKC - 1)),
                                 reads=[rtu, rh], writes=[rpu], inc=(k == NKC - 1))
                        tmp, rtmp = self.tmp.next()
                        S.op("act", lambda: nc.scalar.activation(out=tmp[:], in_=pg[:], func=AF.Silu),
                             reads=[rpg], writes=[rtmp])
                        S.op("dve", lambda: nc.vector.tensor_tensor(out=g[:, j, :], in0=tmp[:], in1=pu[:],
                                                                    op=ALU.mult),
                             reads=[rtmp, rpu], writes=[rg[j]])
                NDB = D // 256
                NQ = NDB * 2
                for db in range(NDB):
                    pds = [self.pd.next() for _ in range(2)]
                    for jh in range(2):
                        q = db * 2 + jh
                        td, rtd = dq.pop(0)
                        if q + 2 < NQ:
                            dq.append(load_d(q + 2))
                        for dd in range(2):
                            pd, rpd = pds[dd]
                            for j2 in range(22):
                                j = jh * 22 + j2
                                S.op("pe", lambda: nc.tensor.matmul(pd[:], td[:, j2, dd * 128:(dd + 1) * 128],
                                                                    g[:, j, :], start=(j == 0), stop=(j == NFC - 1)),
                                     reads=[rtd, rg[j]], writes=[rpd], inc=(j2 == 21))
                    for dd in range(2):
                        i = 2 * db + dd
                        pd, rpd = pds[dd]
                        S.op("dve", lambda: nc.vector.scalar_tensor_tensor(
                            out=x[:, i, :], in0=pd[:], scalar=0.5, in1=x[:, i, :], op0=ALU.mult, op1=ALU.add),
                            reads=[rpd, rx[i]], writes=[rx[i]])

            def load_x(self, it):
                S.dma("sp", self.x[:], xbuf[:, it * TT:(it + 1) * TT].rearrange("(c p) t -> p c t", p=128),
                      writes=self.rx)

            def store_x(self, it):
                S.dma("sp", xbuf[:, it * TT:(it + 1) * TT].rearrange("(c p) t -> p c t", p=128), self.x[:],
                      reads=self.rx)

        def phase1(l):
            with contextlib.ExitStack() as ph:
                tl = TL(ph)
                sbt, pst = tl.sbt, tl.pst
                x, rx, h, rh = tl.x, tl.rx, tl.h, tl.rh
                load_pc(tl.gains[:, 0, :], W["ffn1_norm"][l], tl.rgains)
                load_pc(tl.gains[:, 1, :], W["mix_norm"][l], tl.rgains)
                qkg = sbt("qkg", [128, 2], F32)
                rqkg = Res()
                for half in range(2):
                    S.dma("sp", qkg[half * 64:(half + 1) * 64, 0:1],
                          W["q_norm"][l].rearrange("(p o) -> p o", o=1), writes=[rqkg])
                    S.dma("sp", qkg[half * 64:(half + 1) * 64, 1:2],
                          W["k_norm"][l].rearrange("(p o) -> p o", o=1), writes=[rqkg])
                S.op("dve", lambda: nc.vector.tensor_single_scalar(qkg[:, 0:1], qkg[:, 0:1], 0.125, ALU.mult),
                     reads=[rqkg], writes=[rqkg])
                stg_b = Rot([(sbt("stgb%d" % i, [128, TT], BF16), Res()) for i in range(2)])
                stg_f = Rot([(sbt("stgf%d" % i, [128, TT], F32), Res()) for i in range(2)])
                stk_b = Rot([(sbt("stkb%d" % i, [128, 2, 256], BF16), Res()) for i in range(2)])
                stk_f = Rot([(sbt("stkf%d" % i, [128, 2, 256], F32), Res()) for i in range(2)])
                if l == 0:
                    xin_rot = Rot([(sbt("xin%d" % i, [128, D], F32), Res()) for i in range(2)])
                winv = W["w_in"][l].rearrange("(k p) f -> p k f", p=128)
                for it in range(NT):
                    t0 = it * TT
                    if l == 0:
                        for tb in range(4):
                            xin, rxin = xin_rot.next()
                            S.dma("sp", xin[:], x_in[t0 + tb * 128:t0 + (tb + 1) * 128, :], writes=[rxin])
                            for cq in range(4):
                                pb, rpb = tl.pb_rot.next()
                                for i4 in range(4):
                                    c = cq * 4 + i4
                                    S.op("pe", lambda: nc.tensor.transpose(pb[:, i4 * 128:(i4 + 1) * 128],
                                                                           xin[:, c * 128:(c + 1) * 128], ident[:]),
                                         reads=[rxin, rconst], writes=[rpb], inc=(i4 == 3))
                                dsts = x[:, cq * 4:(cq + 1) * 4, tb * 128:(tb + 1) * 128]
                                srcs = pb[:].rearrange("p (c t) -> p c t", c=4)
                                wr = [rx[cq * 4 + i4] for i4 in range(4)]
                                if cq % 2 == 0:
                                    S.op("act", lambda: nc.scalar.copy(dsts, srcs), reads=[rpb], writes=wr)
                                else:
                                    S.op("dve", lambda: nc.vector.tensor_copy(dsts, srcs), reads=[rpb], writes=wr)
                    else:
                        tl.load_x(it)
                    if STOP == "x":
                        tl.store_x(it)
                        continue
                    tl.rmsnorm(0)
                    if STOP == "n":
                        tl.store_x(it)
                        continue
                    tl.ffn(W["ffn1_w_gate"][l], W["ffn1_w_up"][l], W["ffn1_w_down"][l])
                    tl.store_x(it)
                    if STOP == "f":
                        continue
                    tl.rmsnorm(1)
                    NB = INC // 256 if NBLIM is None else NBLIM
                    nxt = tl.wgu_rot.next()
                    S.dma("pool", nxt[0][:], winv[:, :, 0:256], writes=[nxt[1]])
                    for b in range(NB):
                        wb, rwb = nxt
                        if b + 1 < NB:
                            nxt = tl.wgu_rot.next()
                            S.dma("pool", nxt[0][:], winv[:, :, (b + 1) * 256:(b + 2) * 256], writes=[nxt[1]])
                        tokmajor = b in (4, 5, 12, 13, 16, 17, 18, 19)
                        if not tokmajor:
                            for jj in range(2):
                                ps, rps = tl.pb_rot.next()
                                for k in range(NKC):
                                    S.op("pe", lambda: nc.tensor.matmul(ps[:], wb[:, k, jj * 128:(jj + 1) * 128],
                                                                        h[:, k, :], start=(k == 0),
                                                                        stop=(k == NKC - 1)),
                                         reads=[rwb, rh], writes=[rps], inc=(k == NKC - 1))
                                if b < 4:
                                    c4 = 2 * (b % 2) + jj
                                    gi = 0 if b < 2 else 1
                                    dst = qT_d if b < 2 else kT_d
                                    sq, rsq = tl.sq.next()
                                    S.op("act", lambda: nc.scalar.activation(out=sq[:], in_=ps[:], func=AF.Square),
                                         reads=[rps], writes=[rsq])
                                    S.op("pe", lambda: nc.tensor.matmul(tl.pstat[:], bones_b[:], sq[:], start=True,
                                                                        stop=True),
                                         reads=[rsq, rconst], writes=[tl.rpstat])
                                    S.op("act", lambda: nc.scalar.activation(out=tl.rstd[:], in_=tl.pstat[:],
                                                                             func=AF.Ln, bias=epsb[:],
                                                                             scale=1.0 / 64),
                                         reads=[tl.rpstat, rconst], writes=[tl.rrstd])
                                    S.op("act", lambda: nc.scalar.activation(out=tl.rstd[:], in_=tl.rstd[:],
                                                                             func=AF.Exp, scale=-0.5),
                                         reads=[tl.rrstd], writes=[tl.rrstd])
                                    st, rst = stg_b.next()
                                    S.op("dve", lambda: nc.vector.scalar_tensor_tensor(
                                        out=st[:], in0=ps[:], scalar=qkg[:, gi:gi + 1], in1=tl.rstd[:],
                                        op0=ALU.mult, op1=ALU.mult),
                                        reads=[rps, tl.rrstd, rqkg], writes=[rst])
                                    S.dma("sp", dst[c4 * 128:(c4 + 1) * 128, t0:t0 + TT], st[:], reads=[rst])
                                elif b in (6, 7):
                                    c4 = 2 * (b - 6) + jj
                                    st, rst = stg_b.next()
                                    S.op("act", lambda: nc.scalar.activation(out=st[:], in_=ps[:], func=AF.Silu),
                                         reads=[rps], writes=[rst])
                                    S.dma("sp", hqT_d[c4 * 128:(c4 + 1) * 128, t0:t0 + TT], st[:], reads=[rst])
                                elif b in (8, 9, 10, 11):
                                    dr = 0 if b < 10 else 1
                                    c4 = 2 * ((b - 8) % 2) + jj
                                    st, rst = stg_f.next()
                                    S.op("act", lambda: nc.scalar.copy(st[:], ps[:]), reads=[rps], writes=[rst])
                                    S.dma("sp", zf_d[dr, c4 * 128:(c4 + 1) * 128, t0:t0 + TT], st[:], reads=[rst])
                                else:
                                    c4 = 2 * (b - 14) + jj
                                    st, rst = stg_f.next()
                                    S.op("act", lambda: nc.scalar.activation(out=st[:], in_=ps[:], func=AF.Silu),
                                         reads=[rps], writes=[rst])
                                    S.dma("sp", hgT_d[c4 * 128:(c4 + 1) * 128, t0:t0 + TT], st[:], reads=[rst])
                        else:
                            if b in (4, 5):
                                dst, c0, isf = v_d, (b - 4) * 256, False
                            elif b in (12, 13):
                                dst, c0, isf = vh_d, (b - 12) * 256, False
                            else:
                                dst, c0, isf = su_d, (b - 16) * 256, True
                            for sp2 in range(2):
                                ps, rps = tl.pb_rot.next()
                                for s2 in range(2):
                                    s = sp2 * 2 + s2
                                    for k in range(NKC):
                                        S.op("pe", lambda: nc.tensor.matmul(
                                            ps[:, s2 * 256:(s2 + 1) * 256], h[:, k, s * 128:(s + 1) * 128],
                                            wb[:, k, :], start=(k == 0), stop=(k == NKC - 1)),
                                            reads=[rwb, rh], writes=[rps], inc=(s2 == 1 and k == NKC - 1))
                                st, rst = (stk_f if isf else stk_b).next()
                                S.op("act", lambda: nc.scalar.copy(st[:], ps[:].rearrange("p (s c) -> p s c", s=2)),
                                     reads=[rps], writes=[rst])
                                S.dma("sp", dst[t0 + sp2 * 256:t0 + (sp2 + 1) * 256, c0:c0 + 256].rearrange(
                                    "(s p) c -> p s c", p=128), st[:], reads=[rst])
                S.barrier()

        def phase2(l):
            with contextlib.ExitStack() as ph:
                sbt = lambda n, s_, d: ph.enter_context(nc.sbuf_tensor(uname(n), s_, d))
                pst = lambda n, s_, d: ph.enter_context(nc.psum_tensor(uname(n), s_, d))
                NL = L
                lg = sbt("lg", [128, NL, 8], F32)
                rlg = Res()
                S.dma("sp", lg[:].rearrange("p l (d h) -> p l d h", d=2),
                      W["hg_lb_logits"].rearrange("l d (h p) -> p l d h", p=128), writes=[rlg],
                      allow_slow_non_contiguous=True)
                mx = sbt("mx", [128, 8], F32)
                lbt = sbt("lbt", [128, 8], F32)
                omlt = sbt("omlt", [128, 8], F32)
                ssum = sbt("ssum", [128, 8], F32)
                rlb = Res()
                S.op("dve", lambda: nc.vector.tensor_copy(mx[:], lg[:, 0, :]), reads=[rlg], writes=[rlb])
                for ll in range(1, NL):
                    S.op("dve", lambda: nc.vector.tensor_tensor(out=mx[:], in0=mx[:], in1=lg[:, ll, :], op=ALU.max),
                         reads=[rlg, rlb], writes=[rlb])
                for ll in range(NL):
                    S.op("dve", lambda: nc.vector.tensor_tensor(out=lg[:, ll, :], in0=lg[:, ll, :], in1=mx[:],
                                                                op=ALU.subtract), reads=[rlg, rlb], writes=[rlg])
                S.op("act", lambda: nc.scalar.activation(out=lg[:], in_=lg[:], func=AF.Exp), reads=[rlg],
                     writes=[rlg])
                S.op("dve", lambda: nc.vector.tensor_copy(ssum[:], lg[:, 0, :]), reads=[rlg], writes=[rlb])
                S.op("dve", lambda: nc.vector.memset(lbt[:], 0.0), writes=[rlb])
                for ll in range(1, NL):
                    S.op("dve", lambda: nc.vector.tensor_tensor(out=ssum[:], in0=ssum[:], in1=lg[:, ll, :],
                                                                op=ALU.add), reads=[rlg, rlb], writes=[rlb])
                    if ll <= l:
                        S.op("dve", lambda: nc.vector.tensor_tensor(out=lbt[:], in0=lbt[:], in1=lg[:, ll, :],
                                                                    op=ALU.add), reads=[rlg, rlb], writes=[rlb])
                S.op("dve", lambda: nc.vector.reciprocal(ssum[:], ssum[:]), reads=[rlb], writes=[rlb])
                S.op("dve", lambda: nc.vector.tensor_tensor(out=lbt[:], in0=lbt[:], in1=ssum[:], op=ALU.mult),
                     reads=[rlb], writes=[rlb])
                S.op("dve", lambda: nc.vector.tensor_scalar(omlt[:], lbt[:], -1.0, 1.0, ALU.mult, ALU.add),
                     reads=[rlb], writes=[rlb])
                hgain = sbt("hgain", [128, 4], F32)
                load_pc(hgain[:], W["hg_out_norm"][l], rlb)
                cf = sbt("cf", [128, 128], F32)
                rcf = Res()
                hmask = sbt("hmask", [128, 2, 128], F32)
                rhm = Res()
                for d_ in range(2):
                    S.dma("sp", hmask[:, d_, :], C["c_hgmask"][d_], writes=[rhm])
                cmask = sbt("cmask", [128, 8], F32)
                S.dma("sp", cmask[:], C["c_chunkmask"], writes=[rhm])
                rmask = sbt("rmask", [128, TT], F32)
                S.dma("sp", rmask[:], C["c_resetmask"], writes=[rhm])
                ldq = Rot([(sbt("ldq%d" % i, [128, TT], BF16), Res()) for i in range(2)])
                ldz = Rot([(sbt("ldz%d" % i, [128, TT], F32), Res()) for i in range(2)])
                ldv = Rot([(sbt("ldv%d" % i, [128, 4, 128], BF16), Res()) for i in range(2)])
                ldo = Rot([(sbt("ldo%d" % i, [128, TT], F32), Res()) for i in range(2)])
                ldg = Rot([(sbt("ldg%d" % i, [128, TT], F32), Res()) for i in range(2)])
                A = sbt("A", [128, TT], F32); rA = Res()
                B = sbt("B", [128, TT], F32); rB = Res()
                B2 = sbt("B2", [128, TT], F32); rB2 = Res()
                KK = sbt("KK", [128, TT], F32); rKK = Res()
                LF = sbt("LF", [128, TT], F32); rLF = Res()
                CUM = sbt("CUM", [128, TT], F32); rCUM = Res()
                CU2 = sbt("CU2", [128, TT], F32); rCU2 = Res()
                Qt = sbt("Qt", [128, TT], BF16); rQt = Res()
                Kt = sbt("Kt", [128, TT], BF16); rKt = Res()
                Kh = sbt("Kh", [128, TT], BF16); rKh = Res()
                dec = sbt("dec", [128, 32], F32); rdec = Res()
                AT = sbt("AT", [128, 128], BF16); rAT = Res()
                KhT = sbt("KhT", [128, 128], BF16); rKhT = Res()
                Vb = sbt("Vb", [128, 8, 128], BF16); rVb = Res()
                Sbf = sbt("Sbf", [128, 8, 128], BF16); rSbf = Res()
                osum = sbt("osum", [128, TT], F32); rosum = Res()
                sqb = sbt("sqb", [128, TT], BF16); rsqb = Res()
                rstd = sbt("rstd2", [128, TT], F32); rrstd = Res()
                bst = Rot([(sbt("bst%d" % i, [128, TT], BF16), Res()) for i in range(2)])
                hist = [[(sbt("hist%d_%d" % (h_, i), [128, 9, 128], F32), Res()) for i in range(2)]
                        for h_ in range(4)]
                ps_s = pst("ps_s", [128, TT], F32); rps_s = Res()
                ps_t = pst("ps_t", [128, 1024], BF16); rps_t = Res()
                ps_kv = [(pst("ps_kv%d" % i, [128, TT], F32), Res()) for i in range(2)]
                ps_o = pst("ps_o", [128, TT], F32); rps_o = Res()
                pstat = pst("pstat2", [128, TT], F32); rpstat = Res()
                v3 = lambda t_: t_[:].rearrange("p (n c) -> p n c", c=16)

                for dr in range(2):
                    cur = [0, 0, 0, 0]
                    for h_ in range(4):
                        S.op("dve", lambda: nc.vector.memset(hist[h_][1][0][:, 8, :], 0.0), writes=[hist[h_][1][1]])
                    tiles = list(range(NT)) if dr == 0 else list(range(NT - 1, -1, -1))
                    subs = list(range(4)) if dr == 0 else [3, 2, 1, 0]
                    for it in tiles:
                        t0 = it * TT
                        for h_ in range(4):
                            fs = slice(h_ * 128, (h_ + 1) * 128)
                            q, rq = ldq.next()
                            z, rz = ldz.next()
                            vt, rvt = ldv.next()
                            S.dma("sp", q[:], hqT_d[fs, t0:t0 + TT], writes=[rq])
                            S.dma("sp", z[:], zf_d[dr, fs, t0:t0 + TT], writes=[rz])
                            S.dma("sp", vt[:], vh_d[t0:t0 + TT, fs].rearrange("(s p) c -> p s c", p=128),
                                  writes=[rvt])
                            if dr == 1:
                                of, rof = ldo.next()
                                hg, rhg = ldg.next()
                                S.dma("sp", of[:], ohf_d[fs, t0:t0 + TT], writes=[rof])
                                S.dma("sp", hg[:], hgT_d[fs, t0:t0 + TT], writes=[rhg])
                            lbs = lbt[:, dr * 4 + h_:dr * 4 + h_ + 1]
                            oms = omlt[:, dr * 4 + h_:dr * 4 + h_ + 1]
                            S.op("act", lambda: nc.scalar.activation(out=A[:], in_=z[:], func=AF.Exp, scale=-1.0),
                                 reads=[rz], writes=[rA])
                            S.op("dve", lambda: nc.vector.tensor_single_scalar(B[:], A[:], 1.0, ALU.add),
                                 reads=[rA], writes=[rB])
                            S.op("dve", lambda: nc.vector.reciprocal(B[:], B[:]), reads=[rB], writes=[rB])
                            S.op("dve", lambda: nc.vector.scalar_tensor_tensor(
                                out=KK[:], in0=A[:], scalar=oms, in1=B[:], op0=ALU.mult, op1=ALU.mult),
                                reads=[rA, rB, rlb], writes=[rKK])
                            S.op("act", lambda: nc.scalar.activation(out=LF[:], in_=KK[:], func=AF.Ln, scale=-1.0,
                                                                     bias=onest[:]),
                                 reads=[rKK, rconst], writes=[rLF])
                            S.op("dve", lambda: nc.vector.tensor_tensor_scan(CUM[:], rmask[:], LF[:], 0.0, ALU.mult,
                                                                             ALU.add),
                                 reads=[rhm, rLF], writes=[rCUM])
                            totb = v3(CUM)[:, :, 15:16].to_broadcast([128, 32, 16])
                            if dr == 0:
                                cum, rcum = CUM, rCUM
                            else:
                                S.op("dve", lambda: nc.vector.tensor_tensor(out=v3(CU2), in0=totb, in1=v3(CUM),
                                                                            op=ALU.subtract),
                                     reads=[rCUM], writes=[rCU2])
                                S.op("dve", lambda: nc.vector.tensor_tensor(out=CU2[:], in0=CU2[:], in1=LF[:],
                                                                            op=ALU.add),
                                     reads=[rCU2, rLF], writes=[rCU2])
                                cum, rcum = CU2, rCU2
                            S.op("act", lambda: nc.scalar.activation(out=A[:], in_=cum[:], func=AF.Exp),
                                 reads=[rcum], writes=[rA])
                            S.op("dve", lambda: nc.vector.tensor_tensor(out=Qt[:], in0=q[:], in1=A[:], op=ALU.mult),
                                 reads=[rq, rA], writes=[rQt])
                            S.op("act", lambda: nc.scalar.activation(out=B[:], in_=cum[:], func=AF.Exp, scale=-1.0),
                                 reads=[rcum], writes=[rB])
                            S.op("dve", lambda: nc.vector.tensor_tensor(out=Kt[:], in0=KK[:], in1=B[:], op=ALU.mult),
                                 reads=[rKK, rB], writes=[rKt])
                            S.op("dve", lambda: nc.vector.tensor_tensor(out=v3(B2), in0=totb, in1=v3(cum),
                                                                        op=ALU.subtract),
                                 reads=[rCUM, rcum], writes=[rB2])
                            S.op("act", lambda: nc.scalar.activation(out=B2[:], in_=B2[:], func=AF.Exp),
                                 reads=[rB2], writes=[rB2])
                            S.op("dve", lambda: nc.vector.tensor_tensor(out=Kh[:], in0=KK[:], in1=B2[:],
                                                                        op=ALU.mult),
                                 reads=[rKK, rB2], writes=[rKh])
                            S.op("act", lambda: nc.scalar.activation(out=dec[:], in_=v3(CUM)[:, :, 15], func=AF.Exp),
                                 reads=[rCUM], writes=[rdec])
                            for sub in subs:
                                cs = slice(sub * 128, (sub + 1) * 128)
                                S.op("pe", lambda: nc.tensor.matmul(ps_s[:, 0:128], Kt[:, cs], Qt[:, cs], start=True,
                                                                    stop=True),
                                     reads=[rKt, rQt], writes=[rps_s])
                                S.op("dve", lambda: nc.vector.tensor_tensor(out=AT[:], in0=ps_s[:, 0:128],
                                                                            in1=hmask[:, dr, :], op=ALU.mult),
                                     reads=[rps_s, rhm], writes=[rAT])
                                S.op("pe", lambda: nc.tensor.transpose(ps_t[:, 0:128], Kh[:, cs], ident_b[:]),
                                     reads=[rKh, rconst], writes=[rps_t])
                                S.op("act", lambda: nc.scalar.copy(KhT[:], ps_t[:, 0:128]), reads=[rps_t],
                                     writes=[rKhT])
                                S.op("dve", lambda: nc.vector.tensor_tensor(
                                    out=Vb[:], in0=vt[:, sub, :].unsqueeze(1).to_broadcast([128, 8, 128]),
                                    in1=cmask[:].unsqueeze(2).to_broadcast([128, 8, 128]), op=ALU.mult),
                                    reads=[rvt, rhm], writes=[rVb])
                                for hf in range(2):
                                    S.op("pe", lambda: nc.tensor.matmul(
                                        ps_kv[hf][0][:], KhT[:],
                                        Vb[:, hf * 4:(hf + 1) * 4, :].rearrange("p a b -> p (a b)"),
                                        start=True, stop=True),
                                        reads=[rKhT, rVb], writes=[ps_kv[hf][1]])
                                hc, rhc = hist[h_][cur[h_]]
                                hp, rhp = hist[h_][1 - cur[h_]]
                                cur[h_] = 1 - cur[h_]
                                tok_lo = t0 + sub * 128
                                boundary = (dr == 0 and tok_lo == HALF_T) or (dr == 1 and tok_lo + 128 == HALF_T)
                                lk = linkt if boundary else onest
                                S.op("dve", lambda: nc.vector.tensor_scalar(hc[:, 0, :], hp[:, 8, :], lk[:, 0:1], None,
                                                                            ALU.mult),
                                     reads=[rhp, rconst], writes=[rhc])
                                for i8 in range(8):
                                    n = i8 if dr == 0 else 7 - i8
                                    kvb, rkvb = ps_kv[n // 4]
                                    S.op("dve", lambda: nc.vector.scalar_tensor_tensor(
                                        out=hc[:, i8 + 1, :], in0=hc[:, i8, :],
                                        scalar=dec[:, sub * 8 + n:sub * 8 + n + 1],
                                        in1=kvb[:, (n % 4) * 128:(n % 4 + 1) * 128], op0=ALU.mult, op1=ALU.add),
                                        reads=[rhc, rdec, rkvb], writes=[rhc])
                                S.op("act", lambda: nc.scalar.copy(Sbf[:], hc[:, 0:8, :]), reads=[rhc],
                                     writes=[rSbf])
                                S.op("pe", lambda: nc.tensor.matmul(ps_o[:, cs], vt[:, sub, :], AT[:], start=True,
                                                                    stop=False),
                                     reads=[rvt, rAT], writes=[rps_o], inc=False)
                                for i8 in range(8):
                                    n = i8 if dr == 0 else 7 - i8
                                    c0 = sub * 128 + n * 16
                                    S.op("pe", lambda: nc.tensor.matmul(ps_o[:, c0:c0 + 16], Sbf[:, i8, :],
                                                                        Qt[:, c0:c0 + 16], start=False,
                                                                        stop=(i8 == 7)),
                                         reads=[rSbf, rQt], writes=[rps_o], inc=(i8 == 7))
                            if dr == 0:
                                S.op("act", lambda: nc.scalar.copy(osum[:], ps_o[:]), reads=[rps_o], writes=[rosum])
                                S.dma("sp", ohf_d[fs, t0:t0 + TT], osum[:], reads=[rosum])
                            else:
                                S.op("dve", lambda: nc.vector.tensor_tensor(out=osum[:], in0=ps_o[:], in1=of[:],
                                                                            op=ALU.add),
                                     reads=[rps_o, rof], writes=[rosum])
                                S.op("act", lambda: nc.scalar.activation(out=sqb[:], in_=osum[:], func=AF.Square),
                                     reads=[rosum], writes=[rsqb])
                                S.op("pe", lambda: nc.tensor.matmul(pstat[:], ones_b[:], sqb[:], start=True,
                                                                    stop=True),
                                     reads=[rsqb, rconst], writes=[rpstat])
                                S.op("act", lambda: nc.scalar.activation(out=rstd[:], in_=pstat[:], func=AF.Ln,
                                                                         bias=epsb[:], scale=1.0 / 128),
                                     reads=[rpstat, rconst], writes=[rrstd])
                                S.op("act", lambda: nc.scalar.activation(out=rstd[:], in_=rstd[:], func=AF.Exp,
                                                                         scale=-0.5),
                                     reads=[rrstd], writes=[rrstd])
                                S.op("dve", lambda: nc.vector.scalar_tensor_tensor(
                                    out=hg[:], in0=hg[:], scalar=hgain[:, h_:h_ + 1], in1=rstd[:], op0=ALU.mult,
                                    op1=ALU.mult), reads=[rhg, rlb, rrstd], writes=[rhg])
                                bo, rbo = bst.next()
                                S.op("dve", lambda: nc.vector.tensor_tensor(out=bo[:], in0=osum[:], in1=hg[:],
                                                                            op=ALU.mult),
                                     reads=[rosum, rhg], writes=[rbo])
                                S.dma("sp", mixT_d[512 + h_ * 128:512 + (h_ + 1) * 128, t0:t0 + TT], bo[:],
                                      reads=[rbo])
                S.barrier()

        def phase3a(l):
            GH = 32
            NS = T // 256
            TWO_PI = 2.0 * PI
            with contextlib.ExitStack() as ph:
                sbt = lambda n, s_, d: ph.enter_context(nc.sbuf_tensor(uname(n), s_, d))
                pst = lambda n, s_, d: ph.enter_context(nc.psum_tensor(uname(n), s_, d))
                Tmat = sbt("Tmat", [128, 2, GH, 128], BF16); rTmat = Res()
                EmatT = sbt("EmatT", [128, 2, GH, 128], BF16); rEm = Res()
                Fmat = sbt("Fmat", [64, 2, 2, GH, 128], BF16); rFm = Res()
                A1 = sbt("A1", [64, 2, 2, GH], F32)
                A2 = sbt("A2", [64, 2, 2, GH], F32)
                rA12 = Res()
                link64 = linkt[0:64, 0:1]
                smask = sbt("smask", [128, 2, 128], F32); rsm = Res()
                for d_ in range(2):
                    S.dma("sp", smask[:, d_, :], C["c_s5mask"][d_], writes=[rsm])
                for gh in range(64 // GH):
                    g0 = gh * GH
                    with contextlib.ExitStack() as pp:
                        sbp = lambda n, s_, d: pp.enter_context(nc.sbuf_tensor(uname(n), s_, d))
                        psp = lambda n, s_, d: pp.enter_context(nc.psum_tensor(uname(n), s_, d))
                        Are = sbp("Are", [64, 2, GH], F32); Aim = sbp("Aim", [64, 2, GH], F32)
                        Ldt = sbp("Ldt", [64, 2, GH], F32); mt = sbp("mt", [64, 2, 26], F32)
                        rP = Res()
                        for d_ in range(2):
                            S.dma("sp", Are[:, d_, :], W["s5_a_re"][l, d_, g0:g0 + GH, :].rearrange("g p -> p g"),
                                  writes=[rP], allow_slow_non_contiguous=True)
                            S.dma("sp", Aim[:, d_, :], W["s5_a_im"][l, d_, g0:g0 + GH, :].rearrange("g p -> p g"),
                                  writes=[rP], allow_slow_non_contiguous=True)
                            S.dma("sp", Ldt[:, d_, :], W["s5_log_dt"][l, d_:d_ + 1, g0:g0 + GH].partition_broadcast(64),
                                  writes=[rP])
                        S.dma("sp", mt[:].rearrange("p d m -> p (d m)"),
                              C["c_s5exp"].rearrange("(o d) m -> o (d m)", o=1).partition_broadcast(64), writes=[rP])
                        dta = sbp("dta", [64, 2, GH], F32); ang = sbp("ang", [64, 2, GH], F32)
                        S.op("act", lambda: nc.scalar.activation(out=Ldt[:], in_=Ldt[:], func=AF.Exp), reads=[rP],
                             writes=[rP])
                        S.op("dve", lambda: nc.vector.tensor_tensor(out=dta[:], in0=Ldt[:], in1=Are[:], op=ALU.mult),
                             reads=[rP], writes=[rP])
                        S.op("dve", lambda: nc.vector.tensor_tensor(out=ang[:], in0=Ldt[:], in1=Aim[:], op=ALU.mult),
                             reads=[rP], writes=[rP])
                        SH = [64, 2, 26, GH]
                        mag = sbp("mag", SH, F32); am = sbp("am", SH, F32); tq = sbp("tq", SH, F32)
                        ti = sbp("ti", SH, I32); sn = sbp("sn", SH, F32)
                        PWre = sbp("PWre", SH, F32); PWim = sbp("PWim", SH, F32)
                        bg = lambda a_: a_[:].unsqueeze(2).to_broadcast(SH)
                        bm = lambda a_: a_[:].unsqueeze(3).to_broadcast(SH)
                        S.op("dve", lambda: nc.vector.tensor_tensor(out=mag[:], in0=bg(dta), in1=bm(mt), op=ALU.mult),
                             reads=[rP], writes=[rP])
                        S.op("act", lambda: nc.scalar.activation(out=mag[:], in_=mag[:], func=AF.Exp), reads=[rP],
                             writes=[rP])
                        S.op("dve", lambda: nc.vector.tensor_tensor(out=am[:], in0=bg(ang), in1=bm(mt), op=ALU.mult),
                             reads=[rP], writes=[rP])

                        def sin_of(dst, shift):
                            S.op("dve", lambda: nc.vector.tensor_scalar(tq[:], am[:], shift, 1.0 / TWO_PI, ALU.add,
                                                                        ALU.mult), reads=[rP], writes=[rP])
                            S.op("dve", lambda: nc.vector.tensor_copy(ti[:], tq[:]), reads=[rP], writes=[rP])
                            S.op("dve", lambda: nc.vector.tensor_copy(tq[:], ti[:]), reads=[rP], writes=[rP])
                            S.op("dve", lambda: nc.vector.scalar_tensor_tensor(
                                out=sn[:], in0=tq[:], scalar=-TWO_PI, in1=am[:], op0=ALU.mult, op1=ALU.add),
                                reads=[rP], writes=[rP])
                            if shift != 0.0:
                                S.op("dve", lambda: nc.vector.tensor_single_scalar(sn[:], sn[:], shift, ALU.add),
                                     reads=[rP], writes=[rP])
                            S.op("dve", lambda: nc.vector.tensor_single_scalar(tq[:], sn[:], PI, ALU.is_gt),
                                 reads=[rP], writes=[rP])
                            S.op("dve", lambda: nc.vector.scalar_tensor_tensor(
                                out=sn[:], in0=tq[:], scalar=-TWO_PI, in1=sn[:], op0=ALU.mult, op1=ALU.add),
                                reads=[rP], writes=[rP])
                            S.op("dve", lambda: nc.vector.tensor_single_scalar(tq[:], sn[:], -PI, ALU.is_lt),
                                 reads=[rP], writes=[rP])
                            S.op("dve", lambda: nc.vector.scalar_tensor_tensor(
                                out=sn[:], in0=tq[:], scalar=TWO_PI, in1=sn[:], op0=ALU.mult, op1=ALU.add),
                                reads=[rP], writes=[rP])
                            S.op("act", lambda: nc.scalar.activation(out=sn[:], in_=sn[:], func=AF.Sin), reads=[rP],
                                 writes=[rP])
                            S.op("dve", lambda: nc.vector.tensor_tensor(out=dst[:], in0=mag[:], in1=sn[:],
                                                                        op=ALU.mult), reads=[rP], writes=[rP])

                        sin_of(PWim, 0.0)
                        sin_of(PWre, PI / 2.0)
                        for r_ in range(2):
                            S.op("dve", lambda: nc.vector.tensor_copy(A1[:, :, r_, :], PWre[:, :, 24, :]),
                                 reads=[rP], writes=[rA12])
                        S.op("dve", lambda: nc.vector.tensor_single_scalar(A2[:, :, 0, :], PWim[:, :, 24, :], -1.0,
                                                                           ALU.mult), reads=[rP], writes=[rA12])
                        S.op("dve", lambda: nc.vector.tensor_copy(A2[:, :, 1, :], PWim[:, :, 24, :]), reads=[rP],
                             writes=[rA12])
                        SG = [64, 2, GH]
                        nr = sbp("nr", SG, F32); den = sbp("den", SG, F32); t1 = sbp("t1", SG, F32)
                        fre = sbp("fre", SG, F32); fim = sbp("fim", SG, F32)
                        P1r = PWre[:, :, 25, :]; P1i = PWim[:, :, 25, :]
                        tt = lambda o, a_, b_, op_: S.op("dve", lambda: nc.vector.tensor_tensor(out=o, in0=a_, in1=b_,
                                                                                                op=op_),
                                                         reads=[rP], writes=[rP])
                        S.op("dve", lambda: nc.vector.tensor_single_scalar(nr[:], P1r, -1.0, ALU.add), reads=[rP],
                             writes=[rP])
                        tt(den[:], Are[:], Are[:], ALU.mult)
                        tt(t1[:], Aim[:], Aim[:], ALU.mult)
                        tt(den[:], den[:], t1[:], ALU.add)
                        S.op("dve", lambda: nc.vector.reciprocal(den[:], den[:]), reads=[rP], writes=[rP])
                        tt(fre[:], nr[:], Are[:], ALU.mult)
                        tt(t1[:], P1i, Aim[:], ALU.mult)
                        tt(fre[:], fre[:], t1[:], ALU.add)
                        tt(fre[:], fre[:], den[:], ALU.mult)
                        tt(fim[:], P1i, Are[:], ALU.mult)
                        tt(t1[:], nr[:], Aim[:], ALU.mult)
                        tt(fim[:], fim[:], t1[:], ALU.subtract)
                        tt(fim[:], fim[:], den[:], ALU.mult)
                        SB = [64, 2, GH, 16]
                        Bre = sbp("Bre", SB, F32); Bim = sbp("Bim", SB, F32)
                        Bbr = sbp("Bbr", SB, F32); Bbi = sbp("Bbi", SB, F32); tb_ = sbp("tb_", SB, F32)
                        for d_ in range(2):
                            S.dma("sp", Bre[:, d_, :, :], W["s5_b_re"][l, d_, g0:g0 + GH].rearrange("g p c -> p g c"),
                                  writes=[rP])
                            S.dma("sp", Bim[:, d_, :, :], W["s5_b_im"][l, d_, g0:g0 + GH].rearrange("g p c -> p g c"),
                                  writes=[rP])
                        bc = lambda a_: a_[:].unsqueeze(3).to_broadcast(SB)
                        tt(Bbr[:], bc(fre), Bre[:], ALU.mult)
                        tt(tb_[:], bc(fim), Bim[:], ALU.mult)
                        tt(Bbr[:], Bbr[:], tb_[:], ALU.subtract)
                        tt(Bbi[:], bc(fre), Bim[:], ALU.mult)
                        tt(tb_[:], bc(fim), Bre[:], ALU.mult)
                        tt(Bbi[:], Bbi[:], tb_[:], ALU.add)
                        Cre = sbp("Cre", SB, F32); Cim = sbp("Cim", SB, F32)
                        cin = Rot([(sbp("cin%d" % i_, [128, 64], F32), Res()) for i_ in range(2)])
                        psC = psp("psC", [64, 512], F32); rpsC = Res()
                        for (src, dstc) in (("s5_c_re", Cre), ("s5_c_im", Cim)):
                            for d_ in range(2):
                                for k4 in range(GH // 8):
                                    ci, rci = cin.next()
                                    S.dma("sp", ci[:], W[src][l, d_, g0 + k4 * 8:g0 + k4 * 8 + 8].rearrange(
                                        "g c p -> (g c) p"), writes=[rci])
                                    S.op("pe", lambda: nc.tensor.transpose(psC[:, 0:128], ci[:], ident[:]),
                                         reads=[rci, rconst], writes=[rpsC])
                                    S.op("act", lambda: nc.scalar.copy(
                                        dstc[:, d_, k4 * 8:(k4 + 1) * 8, :].rearrange("p g c -> p (g c)"),
                                        psC[:, 0:128]), reads=[rpsC], writes=[rP])
                        GB2 = 16
                        SE = [64, GB2, 8, 16]
                        Er = sbp("Er", SE, F32); Ei = sbp("Ei", SE, F32)
                        Gr = sbp("Gr", SE, F32); Gi = sbp("Gi", SE, F32)
                        tA = sbp("tA", SE, F32); tB = sbp("tB", SE, F32)
                        psT = psp("psT", [128, 512], F32); rpsT = Res()
                        psE = psp("psE", [128, 512], F32); rpsE = Res()
                        for d_ in range(2):
                          for gb2 in range(GH // GB2):
                            gsl = slice(gb2 * GB2, (gb2 + 1) * GB2)
                            pw = lambda P_, s0: P_[:, d_, s0:s0 + 8, gsl].rearrange("p s g -> p g s").unsqueeze(
                                3).to_broadcast(SE)
                            bb = lambda Q_: Q_[:, d_, gsl, :].unsqueeze(2).to_broadcast(SE)

                            def cmul(outr, outi_neg, s0, Xr, Xi, negate_im):
                                tt(tA[:], pw(PWre, s0), bb(Xr), ALU.mult)
                                tt(tB[:], pw(PWim, s0), bb(Xi), ALU.mult)
                                tt(outr, tA[:], tB[:], ALU.subtract)
                                tt(tA[:], pw(PWre, s0), bb(Xi), ALU.mult)
                                tt(tB[:], pw(PWim, s0), bb(Xr), ALU.mult)
                                if negate_im:
                                    S.op("dve", lambda: nc.vector.scalar_tensor_tensor(
                                        out=outi_neg, in0=tA[:], scalar=-1.0, in1=tB[:], op0=ALU.mult,
                                        op1=ALU.subtract), reads=[rP], writes=[rP])
                                else:
                                    tt(outi_neg, tA[:], tB[:], ALU.add)

                            cmul(Er[:], Ei[:], 0, Bbr, Bbi, False)
                            cmul(Gr[:], Gi[:], 16, Cre, Cim, True)
                            for gl in range(GB2):
                                g_ = gb2 * GB2 + gl
                                e_r = Er[:, gl, :, :].rearrange("p s c -> p (s c)")
                                e_i = Ei[:, gl, :, :].rearrange("p s c -> p (s c)")
                                g_r = Gr[:, gl, :, :].rearrange("p s c -> p (s c)")
                                g_i = Gi[:, gl, :, :].rearrange("p s c -> p (s c)")
                                S.op("pe", lambda: nc.tensor.matmul(psT[:, 0:128], e_r, g_r, start=True, stop=False),
                                     reads=[rP], writes=[rpsT], inc=False)
                                S.op("pe", lambda: nc.tensor.matmul(psT[:, 0:128], e_i, g_i, start=False, stop=True),
                                     reads=[rP], writes=[rpsT])
                                S.op("dve", lambda: nc.vector.tensor_tensor(out=Tmat[:, d_, g_, :], in0=psT[:, 0:128],
                                                                            in1=smask[:, d_, :], op=ALU.mult),
                                     reads=[rpsT, rsm], writes=[rTmat])
                                S.op("pe", lambda: nc.tensor.transpose(psE[:, 0:64], e_r, ident[0:64, 0:64]),
                                     reads=[rP, rconst], writes=[rpsE], inc=False)
                                S.op("pe", lambda: nc.tensor.transpose(psE[:, 64:128], e_i, ident[0:64, 0:64]),
                                     reads=[rP, rconst], writes=[rpsE])
                                S.op("act", lambda: nc.scalar.copy(EmatT[:, d_, g_, :], psE[:, 0:128]),
                                     reads=[rpsE], writes=[rEm])
                            tt(tA[:], pw(PWre, 8), bb(Cre), ALU.mult)
                            tt(tB[:], pw(PWim, 8), bb(Cim), ALU.mult)
                            S.op("dve", lambda: nc.vector.tensor_tensor(
                                out=Fmat[:, d_, 0, gsl, :].rearrange("p g (s c) -> p g s c", c=16), in0=tA[:],
                                in1=tB[:], op=ALU.subtract), reads=[rP], writes=[rFm])
                            tt(tA[:], pw(PWre, 8), bb(Cim), ALU.mult)
                            tt(tB[:], pw(PWim, 8), bb(Cre), ALU.mult)
                            S.op("dve", lambda: nc.vector.scalar_tensor_tensor(
                                out=Fmat[:, d_, 1, gsl, :].rearrange("p g (s c) -> p g s c", c=16), in0=tA[:],
                                scalar=-1.0, in1=tB[:], op0=ALU.mult, op1=ALU.subtract), reads=[rP], writes=[rFm])
                        S.barrier()
                    with contextlib.ExitStack() as mm:
                        sbm = lambda n, s_, d: mm.enter_context(nc.sbuf_tensor(uname(n), s_, d))
                        psm = lambda n, s_, d: mm.enter_context(nc.psum_tensor(uname(n), s_, d))
                        Xtok = Rot([(sbm("Xtok%d" % i_, [32, 8, 256], BF16), Res()) for i_ in range(3)])
                        Xperm = Rot([(sbm("Xperm%d" % i_, [32, 16, 128], BF16), Res()) for i_ in range(2)])
                        U2 = sbm("U2", [128, 3, GH, 32], BF16)
                        rU2 = [Res() for _ in range(3)]
                        Zs = sbm("Zs", [64, 2, 2, GH, 32], F32); rZs = Res()
                        Xh = sbm("Xh", [64, 2, 2, GH, 32], BF16); rXh = Res()
                        Xs = sbm("Xs", [64, 2, 3, GH], F32); rXs = Res()
                        T1 = sbm("T1", [64, 2, 2, GH], F32); rT1 = Res()
                        T2 = sbm("T2", [64, 2, 2, GH], F32); rT2 = Res()
                        Ysb = Rot([(sbm("Ysb%d" % i_, [128, 256], F32), Res()) for i_ in range(2)])
                        Ytok = Rot([(sbm("Ytok%d" % i_, [32, 8, 256], F32), Res()) for i_ in range(2)])
                        psU = psm("psU", [128, 512], F32); rpsU = Res()
                        psZ = Rot([(psm("psZ%d" % i_, [64, 512], F32), Res()) for i_ in range(2)])
                        psY = psm("psY", [128, 512], F32); rpsY = Res()
                        psYT = psm("psYT", [32, 1024], F32); rpsYT = Res()
                        S.op("dve", lambda: nc.vector.memset(Xs[:], 0.0), writes=[rXs])
                        for j in range(NS):
                            tiles = (j, NS - 1 - j)
                            for ui in range(3):
                                d_ = 0 if ui == 0 else 1
                                tok0 = tiles[d_] * 256
                                perm = anti_b[0:32, 96:128] if ui == 1 else ident_b[0:32, 0:32]
                                for gb in range(GH // 16):
                                    xt, rxt = Xtok.next()
                                    c0 = (g0 + gb * 16) * 16
                                    S.dma("pool", xt[:], su_d[tok0:tok0 + 256, c0:c0 + 256].rearrange(
                                        "(n s) c -> n s c", s=8), writes=[rxt])
                                    xp, rxp = Xperm.next()
                                    S.op("act", lambda: nc.scalar.copy(
                                        xp[:].rearrange("n g (s c) -> n g s c", c=16),
                                        xt[:].rearrange("n s (g c) -> n g s c", c=16)), reads=[rxt], writes=[rxp])
                                    for g16 in range(16):
                                        S.op("pe", lambda: nc.tensor.matmul(
                                            psU[:, g16 * 32:(g16 + 1) * 32], xp[:, g16, :],
                                            perm, start=True, stop=True),
                                            reads=[rxp, rconst], writes=[rpsU], inc=(g16 == 15))
                                    S.op("act", lambda: nc.scalar.copy(
                                        U2[:, ui, gb * 16:(gb + 1) * 16, :].rearrange("p g n -> p (g n)"), psU[:]),
                                        reads=[rpsU], writes=[rU2[ui]])
                            for d_ in range(2):
                                for gq in range(GH // 8):
                                    pz, rpz = psZ.next()
                                    for g8 in range(8):
                                        g_ = gq * 8 + g8
                                        for ri in range(2):
                                            S.op("pe", lambda: nc.tensor.matmul(
                                                pz[:, (ri * 8 + g8) * 32:(ri * 8 + g8 + 1) * 32],
                                                EmatT[:, d_, g_, ri * 64:(ri + 1) * 64], U2[:, d_, g_, :],
                                                start=True, stop=True),
                                                reads=[rEm, rU2[d_]], writes=[rpz], inc=(g8 == 7 and ri == 1))
                                    S.op("act", lambda: nc.scalar.copy(
                                        Zs[:, d_, :, gq * 8:(gq + 1) * 8, :],
                                        pz[:].rearrange("p (r g n) -> p r g n", r=2, g=8)),
                                        reads=[rpz], writes=[rZs])
                            if j * 256 == HALF_T:
                                S.op("dve", lambda: nc.vector.tensor_scalar(Xs[:], Xs[:], link64, None, ALU.mult),
                                     reads=[rXs, rconst], writes=[rXs])
                            for i32 in range(32):
                                S.op("act", lambda: nc.scalar.copy(Xh[:, 0, :, :, i32], Xs[:, 0, 0:2, :]),
                                     reads=[rXs], writes=[rXh])
                                S.op("act", lambda: nc.scalar.copy(Xh[:, 1, :, :, 31 - i32], Xs[:, 1, 0:2, :]),
                                     reads=[rXs], writes=[rXh])
                                S.op("dve", lambda: nc.vector.tensor_tensor(out=T1[:], in0=A1[:], in1=Xs[:, :, 0:2, :],
                                                                            op=ALU.mult),
                                     reads=[rA12, rXs], writes=[rT1])
                                S.op("dve", lambda: nc.vector.tensor_tensor(out=T2[:], in0=A2[:], in1=Xs[:, :, 1:3, :],
                                                                            op=ALU.mult),
                                     reads=[rA12, rXs], writes=[rT2])
                                S.op("dve", lambda: nc.vector.tensor_tensor(out=T1[:], in0=T1[:], in1=T2[:],
                                                                            op=ALU.add),
                                     reads=[rT1, rT2], writes=[rT1])
                                S.op("dve", lambda: nc.vector.tensor_tensor(out=Xs[:, :, 0:2, :], in0=T1[:],
                                                                            in1=Zs[:, :, :, :, i32], op=ALU.add),
                                     reads=[rT1, rZs], writes=[rXs])
                                S.op("dve", lambda: nc.vector.tensor_copy(Xs[:, :, 2, :], Xs[:, :, 0, :]),
                                     reads=[rXs], writes=[rXs])
                            for d_ in range(2):
                                un = 0 if d_ == 0 else 2
                                tok0 = tiles[d_] * 256
                                for gb in range(GH // 16):
                                    yt, ryt = Ytok.next()
                                    for gq2 in range(2):
                                        for g8 in range(8):
                                            g_ = gb * 16 + gq2 * 8 + g8
                                            o_ = psY[:, g8 * 32:(g8 + 1) * 32]
                                            S.op("pe", lambda: nc.tensor.matmul(o_, Tmat[:, d_, g_, :],
                                                                                U2[:, un, g_, :], start=True,
                                                                                stop=False),
                                                 reads=[rTmat, rU2[un]], writes=[rpsY], inc=False)
                                            S.op("pe", lambda: nc.tensor.matmul(o_, Fmat[:, d_, 0, g_, :],
                                                                                Xh[:, d_, 0, g_, :], start=False,
                                                                                stop=False),
                                                 reads=[rFm, rXh], writes=[rpsY], inc=False)
                                            S.op("pe", lambda: nc.tensor.matmul(o_, Fmat[:, d_, 1, g_, :],
                                                                                Xh[:, d_, 1, g_, :], start=False,
                                                                                stop=True),
                                                 reads=[rFm, rXh], writes=[rpsY], inc=(g8 == 7))
                                        ys, rys = Ysb.next()
                                        S.op("act", lambda: nc.scalar.copy(ys[:], psY[:, 0:256]), reads=[rpsY],
                                             writes=[rys])
                                        for g8 in range(8):
                                            S.op("pe", lambda: nc.tensor.transpose(
                                                psYT[:, g8 * 128:(g8 + 1) * 128], ys[:, g8 * 32:(g8 + 1) * 32],
                                                ident[:]), reads=[rys, rconst], writes=[rpsYT], inc=(g8 == 7))
                                        S.op("dve", lambda: nc.vector.tensor_copy(
                                            yt[:, :, gq2 * 128:(gq2 + 1) * 128].rearrange("n t (g c) -> n g t c", c=16),
                                            psYT[:].rearrange("n (g t c) -> n g t c", g=8, t=8)),
                                            reads=[rpsYT], writes=[ryt])
                                    c0 = (g0 + gb * 16) * 16
                                    S.dma("sp", yfb_d[d_, tok0:tok0 + 256, c0:c0 + 256].rearrange(
                                        "(n s) c -> n s c", s=8), yt[:], reads=[ryt])
                        S.barrier()
                S.barrier()

        def phase3b(l):
            GC = float(np.sqrt(2.0 / np.pi))
            with contextlib.ExitStack() as ph:
                sbt = lambda n, s_, d: ph.enter_context(nc.sbuf_tensor(uname(n), s_, d))
                pst = lambda n, s_, d: ph.enter_context(nc.psum_tensor(uname(n), s_, d))
                Wg = sbt("Wg", [128, 8, 1024], BF16); rW = Res()
                S.dma("pool", Wg[:], W["s5_w_glu"][l].rearrange("(k p) f -> p k f", p=128), writes=[rW])
                dvec = sbt("dvec", [128, 1024], F32)
                S.dma("sp", dvec[:], W["s5_d"][l].rearrange("(o f) -> o f", o=1).partition_broadcast(128), writes=[rW])
                bg = sbt("bg", [128, 8], F32); og = sbt("og", [128, 8], F32)
                load_pc(bg[:], W["s5_b_glu"][l], rW)
                load_pc(og[:], W["s5_out_norm"][l], rW)
                S.op("dve", lambda: nc.vector.tensor_single_scalar(bg[:], bg[:], -1.0, ALU.mult), reads=[rW],
                     writes=[rW])
                ld = Rot([(sbt("ld3_%d" % i_, [128, 3, 1024], F32), Res()) for i_ in range(2)])
                ya = sbt("ya", [128, 1024], F32); rya = Res()
                yb_ = sbt("yb_", [128, 1024], F32); ryb = Res()
                glb = Rot([(sbt("glb%d" % i_, [128, 1024], BF16), Res()) for i_ in range(2)])
                glT = sbt("glT", [128, 8, TT], BF16); rglT = Res()
                cT = sbt("cT", [128, 8, TT], F32); rcT = [Res() for _ in range(8)]
                et = Rot([(sbt("et%d" % i_, [128, TT], F32), Res()) for i_ in range(2)])
                sq = Rot([(sbt("sq3_%d" % i_, [128, TT], BF16), Res()) for i_ in range(2)])
                rstd = sbt("rstd3", [128, TT], F32); rrstd = Res()
                cst = Rot([(sbt("cst%d" % i_, [128, TT], BF16), Res()) for i_ in range(2)])
                psG = Rot([(pst("psG%d" % i_, [128, 1024], BF16), Res()) for i_ in range(2)])
                psM = Rot([(pst("psM%d" % i_, [128, TT], F32), Res()) for i_ in range(2)])
                pstat = pst("pstat3", [128, TT], F32); rpstat = Res()
                for it in range(NT):
                    t0 = it * TT
                    for sub in range(4):
                        tk = t0 + sub * 128
                        lt, rlt = ld.next()
                        S.dma("sp", lt[:, 0, :], yfb_d[0, tk:tk + 128, :], writes=[rlt])
                        S.dma("sp", lt[:, 1, :], yfb_d[1, tk:tk + 128, :], writes=[rlt])
                        S.dma("sp", lt[:, 2, :], su_d[tk:tk + 128, :], writes=[rlt])
                        S.op("dve", lambda: nc.vector.tensor_tensor(out=ya[:], in0=lt[:, 2, :], in1=dvec[:],
                                                                    op=ALU.mult), reads=[rlt, rW], writes=[rya])
                        S.op("dve", lambda: nc.vector.tensor_tensor(out=ya[:], in0=ya[:], in1=lt[:, 0, :], op=ALU.add),
                             reads=[rya, rlt], writes=[rya])
                        S.op("dve", lambda: nc.vector.tensor_tensor(out=ya[:], in0=ya[:], in1=lt[:, 1, :], op=ALU.add),
                             reads=[rya, rlt], writes=[rya])
                        S.op("dve", lambda: nc.vector.tensor_tensor(out=yb_[:], in0=ya[:], in1=ya[:], op=ALU.mult),
                             reads=[rya], writes=[ryb])
                        S.op("dve", lambda: nc.vector.tensor_scalar(yb_[:], yb_[:], 0.044715, 1.0, ALU.mult, ALU.add),
                             reads=[ryb], writes=[ryb])
                        S.op("dve", lambda: nc.vector.tensor_tensor(out=yb_[:], in0=yb_[:], in1=ya[:], op=ALU.mult),
                             reads=[ryb, rya], writes=[ryb])
                        S.op("dve", lambda: nc.vector.tensor_single_scalar(yb_[:], yb_[:], -30.0, ALU.max),
                             reads=[ryb], writes=[ryb])
                        S.op("act", lambda: nc.scalar.activation(out=yb_[:], in_=yb_[:], func=AF.Exp,
                                                                 scale=-2.0 * GC), reads=[ryb], writes=[ryb])
                        S.op("dve", lambda: nc.vector.tensor_single_scalar(yb_[:], yb_[:], 1.0, ALU.add),
                             reads=[ryb], writes=[ryb])
                        S.op("dve", lambda: nc.vector.reciprocal(yb_[:], yb_[:]), reads=[ryb], writes=[ryb])
                        gl, rgl = glb.next()
                        S.op("dve", lambda: nc.vector.tensor_tensor(out=gl[:], in0=ya[:], in1=yb_[:], op=ALU.mult),
                             reads=[rya, ryb], writes=[rgl])
                        pg, rpg = psG.next()
                        for k in range(8):
                            S.op("pe", lambda: nc.tensor.transpose(pg[:, k * 128:(k + 1) * 128],
                                                                   gl[:, k * 128:(k + 1) * 128], ident_b[:]),
                                 reads=[rgl, rconst], writes=[rpg], inc=(k == 7))
                        S.op("act", lambda: nc.scalar.copy(glT[:, :, sub * 128:(sub + 1) * 128],
                                                           pg[:].rearrange("p (k t) -> p k t", k=8)),
                             reads=[rpg], writes=[rglT])
                    for oc in range(8):
                        pm, rpm = psM.next()
                        for k in range(8):
                            S.op("pe", lambda: nc.tensor.matmul(pm[:], Wg[:, k, oc * 128:(oc + 1) * 128], glT[:, k, :],
                                                                start=(k == 0), stop=(k == 7)),
                                 reads=[rW, rglT], writes=[rpm], inc=(k == 7))
                        e_, re_ = et.next()
                        S.op("act", lambda: nc.scalar.activation(out=e_[:], in_=pm[:], func=AF.Exp, scale=-1.0,
                                                                 bias=bg[:, oc:oc + 1]), reads=[rpm, rW], writes=[re_])
                        S.op("dve", lambda: nc.vector.tensor_single_scalar(e_[:], e_[:], 1.0, ALU.add), reads=[re_],
                             writes=[re_])
                        S.op("dve", lambda: nc.vector.reciprocal(e_[:], e_[:]), reads=[re_], writes=[re_])
                        S.op("dve", lambda: nc.vector.tensor_tensor(out=cT[:, oc, :], in0=glT[:, oc, :], in1=e_[:],
                                                                    op=ALU.mult), reads=[rglT, re_], writes=[rcT[oc]])
                        sq_, rsq_ = sq.next()
                        S.op("act", lambda: nc.scalar.activation(out=sq_[:], in_=cT[:, oc, :], func=AF.Square),
                             reads=[rcT[oc]], writes=[rsq_])
                        S.op("pe", lambda: nc.tensor.matmul(pstat[:], ones_b[:], sq_[:], start=(oc == 0),
                                                            stop=(oc == 7)), reads=[rsq_, rconst], writes=[rpstat])
                    S.op("act", lambda: nc.scalar.activation(out=rstd[:], in_=pstat[:], func=AF.Ln, bias=epsb[:],
                                                             scale=1.0 / 1024), reads=[rpstat, rconst], writes=[rrstd])
                    S.op("act", lambda: nc.scalar.activation(out=rstd[:], in_=rstd[:], func=AF.Exp, scale=-0.5),
                         reads=[rrstd], writes=[rrstd])
                    for oc in range(8):
                        cs_, rcs_ = cst.next()
                        S.op("dve", lambda: nc.vector.scalar_tensor_tensor(
                            out=cs_[:], in0=cT[:, oc, :], scalar=og[:, oc:oc + 1], in1=rstd[:], op0=ALU.mult,
                            op1=ALU.mult), reads=[rcT[oc], rW, rrstd], writes=[rcs_])
                        S.dma("sp", mixT_d[1024 + oc * 128:1024 + (oc + 1) * 128, t0:t0 + TT], cs_[:], reads=[rcs_])
                S.barrier()

        def phase4a(l):
            with contextlib.ExitStack() as ph:
                sbt = lambda n, s_, d: ph.enter_context(nc.sbuf_tensor(uname(n), s_, d))
                pst = lambda n, s_, d: ph.enter_context(nc.psum_tensor(uname(n), s_, d))
                NTB = T // 128
                HR = NROW // 2
                qT = sbt("qTs", [128, 4, T], BF16)
                kT = sbt("kTs", [128, 4, T], BF16)
                va = sbt("va", [128, NTB, 512], BF16)
                vb = sbt("vb", [128, NTB - 1, 512], BF16)
                rin = Res()
                for c in range(4):
                    S.dma("sp", qT[:, c, :], qT_d[c * 128:(c + 1) * 128, :], writes=[rin])
                    S.dma("sp", kT[:, c, :], kT_d[c * 128:(c + 1) * 128, :], writes=[rin])
                S.dma("sp", va[:], v_d.rearrange("(n p) c -> p n c", p=128), writes=[rin])
                S.dma("sp", vb[:], v_d[64:T - 64, :].rearrange("(n p) c -> p n c", p=128), writes=[rin])
                bias4 = sbt("bias4", [64, 4096], F32)
                S.dma("sp", bias4[:], C["nabias"][l, 4], writes=[rin])
                biasE = Rot([(sbt("biasE%d" % i_, [64, 4096], F32), Res()) for i_ in range(2)])
                gainb = sbt("gainb", [64, 512], F32)
                S.dma("sp", gainb[:], W["attn_out_norm"][l].rearrange("(o f) -> o f", o=1).partition_broadcast(64),
                      writes=[rin])
                Sb = Rot([(sbt("Sb%d" % i_, [64, 512], F32), Res()) for i_ in range(2)])
                Pm = Rot([(sbt("Pm%d" % i_, [64, 512], BF16), Res()) for i_ in range(2)])
                PT = Rot([(sbt("PT%d" % i_, [128, 4, 64], BF16), Res()) for i_ in range(2)])
                stat = Rot([(sbt("nst%d" % i_, [64, 4], F32), Res()) for i_ in range(4)])
                araw = [(sbt("araw%d" % i_, [64, 512], F32), Res()) for i_ in range(2)]
                anb = sbt("anb", [64, 512], BF16); ranb = Res()
                junk = sbt("junk", [64, 512], BF16); rjunk = Res()
                nst2 = sbt("nst2", [64, 2], F32); rnst2 = Res()
                aT = Rot([(sbt("aT%d" % i_, [128, 4, TT], BF16), Res()) for i_ in range(2)])
                ps_S = Rot([(pst("psS%d" % i_, [64, 512], F32), Res()) for i_ in range(2)])
                ps_T = Rot([(pst("psT%d" % i_, [128, 1024], BF16), Res()) for i_ in range(2)])
                ps_O = [(pst("psO%d" % i_, [64, 512], F32), Res()) for i_ in range(2)]
                ps_A = pst("psA", [128, 1024], BF16); rps_A = Res()

                def attend(r, rs, vi):
                    dl = r - rs
                    if dl == 4:
                        bt, rbt = bias4, rin
                    else:
                        bt, rbt = biasE.next()
                        S.dma("sp", bt[:], C["nabias"][l, dl], writes=[rbt])
                    po, rpo = ps_O[vi]
                    ar, rar = araw[vi]
                    for hh in range(8):
                        c = hh // 2
                        pb0 = (hh % 2) * 64
                        pS, rpS = ps_S.next()
                        S.op("pe", lambda: nc.tensor.matmul(pS[:], qT[pb0:pb0 + 64, c, r * 64:(r + 1) * 64],
                                                            kT[pb0:pb0 + 64, c, rs * 64:(rs + 8) * 64], start=True,
                                                            stop=True),
                             reads=[rin], writes=[rpS])
                        sb_, rsb = Sb.next()
                        st_, rst_ = stat.next()
                        S.op("dve", lambda: nc.vector.tensor_tensor(out=sb_[:], in0=pS[:],
                                                                    in1=bt[:, hh * 512:(hh + 1) * 512], op=ALU.add),
                             reads=[rpS, rbt], writes=[rsb])
                        S.op("dve", lambda: nc.vector.reduce_max(out=st_[:, 0:1], in_=sb_[:], axis=AX.X),
                             reads=[rsb], writes=[rst_])
                        S.op("dve", lambda: nc.vector.tensor_single_scalar(st_[:, 1:2], st_[:, 0:1], -1.0, ALU.mult),
                             reads=[rst_], writes=[rst_])
                        pm, rpm = Pm.next()
                        S.op("act", lambda: nc.scalar.activation(out=pm[:], in_=sb_[:], func=AF.Exp,
                                                                 bias=st_[:, 1:2], scale=1.0,
                                                                 accum_out=st_[:, 2:3]),
                             reads=[rsb, rst_], writes=[rpm, rst_])
                        pT, rpT = ps_T.next()
                        for j in range(4):
                            S.op("pe", lambda: nc.tensor.transpose(pT[:, j * 64:(j + 1) * 64],
                                                                   pm[:, j * 128:(j + 1) * 128], ident_b[0:64, 0:64]),
                                 reads=[rpm, rconst], writes=[rpT], inc=(j == 3))
                        pt_, rpt_ = PT.next()
                        S.op("act", lambda: nc.scalar.copy(pt_[:].rearrange("p a b -> p (a b)"), pT[:, 0:256]),
                             reads=[rpT], writes=[rpt_])
                        for j in range(4):
                            if rs % 2 == 0:
                                vsrc = va[:, rs // 2 + j, hh * 64:(hh + 1) * 64]
                            else:
                                vsrc = vb[:, (rs - 1) // 2 + j, hh * 64:(hh + 1) * 64]
                            S.op("pe", lambda: nc.tensor.matmul(po[:, hh * 64:(hh + 1) * 64], pt_[:, j, :], vsrc,
                                                                start=(j == 0), stop=(j == 3)),
                                 reads=[rpt_, rin], writes=[rpo], inc=(j == 3))
                        S.op("dve", lambda: nc.vector.reciprocal(st_[:, 3:4], st_[:, 2:3]), reads=[rst_],
                             writes=[rst_])
                        S.op("dve", lambda: nc.vector.tensor_scalar(ar[:, hh * 64:(hh + 1) * 64],
                                                                    po[:, hh * 64:(hh + 1) * 64], st_[:, 3:4], None,
                                                                    ALU.mult),
                             reads=[rpo, rst_], writes=[rar])

                for it in range(NT):
                    t0 = it * TT
                    at, rat = aT.next()
                    for r8 in range(8):
                        r = it * 8 + r8
                        rs_s = min(max(r - 4, 0), NROW - 8)
                        base = 0 if r < HR else HR
                        rs_p = base + min(max(r - base - 4, 0), HR - 8)
                        attend(r, rs_s, 0)
                        ar, rar = araw[0]
                        if rs_p != rs_s:
                            attend(r, rs_p, 1)
                            ap_, rap = araw[1]
                            S.op("dve", lambda: nc.vector.tensor_tensor(out=ar[:], in0=ar[:], in1=ap_[:],
                                                                        op=ALU.subtract),
                                 reads=[rar, rap], writes=[rar])
                            S.op("dve", lambda: nc.vector.scalar_tensor_tensor(
                                out=ar[:], in0=ar[:], scalar=linkt[0:64, 0:1], in1=ap_[:], op0=ALU.mult,
                                op1=ALU.add), reads=[rar, rap, rconst], writes=[rar])
                        S.op("act", lambda: nc.scalar.activation(out=junk[:], in_=ar[:], func=AF.Square,
                                                                 accum_out=nst2[:, 0:1]),
                             reads=[rar], writes=[rjunk, rnst2])
                        S.op("act", lambda: nc.scalar.activation(out=nst2[:, 1:2], in_=nst2[:, 0:1], func=AF.Ln,
                                                                 bias=epsb[0:64, :], scale=1.0 / 512),
                             reads=[rnst2, rconst], writes=[rnst2])
                        S.op("act", lambda: nc.scalar.activation(out=nst2[:, 1:2], in_=nst2[:, 1:2], func=AF.Exp,
                                                                 scale=-0.5),
                             reads=[rnst2], writes=[rnst2])
                        S.op("dve", lambda: nc.vector.scalar_tensor_tensor(
                            out=anb[:], in0=ar[:], scalar=nst2[:, 1:2], in1=gainb[:], op0=ALU.mult, op1=ALU.mult),
                            reads=[rar, rnst2, rin], writes=[ranb])
                        for c in range(4):
                            S.op("pe", lambda: nc.tensor.transpose(ps_A[:, c * 64:(c + 1) * 64],
                                                                   anb[:, c * 128:(c + 1) * 128],
                                                                   ident_b[0:64, 0:64]),
                                 reads=[ranb, rconst], writes=[rps_A], inc=(c == 3))
                        S.op("act", lambda: nc.scalar.copy(at[:, :, r8 * 64:(r8 + 1) * 64],
                                                           ps_A[:, 0:256].rearrange("p (c t) -> p c t", c=4)),
                             reads=[rps_A], writes=[rat])
                    S.dma("sp", mixT_d[0:512, t0:t0 + TT].rearrange("(c p) t -> p c t", p=128), at[:], reads=[rat])
                S.barrier()

        def phase4b(l):
            with contextlib.ExitStack() as ph:
                tl = TL(ph)
                sbt = tl.sbt
                x, rx, h, rh = tl.x, tl.rx, tl.h, tl.rh
                load_pc(tl.gains[:, 0, :], W["ffn2_norm"][l], tl.rgains)
                load_pc(tl.gains[:, 1, :], W["final_norm"][l], tl.rgains)
                last = (l == L - 1)
                if last:
                    yt_rot = Rot([(sbt("yt%d" % i, [128, D], F32), Res()) for i in range(2)])
                wov = W["w_out"][l].rearrange("(k p) f -> p k f", p=128)
                for it in range(NT):
                    t0 = it * TT
                    tl.load_x(it)
                    S.dma("sp", h[:], mixT_d[:, t0:t0 + TT].rearrange("(c p) t -> p c t", p=128), writes=[rh])
                    NB = D // 256
                    nxt = tl.wgu_rot.next()
                    S.dma("pool", nxt[0][:], wov[:, :, 0:256], writes=[nxt[1]])
                    for b in range(NB):
                        wb, rwb = nxt
                        if b + 1 < NB:
                            nxt = tl.wgu_rot.next()
                            S.dma("pool", nxt[0][:], wov[:, :, (b + 1) * 256:(b + 2) * 256], writes=[nxt[1]])
                        for jj in range(2):
                            i = 2 * b + jj
                            ps, rps = tl.pb_rot.next()
                            for k in range(NKC):
                                S.op("pe", lambda: nc.tensor.matmul(ps[:], wb[:, k, jj * 128:(jj + 1) * 128],
                                                                    h[:, k, :], start=(k == 0), stop=(k == NKC - 1)),
                                     reads=[rwb, rh], writes=[rps], inc=(k == NKC - 1))
                            S.op("dve", lambda: nc.vector.tensor_tensor(out=x[:, i, :], in0=ps[:], in1=x[:, i, :],
                                                                        op=ALU.add),
                                 reads=[rps, rx[i]], writes=[rx[i]])
                    tl.rmsnorm(0)
                    tl.ffn(W["ffn2_w_gate"][l], W["ffn2_w_up"][l], W["ffn2_w_down"][l])
                    tl.rmsnorm(1, inplace=True)
                    if not last:
                        tl.store_x(it)
                    else:
                        for tb in range(4):
                            yt, ryt = yt_rot.next()
                            for cq in range(4):
                                pb, rpb = tl.pb_rot.next()
                                for i4 in range(4):
                                    c = cq * 4 + i4
                                    S.op("pe", lambda: nc.tensor.transpose(pb[:, i4 * 128:(i4 + 1) * 128],
                                                                           x[:, c, tb * 128:(tb + 1) * 128],
                                                                           ident[:]),
                                         reads=[rx[c], rconst], writes=[rpb], inc=(i4 == 3))
                                if cq % 2 == 0:
                                    S.op("act", lambda: nc.scalar.copy(yt[:, cq * 512:(cq + 1) * 512], pb[:]),
                                         reads=[rpb], writes=[ryt])
                                else:
                                    S.op("dve", lambda: nc.vector.tensor_copy(yt[:, cq * 512:(cq + 1) * 512], pb[:]),
                                         reads=[rpb], writes=[ryt])
                            S.dma("sp", y_out[t0 + tb * 128:t0 + (tb + 1) * 128, :], yt[:], reads=[ryt])
                S.barrier()

        for l in range(L):
            if "p1" in phases:
                phase1(l)
            if "p2" in phases:
                phase2(l)
            if "p3a" in phases:
                phase3a(l)
            if "p3b" in phases:
                phase3b(l)
            if "p4a" in phases:
                phase4a(l)
            if "p4b" in phases:
                phase4b(l)
    return nc


T_CORE = 4096
DEPTH = 4
ALL_PHASES = ("p1", "p2", "p3a", "p3b", "p4a", "p4b")


def kernel(**inputs):
    xp = np.ascontiguousarray(np.asarray(inputs["x_prompt"], dtype=np.float32))
    xs = np.ascontiguousarray(np.asarray(inputs["x_sample"], dtype=np.float32))
    wts = {n: np.ascontiguousarray(np.asarray(inputs[n], dtype=np.float32)) for n, _ in WSPECS}
    rel_bias = np.asarray(inputs["rel_bias"], dtype=np.float32)
    nc = build(T_CORE, DEPTH, dbg=False, phases=ALL_PHASES)
    consts = [host_consts(T_CORE, DEPTH, rel_bias, 0), host_consts(T_CORE, DEPTH, rel_bias, 1)]
    in_maps = []
    for core in range(8):
        if core < 4:
            x = xp[2 * core:2 * core + 2].reshape(T_CORE, D)
            cst = consts[0]
        else:
            x = xs[core - 4].reshape(T_CORE, D)
            cst = consts[1]
        m = {"x": x}
        m.update(wts)
        m.update(cst)
        in_maps.append(m)
    res = run_bass_kernel_spmd(nc, in_maps, core_ids=list(range(8)))
    outs = [np.asarray(r["y"], dtype=np.float32) for r in res.results]
    y_prompt = np.concatenate([o.reshape(2, 2048, D) for o in outs[:4]], axis=0)
    y_sample = np.stack([o.reshape(4096, D) for o in outs[4:]], axis=0)
    return (y_prompt, y_sample)
```

```python
import contextlib
import numpy as np
import concourse.bass as bass
import concourse.mybir as mybir
from concourse.bass_utils import run_bass_kernel_spmd

F32 = mybir.dt.float32
BF16 = mybir.dt.bfloat16
I32 = mybir.dt.int32
AF = mybir.ActivationFunctionType
ALU = mybir.AluOpType
AX = mybir.AxisListType

D = 2048
FF = 5632
NKC = 16
NFC = 44
TT = 512
INC = 5120
EPS = 1e-6
NEG = -30000.0
STOP = None
NBLIM = None
PI = float(np.pi)


class Res:
    __slots__ = ("w", "r")

    def __init__(self):
        self.w = None
        self.r = {}


class Sched:
    def __init__(self, nc, es, n_dma=14):
        self.nc = nc
        self.eng = {"pe": nc.tensor, "act": nc.scalar, "dve": nc.vector, "pool": nc.gpsimd, "sp": nc.sync}
        self.sem = {}
        self.cnt = {}
        for e in ("pe", "act", "dve", "pool"):
            self.sem[e] = es.enter_context(nc.semaphore("s_" + e))
            self.cnt[e] = 0
        self.dq = {"sp": [], "pool": []}
        self.dq_next = {"sp": 0, "pool": 0}
        for q in ("sp", "pool"):
            for i in range(n_dma):
                pid = "d_%s_%d" % (q, i)
                self.sem[pid] = es.enter_context(nc.semaphore(pid))
                self.cnt[pid] = 0
                self.dq[q].append(pid)
        self.seen = {e: {} for e in self.eng}
        self.n_ins = 0

    def _wait(self, e, pid, val):
        if val > 0 and self.seen[e].get(pid, 0) < val:
            self.eng[e].wait_ge(self.sem[pid], val)
            self.seen[e][pid] = val

    def _deps(self, e, reads, writes):
        deps = {}
        for b in reads:
            if b.w is not None:
                p, v = b.w
                if deps.get(p, 0) < v:
                    deps[p] = v
        for b in writes:
            if b.w is not None:
                p, v = b.w
                if deps.get(p, 0) < v:
                    deps[p] = v
            for p, v in b.r.items():
                if deps.get(p, 0) < v:
                    deps[p] = v
        for p, v in deps.items():
            if p == e and e == "pe":
                continue
            self._wait(e, p, v)

    def _record(self, pid, val, reads, writes):
        for b in reads:
            if b.r.get(pid, 0) < val:
                b.r[pid] = val
        for b in writes:
            b.w = (pid, val)
            b.r = {}

    def op(self, e, fn, reads=(), writes=(), inc=True):
        self._deps(e, reads, writes)
        ins = fn()
        tick = self.cnt[e] + 1
        if inc:
            ins.then_inc(self.sem[e], 1)
            self.cnt[e] = tick
        self._record(e, tick, reads, writes)
        self.n_ins += 1
        return ins

    def dma(self, q, out, in_, reads=(), writes=(), **kw):
        self._deps(q, reads, writes)
        i = self.dq_next[q]
        self.dq_next[q] = (i + 1) % len(self.dq[q])
        pid = self.dq[q][i]
        prev = self.cnt[pid]
        self._wait(q, pid, prev)
        ins = self.eng[q].dma_start(out=out, in_=in_, **kw)
        ins.then_inc(self.sem[pid], 16)
        self.cnt[pid] = prev + 16
        self._record(pid, prev + 16, reads, writes)
        self.n_ins += 1
        return ins

    def barrier(self):
        for e in self.eng:
            for pid, v in self.cnt.items():
                if pid == e and e == "pe":
                    continue
                self._wait(e, pid, v)


class Rot:
    def __init__(self, items):
        self.items = items
        self.i = 0

    def next(self):
        it = self.items[self.i]
        self.i = (self.i + 1) % len(self.items)
        return it


WSPECS = [
    ("ffn1_norm", (D,)), ("ffn1_w_gate", (D, FF)), ("ffn1_w_up", (D, FF)), ("ffn1_w_down", (FF, D)),
    ("mix_norm", (D,)), ("w_in", (D, INC)), ("q_norm", (64,)), ("k_norm", (64,)),
    ("attn_out_norm", (512,)), ("hg_lb_logits", (2, 512)), ("hg_out_norm", (512,)),
    ("s5_a_re", (2, 64, 64)), ("s5_a_im", (2, 64, 64)), ("s5_log_dt", (2, 64)),
    ("s5_b_re", (2, 64, 64, 16)), ("s5_b_im", (2, 64, 64, 16)), ("s5_c_re", (2, 64, 16, 64)),
    ("s5_c_im", (2, 64, 16, 64)), ("s5_d", (1024,)), ("s5_w_glu", (1024, 1024)), ("s5_b_glu", (1024,)),
    ("s5_out_norm", (1024,)), ("w_out", (D, D)), ("ffn2_norm", (D,)), ("ffn2_w_gate", (D, FF)),
    ("ffn2_w_up", (D, FF)), ("ffn2_w_down", (FF, D)), ("final_norm", (D,)),
]


def host_consts(T, L, rel_bias, link):
    c = {}
    c["c_ident"] = np.eye(128, dtype=np.float32)
    c["c_anti"] = np.eye(128, dtype=np.float32)[::-1].copy()
    bo = np.zeros((128, 128), np.float32)
    bo[:64, :64] = 1.0
    bo[64:, 64:] = 1.0
    c["c_blockones"] = bo
    s = np.arange(128)
    same = (s[:, None] // 16) == (s[None, :] // 16)
    hm = np.zeros((2, 128, 128), np.float32)
    hm[0] = (same & (s[:, None] <= s[None, :])).astype(np.float32)
    hm[1] = (same & (s[:, None] >= s[None, :])).astype(np.float32)
    c["c_hgmask"] = hm
    c["c_chunkmask"] = ((s[:, None] // 16) == np.arange(8)[None, :]).astype(np.float32)
    rm = np.ones((128, TT), np.float32)
    rm[:, ::16] = 0.0
    c["c_resetmask"] = rm
    sp = s // 16
    sm = np.zeros((2, 128, 128), np.float32)
    sm[0] = (sp[None, :] >= sp[:, None]).astype(np.float32)
    sm[1] = (sp[:, None] >= sp[None, :]).astype(np.float32)
    c["c_s5mask"] = sm
    c["link"] = np.full((128, 1), float(link), np.float32)
    ex = np.zeros((2, 26), np.float32)
    k8 = np.arange(8)
    ex[0, 0:8] = 7 - k8; ex[1, 0:8] = k8
    ex[0, 8:16] = k8 + 1; ex[1, 8:16] = 8 - k8
    ex[0, 16:24] = k8 - 7; ex[1, 16:24] = -k8
    ex[:, 24] = 8; ex[:, 25] = 1
    c["c_s5exp"] = ex
    q = np.arange(64)
    cs = np.clip(q - 8, 0, 48)
    j = np.arange(64)
    inwin = (j[None, :] >= cs[:, None]) & (j[None, :] < cs[:, None] + 16)
    jj = np.clip(j[None, :] - q[:, None] + 15, 0, 30)
    nb = np.full((L, 8, 64, 8, 8, 64), NEG, np.float32)
    for dl in range(8):
        for a in range(8):
            rb = rel_bias[:L, :, a - dl + 7, :]
            g = rb[:, :, jj]
            g = np.where(inwin[None, None], g, np.float32(NEG))
            nb[:, dl, :, :, a, :] = np.transpose(g, (0, 2, 1, 3))
    c["nabias"] = nb.reshape(L, 8, 64, 8 * 512)
    return c


CONST_SHAPES = lambda L: {
    "c_ident": (128, 128), "c_anti": (128, 128), "c_blockones": (128, 128), "c_hgmask": (2, 128, 128),
    "c_chunkmask": (128, 8), "c_resetmask": (128, TT), "c_s5mask": (2, 128, 128), "link": (128, 1),
    "c_s5exp": (2, 26),
    "nabias": (L, 8, 64, 8 * 512),
}


def build(T, L, dbg=False, phases=("p1", "p2", "p3", "p4")):
    NT = T // TT
    NROW = T // 64
    HALF_T = T // 2
    nc = bass.Bass("TRN2", target_bir_lowering=False)

    def din(name, shape, dt=F32):
        return nc.dram_tensor(name, list(shape), dt, kind="ExternalInput").ap()

    def dscr(name, shape, dt):
        return nc.dram_tensor(name, list(shape), dt, kind=("ExternalOutput" if dbg else "Internal")).ap()

    x_in = din("x", [T, D])
    W = {n: din(n, (L,) + tuple(s)) for n, s in WSPECS}
    C = {n: din(n, s) for n, s in CONST_SHAPES(L).items()}
    y_out = nc.dram_tensor("y", [T, D], F32, kind="ExternalOutput").ap()

    xbuf = dscr("xbuf", [D, T], F32)
    qT_d = dscr("qT", [512, T], BF16)
    kT_d = dscr("kT", [512, T], BF16)
    v_d = dscr("vtok", [T, 512], BF16)
    hqT_d = dscr("hqT", [512, T], BF16)
    zf_d = dscr("zf", [2, 512, T], F32)
    vh_d = dscr("vh", [T, 512], BF16)
    hgT_d = dscr("hgT", [512, T], F32)
    su_d = dscr("sutok", [T, 1024], F32)
    ohf_d = dscr("ohf", [512, T], F32)
    yfb_d = dscr("yfb", [2, T, 1024], F32)
    mixT_d = dscr("mixT", [D, T], BF16)

    es = contextlib.ExitStack()
    with es:
        S = Sched(nc, es)

        _uid = [0]

        def uname(n):
            _uid[0] += 1
            return "t%d_%s" % (_uid[0], n)

        def gsb(name, shape, dt):
            return es.enter_context(nc.sbuf_tensor(uname(name), shape, dt))

        ident = gsb("ident", [128, 128], F32)
        ident_b = gsb("ident_b", [128, 128], BF16)
        anti_b = gsb("anti_b", [128, 128], BF16)
        ones_b = gsb("ones_b", [128, 128], BF16)
        bones_b = gsb("bones_b", [128, 128], BF16)
        epsb = gsb("epsb", [128, 1], F32)
        linkt = gsb("linkt", [128, 1], F32)
        onest = gsb("onest", [128, 1], F32)
        rconst = Res()
        tmpc = gsb("tmpc", [128, 128], F32)
        rtmpc = Res()
        S.dma("sp", ident[:], C["c_ident"], writes=[rconst])
        S.op("dve", lambda: nc.vector.tensor_copy(ident_b[:], ident[:]), reads=[rconst], writes=[rconst])
        S.dma("sp", tmpc[:], C["c_anti"], writes=[rtmpc])
        S.op("dve", lambda: nc.vector.tensor_copy(anti_b[:], tmpc[:]), reads=[rtmpc], writes=[rconst])
        S.dma("sp", tmpc[:], C["c_blockones"], reads=[], writes=[rtmpc])
        S.op("dve", lambda: nc.vector.tensor_copy(bones_b[:], tmpc[:]), reads=[rtmpc], writes=[rconst])
        S.op("dve", lambda: nc.vector.memset(ones_b[:], 1.0), writes=[rconst])
        S.op("dve", lambda: nc.vector.memset(epsb[:], EPS), writes=[rconst])
        S.op("dve", lambda: nc.vector.memset(onest[:], 1.0), writes=[rconst])
        S.dma("sp", linkt[:], C["link"], writes=[rconst])
        S.barrier()

        def load_pc(dst, src1d, res):
            S.dma("sp", dst, src1d.rearrange("(c p) -> p c", p=128), writes=[res],
                  allow_slow_non_contiguous=True)

        class TL:
            def __init__(self, ph, with_ffn=True):
                sbt = lambda n, s, d: ph.enter_context(nc.sbuf_tensor(uname(n), s, d))
                pst = lambda n, s, d: ph.enter_context(nc.psum_tensor(uname(n), s, d))
                self.sbt, self.pst = sbt, pst
                self.x = sbt("x", [128, NKC, TT], F32)
                self.rx = [Res() for _ in range(NKC)]
                self.h = sbt("h", [128, NKC, TT], BF16)
                self.rh = Res()
                self.sq = Rot([(sbt("sq%d" % i, [128, TT], BF16), Res()) for i in range(2)])
                self.rstd = sbt("rstd", [128, TT], F32)
                self.rrstd = Res()
                self.tmp = Rot([(sbt("tmp%d" % i, [128, TT], F32), Res()) for i in range(2)])
                self.wgu = [(sbt("wgu%d" % i, [128, NKC, 256], BF16), Res()) for i in range(4)]
                self.wgu_rot = Rot(self.wgu)
                self.pbank = [(pst("pb%d" % i, [128, TT], F32), Res()) for i in range(4)]
                self.pb_rot = Rot(self.pbank)
                self.pstat = pst("pstat", [128, TT], F32)
                self.rpstat = Res()
                if with_ffn:
                    self.g = sbt("g", [128, NFC, TT], BF16)
                    self.rg = [Res() for _ in range(NFC)]
                    self.wd = Rot([(sbt("wd%d" % i, [128, NFC // 2, 256], BF16), Res()) for i in range(3)])
                    self.pd = Rot([(pst("pd%d" % i, [128, TT], F32), Res()) for i in range(2)])
                self.gains = sbt("gains", [128, 4, NKC], F32)
                self.rgains = Res()

            def rmsnorm(self, gi, inplace=False):
                x, rx = self.x, self.rx
                for c in range(NKC):
                    sq, rsq = self.sq.next()
                    S.op("act", lambda: nc.scalar.activation(out=sq[:], in_=x[:, c, :], func=AF.Square),
                         reads=[rx[c]], writes=[rsq])
                    S.op("pe", lambda: nc.tensor.matmul(self.pstat[:], ones_b[:], sq[:], start=(c == 0),
                                                        stop=(c == NKC - 1)),
                         reads=[rsq, rconst], writes=[self.rpstat])
                S.op("act", lambda: nc.scalar.activation(out=self.rstd[:], in_=self.pstat[:], func=AF.Ln,
                                                         bias=epsb[:], scale=1.0 / D),
                     reads=[self.rpstat, rconst], writes=[self.rrstd])
                S.op("act", lambda: nc.scalar.activation(out=self.rstd[:], in_=self.rstd[:], func=AF.Exp,
                                                         scale=-0.5),
                     reads=[self.rrstd], writes=[self.rrstd])
                for c in range(NKC):
                    if inplace:
                        S.op("dve", lambda: nc.vector.scalar_tensor_tensor(
                            out=x[:, c, :], in0=x[:, c, :], scalar=self.gains[:, gi, c:c + 1], in1=self.rstd[:],
                            op0=ALU.mult, op1=ALU.mult),
                            reads=[rx[c], self.rrstd, self.rgains], writes=[rx[c]])
                    else:
                        S.op("dve", lambda: nc.vector.scalar_tensor_tensor(
                            out=self.h[:, c, :], in0=x[:, c, :], scalar=self.gains[:, gi, c:c + 1],
                            in1=self.rstd[:], op0=ALU.mult, op1=ALU.mult),
                            reads=[rx[c], self.rrstd, self.rgains], writes=[self.rh])

            def ffn(self, wg_ap, wu_ap, wd_ap):
                wgv = wg_ap.rearrange("(k p) f -> p k f", p=128)
                wuv = wu_ap.rearrange("(k p) f -> p k f", p=128)
                wdv = wd_ap.rearrange("(j p) d -> p j d", p=128)
                x, rx, h, rh, g, rg = self.x, self.rx, self.h, self.rh, self.g, self.rg

                def load_gu(jb):
                    tg, rtg = self.wgu_rot.next()
                    tu, rtu = self.wgu_rot.next()
                    S.dma("pool", tg[:], wgv[:, :, jb * 256:(jb + 1) * 256], writes=[rtg])
                    S.dma("pool", tu[:], wuv[:, :, jb * 256:(jb + 1) * 256], writes=[rtu])
                    return tg, rtg, tu, rtu

                def load_d(q):
                    db, jh = q // 2, q % 2
                    td, rtd = self.wd.next()
                    S.dma("pool", td[:], wdv[:, jh * 22:(jh + 1) * 22, db * 256:(db + 1) * 256], writes=[rtd])
                    return td, rtd

                NJB = FF // 256
                nxt = load_gu(0)
                dq = [load_d(0)]
                for jb in range(NJB):
                    tg, rtg, tu, rtu = nxt
                    if jb + 1 < NJB:
                        nxt = load_gu(jb + 1)
                    else:
                        dq.append(load_d(1))
                    for jj in range(2):
                        j = 2 * jb + jj
                        pg, rpg = self.pb_rot.next()
                        pu, rpu = self.pb_rot.next()
                        for k in range(NKC):
                            S.op("pe", lambda: nc.tensor.matmul(pg[:], tg[:, k, jj * 128:(jj + 1) * 128], h[:, k, :],
                                                                start=(k == 0), stop=(k == NKC - 1)),
                                 reads=[rtg, rh], writes=[rpg], inc=(k == NKC - 1))
                        for k in range(NKC):
                            S.op("pe", lambda: nc.tensor.matmul(pu[:], tu[:, k, jj * 128:(jj + 1) * 128], h[:, k, :],
                                                                start=(k == 0), stop=(k == NKC - 1)),
                                 reads=[rtu, rh], writes=[rpu], inc=(k == NKC - 1))
                        tmp, rtmp = self.tmp.next()
                        S.op("act", lambda: nc.scalar.activation(out=tmp[:], in_=pg[:], func=AF.Silu),
                             reads=[rpg], writes=[rtmp])
                        S.op("dve", lambda: nc.vector.tensor_tensor(out=g[:, j, :], in0=tmp[:], in1=pu[:],
                                                                    op=ALU.mult),
                             reads=[rtmp, rpu], writes=[rg[j]])
                NDB = D // 256
                NQ = NDB * 2
                for db in range(NDB):
                    pds = [self.pd.next() for _ in range(2)]
                    for jh in range(2):
                        q = db * 2 + jh
                        td, rtd = dq.pop(0)
                        if q + 2 < NQ:
                            dq.append(load_d(q + 2))
                        for dd in range(2):
                            pd, rpd = pds[dd]
                            for j2 in range(22):
                                j = jh * 22 + j2
                                S.op("pe", lambda: nc.tensor.matmul(pd[:], td[:, j2, dd * 128:(dd + 1) * 128],
                                                                    g[:, j, :], start=(j == 0), stop=(j == NFC - 1)),
                                     reads=[rtd, rg[j]], writes=[rpd], inc=(j2 == 21))
                    for dd in range(2):
                        i = 2 * db + dd
                        pd, rpd = pds[dd]
                        S.op("dve", lambda: nc.vector.scalar_tensor_tensor(
                            out=x[:, i, :], in0=pd[:], scalar=0.5, in1=x[:, i, :], op0=ALU.mult, op1=ALU.add),
                            reads=[rpd, rx[i]], writes=[rx[i]])

            def load_x(self, it):
                S.dma("sp", self.x[:], xbuf[:, it * TT:(it + 1) * TT].rearrange("(c p) t -> p c t", p=128),
                      writes=self.rx)

            def store_x(self, it):
                S.dma("sp", xbuf[:, it * TT:(it + 1) * TT].rearrange("(c p) t -> p c t", p=128), self.x[:],
                      reads=self.rx)

        def phase1(l):
            with contextlib.ExitStack() as ph:
                tl = TL(ph)
                sbt, pst = tl.sbt, tl.pst
                x, rx, h, rh = tl.x, tl.rx, tl.h, tl.rh
                load_pc(tl.gains[:, 0, :], W["ffn1_norm"][l], tl.rgains)
                load_pc(tl.gains[:, 1, :], W["mix_norm"][l], tl.rgains)
                qkg = sbt("qkg", [128, 2], F32)
                rqkg = Res()
                for half in range(2):
                    S.dma("sp", qkg[half * 64:(half + 1) * 64, 0:1],
                          W["q_norm"][l].rearrange("(p o) -> p o", o=1), writes=[rqkg])
                    S.dma("sp", qkg[half * 64:(half + 1) * 64, 1:2],
                          W["k_norm"][l].rearrange("(p o) -> p o", o=1), writes=[rqkg])
                S.op("dve", lambda: nc.vector.tensor_single_scalar(qkg[:, 0:1], qkg[:, 0:1], 0.125, ALU.mult),
                     reads=[rqkg], writes=[rqkg])
                stg_b = Rot([(sbt("stgb%d" % i, [128, TT], BF16), Res()) for i in range(2)])
                stg_f = Rot([(sbt("stgf%d" % i, [128, TT], F32), Res()) for i in range(2)])
                stk_b = Rot([(sbt("stkb%d" % i, [128, 2, 256], BF16), Res()) for i in range(2)])
                stk_f = Rot([(sbt("stkf%d" % i, [128, 2, 256], F32), Res()) for i in range(2)])
                if l == 0:
                    xin_rot = Rot([(sbt("xin%d" % i, [128, D], F32), Res()) for i in range(2)])
                winv = W["w_in"][l].rearrange("(k p) f -> p k f", p=128)
                for it in range(NT):
                    t0 = it * TT
                    if l == 0:
                        for tb in range(4):
                            xin, rxin = xin_rot.next()
                            S.dma("sp", xin[:], x_in[t0 + tb * 128:t0 + (tb + 1) * 128, :], writes=[rxin])
                            for cq in range(4):
                                pb, rpb = tl.pb_rot.next()
                                for i4 in range(4):
                                    c = cq * 4 + i4
                                    S.op("pe", lambda: nc.tensor.transpose(pb[:, i4 * 128:(i4 + 1) * 128],
                                                                           xin[:, c * 128:(c + 1) * 128], ident[:]),
                                         reads=[rxin, rconst], writes=[rpb], inc=(i4 == 3))
                                dsts = x[:, cq * 4:(cq + 1) * 4, tb * 128:(tb + 1) * 128]
                                srcs = pb[:].rearrange("p (c t) -> p c t", c=4)
                                wr = [rx[cq * 4 + i4] for i4 in range(4)]
                                if cq % 2 == 0:
                                    S.op("act", lambda: nc.scalar.copy(dsts, srcs), reads=[rpb], writes=wr)
                                else:
                                    S.op("dve", lambda: nc.vector.tensor_copy(dsts, srcs), reads=[rpb], writes=wr)
                    else:
                        tl.load_x(it)
                    if STOP == "x":
                        tl.store_x(it)
                        continue
                    tl.rmsnorm(0)
                    if STOP == "n":
                        tl.store_x(it)
                        continue
                    tl.ffn(W["ffn1_w_gate"][l], W["ffn1_w_up"][l], W["ffn1_w_down"][l])
                    tl.store_x(it)
                    if STOP == "f":
                        continue
                    tl.rmsnorm(1)
                    NB = INC // 256 if NBLIM is None else NBLIM
                    nxt = tl.wgu_rot.next()
                    S.dma("pool", nxt[0][:], winv[:, :, 0:256], writes=[nxt[1]])
                    for b in range(NB):
                        wb, rwb = nxt
                        if b + 1 < NB:
                            nxt = tl.wgu_rot.next()
                            S.dma("pool", nxt[0][:], winv[:, :, (b + 1) * 256:(b + 2) * 256], writes=[nxt[1]])
                        tokmajor = b in (4, 5, 12, 13, 16, 17, 18, 19)
                        if not tokmajor:
                            for jj in range(2):
                                ps, rps = tl.pb_rot.next()
                                for k in range(NKC):
                                    S.op("pe", lambda: nc.tensor.matmul(ps[:], wb[:, k, jj * 128:(jj + 1) * 128],
                                                                        h[:, k, :], start=(k == 0),
                                                                        stop=(k == NKC - 1)),
                                         reads=[rwb, rh], writes=[rps], inc=(k == NKC - 1))
                                if b < 4:
                                    c4 = 2 * (b % 2) + jj
                                    gi = 0 if b < 2 else 1
                                    dst = qT_d if b < 2 else kT_d
                                    sq, rsq = tl.sq.next()
                                    S.op("act", lambda: nc.scalar.activation(out=sq[:], in_=ps[:], func=AF.Square),
                                         reads=[rps], writes=[rsq])
                                    S.op("pe", lambda: nc.tensor.matmul(tl.pstat[:], bones_b[:], sq[:], start=True,
                                                                        stop=True),
                                         reads=[rsq, rconst], writes=[tl.rpstat])
                                    S.op("act", lambda: nc.scalar.activation(out=tl.rstd[:], in_=tl.pstat[:],
                                                                             func=AF.Ln, bias=epsb[:],
                                                                             scale=1.0 / 64),
                                         reads=[tl.rpstat, rconst], writes=[tl.rrstd])
                                    S.op("act", lambda: nc.scalar.activation(out=tl.rstd[:], in_=tl.rstd[:],
                                                                             func=AF.Exp, scale=-0.5),
                                         reads=[tl.rrstd], writes=[tl.rrstd])
                                    st, rst = stg_b.next()
                                    S.op("dve", lambda: nc.vector.scalar_tensor_tensor(
                                        out=st[:], in0=ps[:], scalar=qkg[:, gi:gi + 1], in1=tl.rstd[:],
                                        op0=ALU.mult, op1=ALU.mult),
                                        reads=[rps, tl.rrstd, rqkg], writes=[rst])
                                    S.dma("sp", dst[c4 * 128:(c4 + 1) * 128, t0:t0 + TT], st[:], reads=[rst])
                                elif b in (6, 7):
                                    c4 = 2 * (b - 6) + jj
                                    st, rst = stg_b.next()
                                    S.op("act", lambda: nc.scalar.activation(out=st[:], in_=ps[:], func=AF.Silu),
                                         reads=[rps], writes=[rst])
                                    S.dma("sp", hqT_d[c4 * 128:(c4 + 1) * 128, t0:t0 + TT], st[:], reads=[rst])
                                elif b in (8, 9, 10, 11):
                                    dr = 0 if b < 10 else 1
                                    c4 = 2 * ((b - 8) % 2) + jj
                                    st, rst = stg_f.next()
                                    S.op("act", lambda: nc.scalar.copy(st[:], ps[:]), reads=[rps], writes=[rst])
                                    S.dma("sp", zf_d[dr, c4 * 128:(c4 + 1) * 128, t0:t0 + TT], st[:], reads=[rst])
                                else:
                                    c4 = 2 * (b - 14) + jj
                                    st, rst = stg_f.next()
                                    S.op("act", lambda: nc.scalar.activation(out=st[:], in_=ps[:], func=AF.Silu),
                                         reads=[rps], writes=[rst])
                                    S.dma("sp", hgT_d[c4 * 128:(c4 + 1) * 128, t0:t0 + TT], st[:], reads=[rst])
                        else:
                            if b in (4, 5):
                                dst, c0, isf = v_d, (b - 4) * 256, False
                            elif b in (12, 13):
                                dst, c0, isf = vh_d, (b - 12) * 256, False
                            else:
                                dst, c0, isf = su_d, (b - 16) * 256, True
                            for sp2 in range(2):
                                ps, rps = tl.pb_rot.next()
                                for s2 in range(2):
                                    s = sp2 * 2 + s2
                                    for k in range(NKC):
                                        S.op("pe", lambda: nc.tensor.matmul(
                                            ps[:, s2 * 256:(s2 + 1) * 256], h[:, k, s * 128:(s + 1) * 128],
                                            wb[:, k, :], start=(k == 0), stop=(k == NKC - 1)),
                                            reads=[rwb, rh], writes=[rps], inc=(s2 == 1 and k == NKC - 1))
                                st, rst = (stk_f if isf else stk_b).next()
                                S.op("act", lambda: nc.scalar.copy(st[:], ps[:].rearrange("p (s c) -> p s c", s=2)),
                                     reads=[rps], writes=[rst])
                                S.dma("sp", dst[t0 + sp2 * 256:t0 + (sp2 + 1) * 256, c0:c0 + 256].rearrange(
                                    "(s p) c -> p s c", p=128), st[:], reads=[rst])
                S.barrier()

        def phase2(l):
            with contextlib.ExitStack() as ph:
                sbt = lambda n, s_, d: ph.enter_context(nc.sbuf_tensor(uname(n), s_, d))
                pst = lambda n, s_, d: ph.enter_context(nc.psum_tensor(uname(n), s_, d))
                NL = L
                lg = sbt("lg", [128, NL, 8], F32)
                rlg = Res()
                S.dma("sp", lg[:].rearrange("p l (d h) -> p l d h", d=2),
                      W["hg_lb_logits"].rearrange("l d (h p) -> p l d h", p=128), writes=[rlg],
                      allow_slow_non_contiguous=True)
                mx = sbt("mx", [128, 8], F32)
                lbt = sbt("lbt", [128, 8], F32)
                omlt = sbt("omlt", [128, 8], F32)
                ssum = sbt("ssum", [128, 8], F32)
                rlb = Res()
                S.op("dve", lambda: nc.vector.tensor_copy(mx[:], lg[:, 0, :]), reads=[rlg], writes=[rlb])
                for ll in range(1, NL):
                    S.op("dve", lambda: nc.vector.tensor_tensor(out=mx[:], in0=mx[:], in1=lg[:, ll, :], op=ALU.max),
                         reads=[rlg, rlb], writes=[rlb])
                for ll in range(NL):
                    S.op("dve", lambda: nc.vector.tensor_tensor(out=lg[:, ll, :], in0=lg[:, ll, :], in1=mx[:],
                                                                op=ALU.subtract), reads=[rlg, rlb], writes=[rlg])
                S.op("act", lambda: nc.scalar.activation(out=lg[:], in_=lg[:], func=AF.Exp), reads=[rlg],
                     writes=[rlg])
                S.op("dve", lambda: nc.vector.tensor_copy(ssum[:], lg[:, 0, :]), reads=[rlg], writes=[rlb])
                S.op("dve", lambda: nc.vector.memset(lbt[:], 0.0), writes=[rlb])
                for ll in range(1, NL):
                    S.op("dve", lambda: nc.vector.tensor_tensor(out=ssum[:], in0=ssum[:], in1=lg[:, ll, :],
                                                                op=ALU.add), reads=[rlg, rlb], writes=[rlb])
                    if ll <= l:
                        S.op("dve", lambda: nc.vector.tensor_tensor(out=lbt[:], in0=lbt[:], in1=lg[:, ll, :],
                                                                    op=ALU.add), reads=[rlg, rlb], writes=[rlb])
                S.op("dve", lambda: nc.vector.reciprocal(ssum[:], ssum[:]), reads=[rlb], writes=[rlb])
                S.op("dve", lambda: nc.vector.tensor_tensor(out=lbt[:], in0=lbt[:], in1=ssum[:], op=ALU.mult),
                     reads=[rlb], writes=[rlb])
                S.op("dve", lambda: nc.vector.tensor_scalar(omlt[:], lbt[:], -1.0, 1.0, ALU.mult, ALU.add),
                     reads=[rlb], writes=[rlb])
                hgain = sbt("hgain", [128, 4], F32)
                load_pc(hgain[:], W["hg_out_norm"][l], rlb)
                cf = sbt("cf", [128, 128], F32)
                rcf = Res()
                hmask = sbt("hmask", [128, 2, 128], F32)
                rhm = Res()
                for d_ in range(2):
                    S.dma("sp", hmask[:, d_, :], C["c_hgmask"][d_], writes=[rhm])
                cmask = sbt("cmask", [128, 8], F32)
                S.dma("sp", cmask[:], C["c_chunkmask"], writes=[rhm])
                rmask = sbt("rmask", [128, TT], F32)
                S.dma("sp", rmask[:], C["c_resetmask"], writes=[rhm])
                ldq = Rot([(sbt("ldq%d" % i, [128, TT], BF16), Res()) for i in range(2)])
                ldz = Rot([(sbt("ldz%d" % i, [128, TT], F32), Res()) for i in range(2)])
                ldv = Rot([(sbt("ldv%d" % i, [128, 4, 128], BF16), Res()) for i in range(2)])
                ldo = Rot([(sbt("ldo%d" % i, [128, TT], F32), Res()) for i in range(2)])
                ldg = Rot([(sbt("ldg%d" % i, [128, TT], F32), Res()) for i in range(2)])
                A = sbt("A", [128, TT], F32); rA = Res()
                B = sbt("B", [128, TT], F32); rB = Res()
                B2 = sbt("B2", [128, TT], F32); rB2 = Res()
                KK = sbt("KK", [128, TT], F32); rKK = Res()
                LF = sbt("LF", [128, TT], F32); rLF = Res()
                CUM = sbt("CUM", [128, TT], F32); rCUM = Res()
                CU2 = sbt("CU2", [128, TT], F32); rCU2 = Res()
                Qt = sbt("Qt", [128, TT], BF16); rQt = Res()
                Kt = sbt("Kt", [128, TT], BF16); rKt = Res()
                Kh = sbt("Kh", [128, TT], BF16); rKh = Res()
                dec = sbt("dec", [128, 32], F32); rdec = Res()
                AT = sbt("AT", [128, 128], BF16); rAT = Res()
                KhT = sbt("KhT", [128, 128], BF16); rKhT = Res()
                Vb = sbt("Vb", [128, 8, 128], BF16); rVb = Res()
                Sbf = sbt("Sbf", [128, 8, 128], BF16); rSbf = Res()
                osum = sbt("osum", [128, TT], F32); rosum = Res()
                sqb = sbt("sqb", [128, TT], BF16); rsqb = Res()
                rstd = sbt("rstd2", [128, TT], F32); rrstd = Res()
                bst = Rot([(sbt("bst%d" % i, [128, TT], BF16), Res()) for i in range(2)])
                hist = [[(sbt("hist%d_%d" % (h_, i), [128, 9, 128], F32), Res()) for i in range(2)]
                        for h_ in range(4)]
                ps_s = pst("ps_s", [128, TT], F32); rps_s = Res()
                ps_t = pst("ps_t", [128, 1024], BF16); rps_t = Res()
                ps_kv = [(pst("ps_kv%d" % i, [128, TT], F32), Res()) for i in range(2)]
                ps_o = pst("ps_o", [128, TT], F32); rps_o = Res()
                pstat = pst("pstat2", [128, TT], F32); rpstat = Res()
                v3 = lambda t_: t_[:].rearrange("p (n c) -> p n c", c=16)

                for dr in range(2):
                    cur = [0, 0, 0, 0]
                    for h_ in range(4):
                        S.op("dve", lambda: nc.vector.memset(hist[h_][1][0][:, 8, :], 0.0), writes=[hist[h_][1][1]])
                    tiles = list(range(NT)) if dr == 0 else list(range(NT - 1, -1, -1))
                    subs = list(range(4)) if dr == 0 else [3, 2, 1, 0]
                    for it in tiles:
                        t0 = it * TT
                        for h_ in range(4):
                            fs = slice(h_ * 128, (h_ + 1) * 128)
                            q, rq = ldq.next()
                            z, rz = ldz.next()
                            vt, rvt = ldv.next()
                            S.dma("sp", q[:], hqT_d[fs, t0:t0 + TT], writes=[rq])
                            S.dma("sp", z[:], zf_d[dr, fs, t0:t0 + TT], writes=[rz])
                            S.dma("sp", vt[:], vh_d[t0:t0 + TT, fs].rearrange("(s p) c -> p s c", p=128),
                                  writes=[rvt])
                            if dr == 1:
                                of, rof = ldo.next()
                                hg, rhg = ldg.next()
                                S.dma("sp", of[:], ohf_d[fs, t0:t0 + TT], writes=[rof])
                                S.dma("sp", hg[:], hgT_d[fs, t0:t0 + TT], writes=[rhg])
                            lbs = lbt[:, dr * 4 + h_:dr * 4 + h_ + 1]
                            oms = omlt[:, dr * 4 + h_:dr * 4 + h_ + 1]
                            S.op("act", lambda: nc.scalar.activation(out=A[:], in_=z[:], func=AF.Exp, scale=-1.0),
                                 reads=[rz], writes=[rA])
                            S.op("dve", lambda: nc.vector.tensor_single_scalar(B[:], A[:], 1.0, ALU.add),
                                 reads=[rA], writes=[rB])
                            S.op("dve", lambda: nc.vector.reciprocal(B[:], B[:]), reads=[rB], writes=[rB])
                            S.op("dve", lambda: nc.vector.scalar_tensor_tensor(
                                out=KK[:], in0=A[:], scalar=oms, in1=B[:], op0=ALU.mult, op1=ALU.mult),
                                reads=[rA, rB, rlb], writes=[rKK])
                            S.op("act", lambda: nc.scalar.activation(out=LF[:], in_=KK[:], func=AF.Ln, scale=-1.0,
                                                                     bias=onest[:]),
                                 reads=[rKK, rconst], writes=[rLF])
                            S.op("dve", lambda: nc.vector.tensor_tensor_scan(CUM[:], rmask[:], LF[:], 0.0, ALU.mult,
                                                                             ALU.add),
                                 reads=[rhm, rLF], writes=[rCUM])
                            totb = v3(CUM)[:, :, 15:16].to_broadcast([128, 32, 16])
                            if dr == 0:
                                cum, rcum = CUM, rCUM
                            else:
                                S.op("dve", lambda: nc.vector.tensor_tensor(out=v3(CU2), in0=totb, in1=v3(CUM),
                                                                            op=ALU.subtract),
                                     reads=[rCUM], writes=[rCU2])
                                S.op("dve", lambda: nc.vector.tensor_tensor(out=CU2[:], in0=CU2[:], in1=LF[:],
                                                                            op=ALU.add),
                                     reads=[rCU2, rLF], writes=[rCU2])
                                cum, rcum = CU2, rCU2
                            S.op("act", lambda: nc.scalar.activation(out=A[:], in_=cum[:], func=AF.Exp),
                                 reads=[rcum], writes=[rA])
                            S.op("dve", lambda: nc.vector.tensor_tensor(out=Qt[:], in0=q[:], in1=A[:], op=ALU.mult),
                                 reads=[rq, rA], writes=[rQt])
                            S.op("act", lambda: nc.scalar.activation(out=B[:], in_=cum[:], func=AF.Exp, scale=-1.0),
                                 reads=[rcum], writes=[rB])
                            S.op("dve", lambda: nc.vector.tensor_tensor(out=Kt[:], in0=KK[:], in1=B[:], op=ALU.mult),
                                 reads=[rKK, rB], writes=[rKt])
                            S.op("dve", lambda: nc.vector.tensor_tensor(out=v3(B2), in0=totb, in1=v3(cum),
                                                                        op=ALU.subtract),
                                 reads=[rCUM, rcum], writes=[rB2])
                            S.op("act", lambda: nc.scalar.activation(out=B2[:], in_=B2[:], func=AF.Exp),
                                 reads=[rB2], writes=[rB2])
                            S.op("dve", lambda: nc.vector.tensor_tensor(out=Kh[:], in0=KK[:], in1=B2[:],
                                                                        op=ALU.mult),
                                 reads=[rKK, rB2], writes=[rKh])
                            S.op("act", lambda: nc.scalar.activation(out=dec[:], in_=v3(CUM)[:, :, 15], func=AF.Exp),
                                 reads=[rCUM], writes=[rdec])
                            for sub in subs:
                                cs = slice(sub * 128, (sub + 1) * 128)
                                S.op("pe", lambda: nc.tensor.matmul(ps_s[:, 0:128], Kt[:, cs], Qt[:, cs], start=True,
                                                                    stop=True),
                                     reads=[rKt, rQt], writes=[rps_s])
                                S.op("dve", lambda: nc.vector.tensor_tensor(out=AT[:], in0=ps_s[:, 0:128],
                                                                            in1=hmask[:, dr, :], op=ALU.mult),
                                     reads=[rps_s, rhm], writes=[rAT])
                                S.op("pe", lambda: nc.tensor.transpose(ps_t[:, 0:128], Kh[:, cs], ident_b[:]),
                                     reads=[rKh, rconst], writes=[rps_t])
                                S.op("act", lambda: nc.scalar.copy(KhT[:], ps_t[:, 0:128]), reads=[rps_t],
                                     writes=[rKhT])
                                S.op("dve", lambda: nc.vector.tensor_tensor(
                                    out=Vb[:], in0=vt[:, sub, :].unsqueeze(1).to_broadcast([128, 8, 128]),
                                    in1=cmask[:].unsqueeze(2).to_broadcast([128, 8, 128]), op=ALU.mult),
                                    reads=[rvt, rhm], writes=[rVb])
                                for hf in range(2):
                                    S.op("pe", lambda: nc.tensor.matmul(
                                        ps_kv[hf][0][:], KhT[:],
                                        Vb[:, hf * 4:(hf + 1) * 4, :].rearrange("p a b -> p (a b)"),
                                        start=True, stop=True),
                                        reads=[rKhT, rVb], writes=[ps_kv[hf][1]])
                                hc, rhc = hist[h_][cur[h_]]
                                hp, rhp = hist[h_][1 - cur[h_]]
                                cur[h_] = 1 - cur[h_]
                                tok_lo = t0 + sub * 128
                                boundary = (dr == 0 and tok_lo == HALF_T) or (dr == 1 and tok_lo + 128 == HALF_T)
                                lk = linkt if boundary else onest
                                S.op("dve", lambda: nc.vector.tensor_scalar(hc[:, 0, :], hp[:, 8, :], lk[:, 0:1], None,
                                                                            ALU.mult),
                                     reads=[rhp, rconst], writes=[rhc])
                                for i8 in range(8):
                                    n = i8 if dr == 0 else 7 - i8
                                    kvb, rkvb = ps_kv[n // 4]
                                    S.op("dve", lambda: nc.vector.scalar_tensor_tensor(
                                        out=hc[:, i8 + 1, :], in0=hc[:, i8, :],
                                        scalar=dec[:, sub * 8 + n:sub * 8 + n + 1],
                                        in1=kvb[:, (n % 4) * 128:(n % 4 + 1) * 128], op0=ALU.mult, op1=ALU.add),
                                        reads=[rhc, rdec, rkvb], writes=[rhc])
                                S.op("act", lambda: nc.scalar.copy(Sbf[:], hc[:, 0:8, :]), reads=[rhc],
                                     writes=[rSbf])
                                S.op("pe", lambda: nc.tensor.matmul(ps_o[:, cs], vt[:, sub, :], AT[:], start=True,
                                                                    stop=False),
                                     reads=[rvt, rAT], writes=[rps_o], inc=False)
                                for i8 in range(8):
                                    n = i8 if dr == 0 else 7 - i8
                                    c0 = sub * 128 + n * 16
                                    S.op("pe", lambda: nc.tensor.matmul(ps_o[:, c0:c0 + 16], Sbf[:, i8, :],
                                                                        Qt[:, c0:c0 + 16], start=False,
                                                                        stop=(i8 == 7)),
                                         reads=[rSbf, rQt], writes=[rps_o], inc=(i8 == 7))
                            if dr == 0:
                                S.op("act", lambda: nc.scalar.copy(osum[:], ps_o[:]), reads=[rps_o], writes=[rosum])
                                S.dma("sp", ohf_d[fs, t0:t0 + TT], osum[:], reads=[rosum])
                            else:
                                S.op("dve", lambda: nc.vector.tensor_tensor(out=osum[:], in0=ps_o[:], in1=of[:],
                                                                            op=ALU.add),
                                     reads=[rps_o, rof], writes=[rosum])
                                S.op("act", lambda: nc.scalar.activation(out=sqb[:], in_=osum[:], func=AF.Square),
                                     reads=[rosum], writes=[rsqb])
                                S.op("pe", lambda: nc.tensor.matmul(pstat[:], ones_b[:], sqb[:], start=True,
                                                                    stop=True),
                                     reads=[rsqb, rconst], writes=[rpstat])
                                S.op("act", lambda: nc.scalar.activation(out=rstd[:], in_=pstat[:], func=AF.Ln,
                                                                         bias=epsb[:], scale=1.0 / 128),
                                     reads=[rpstat, rconst], writes=[rrstd])
                                S.op("act", lambda: nc.scalar.activation(out=rstd[:], in_=rstd[:], func=AF.Exp,
                                                                         scale=-0.5),
                                     reads=[rrstd], writes=[rrstd])
                                S.op("dve", lambda: nc.vector.scalar_tensor_tensor(
                                    out=hg[:], in0=hg[:], scalar=hgain[:, h_:h_ + 1], in1=rstd[:], op0=ALU.mult,
                                    op1=ALU.mult), reads=[rhg, rlb, rrstd], writes=[rhg])
                                bo, rbo = bst.next()
                                S.op("dve", lambda: nc.vector.tensor_tensor(out=bo[:], in0=osum[:], in1=hg[:],
                                                                            op=ALU.mult),
                                     reads=[rosum, rhg], writes=[rbo])
                                S.dma("sp", mixT_d[512 + h_ * 128:512 + (h_ + 1) * 128, t0:t0 + TT], bo[:],
                                      reads=[rbo])
                S.barrier()

        def phase3a(l):
            GH = 32
            NS = T // 256
            TWO_PI = 2.0 * PI
            with contextlib.ExitStack() as ph:
                sbt = lambda n, s_, d: ph.enter_context(nc.sbuf_tensor(uname(n), s_, d))
                pst = lambda n, s_, d: ph.enter_context(nc.psum_tensor(uname(n), s_, d))
                Tmat = sbt("Tmat", [128, 2, GH, 128], BF16); rTmat = Res()
                EmatT = sbt("EmatT", [128, 2, GH, 128], BF16); rEm = Res()
                Fmat = sbt("Fmat", [64, 2, 2, GH, 128], BF16); rFm = Res()
                A1 = sbt("A1", [64, 2, 2, GH], F32)
                A2 = sbt("A2", [64, 2, 2, GH], F32)
                rA12 = Res()
                link64 = linkt[0:64, 0:1]
                smask = sbt("smask", [128, 2, 128], F32); rsm = Res()
                for d_ in range(2):
                    S.dma("sp", smask[:, d_, :], C["c_s5mask"][d_], writes=[rsm])
                for gh in range(64 // GH):
                    g0 = gh * GH
                    with contextlib.ExitStack() as pp:
                        sbp = lambda n, s_, d: pp.enter_context(nc.sbuf_tensor(uname(n), s_, d))
                        psp = lambda n, s_, d: pp.enter_context(nc.psum_tensor(uname(n), s_, d))
                        Are = sbp("Are", [64, 2, GH], F32); Aim = sbp("Aim", [64, 2, GH], F32)
                        Ldt = sbp("Ldt", [64, 2, GH], F32); mt = sbp("mt", [64, 2, 26], F32)
                        rP = Res()
                        for d_ in range(2):
                            S.dma("sp", Are[:, d_, :], W["s5_a_re"][l, d_, g0:g0 + GH, :].rearrange("g p -> p g"),
                                  writes=[rP], allow_slow_non_contiguous=True)
                            S.dma("sp", Aim[:, d_, :], W["s5_a_im"][l, d_, g0:g0 + GH, :].rearrange("g p -> p g"),
                                  writes=[rP], allow_slow_non_contiguous=True)
                            S.dma("sp", Ldt[:, d_, :], W["s5_log_dt"][l, d_:d_ + 1, g0:g0 + GH].partition_broadcast(64),
                                  writes=[rP])
                        S.dma("sp", mt[:].rearrange("p d m -> p (d m)"),
                              C["c_s5exp"].rearrange("(o d) m -> o (d m)", o=1).partition_broadcast(64), writes=[rP])
                        dta = sbp("dta", [64, 2, GH], F32); ang = sbp("ang", [64, 2, GH], F32)
                        S.op("act", lambda: nc.scalar.activation(out=Ldt[:], in_=Ldt[:], func=AF.Exp), reads=[rP],
                             writes=[rP])
                        S.op("dve", lambda: nc.vector.tensor_tensor(out=dta[:], in0=Ldt[:], in1=Are[:], op=ALU.mult),
                             reads=[rP], writes=[rP])
                        S.op("dve", lambda: nc.vector.tensor_tensor(out=ang[:], in0=Ldt[:], in1=Aim[:], op=ALU.mult),
                             reads=[rP], writes=[rP])
                        SH = [64, 2, 26, GH]
                        mag = sbp("mag", SH, F32); am = sbp("am", SH, F32); tq = sbp("tq", SH, F32)
                        ti = sbp("ti", SH, I32); sn = sbp("sn", SH, F32)
                        PWre = sbp("PWre", SH, F32); PWim = sbp("PWim", SH, F32)
                        bg = lambda a_: a_[:].unsqueeze(2).to_broadcast(SH)
                        bm = lambda a_: a_[:].unsqueeze(3).to_broadcast(SH)
                        S.op("dve", lambda: nc.vector.tensor_tensor(out=mag[:], in0=bg(dta), in1=bm(mt), op=ALU.mult),
                             reads=[rP], writes=[rP])
                        S.op("act", lambda: nc.scalar.activation(out=mag[:], in_=mag[:], func=AF.Exp), reads=[rP],
                             writes=[rP])
                        S.op("dve", lambda: nc.vector.tensor_tensor(out=am[:], in0=bg(ang), in1=bm(mt), op=ALU.mult),
                             reads=[rP], writes=[rP])

                        def sin_of(dst, shift):
                            S.op("dve", lambda: nc.vector.tensor_scalar(tq[:], am[:], shift, 1.0 / TWO_PI, ALU.add,
                                                                        ALU.mult), reads=[rP], writes=[rP])
                            S.op("dve", lambda: nc.vector.tensor_copy(ti[:], tq[:]), reads=[rP], writes=[rP])
                            S.op("dve", lambda: nc.vector.tensor_copy(tq[:], ti[:]), reads=[rP], writes=[rP])
                            S.op("dve", lambda: nc.vector.scalar_tensor_tensor(
                                out=sn[:], in0=tq[:], scalar=-TWO_PI, in1=am[:], op0=ALU.mult, op1=ALU.add),
                                reads=[rP], writes=[rP])
                            if shift != 0.0:
                                S.op("dve", lambda: nc.vector.tensor_single_scalar(sn[:], sn[:], shift, ALU.add),
                                     reads=[rP], writes=[rP])
                            S.op("dve", lambda: nc.vector.tensor_single_scalar(tq[:], sn[:], PI, ALU.is_gt),
                                 reads=[rP], writes=[rP])
                            S.op("dve", lambda: nc.vector.scalar_tensor_tensor(
                                out=sn[:], in0=tq[:], scalar=-TWO_PI, in1=sn[:], op0=ALU.mult, op1=ALU.add),
                                reads=[rP], writes=[rP])
                            S.op("dve", lambda: nc.vector.tensor_single_scalar(tq[:], sn[:], -PI, ALU.is_lt),
                                 reads=[rP], writes=[rP])
                            S.op("dve", lambda: nc.vector.scalar_tensor_tensor(
                                out=sn[:], in0=tq[:], scalar=TWO_PI, in1=sn[:], op0=ALU.mult, op1=ALU.add),
                                reads=[rP], writes=[rP])
                            S.op("act", lambda: nc.scalar.activation(out=sn[:], in_=sn[:], func=AF.Sin), reads=[rP],
                                 writes=[rP])
                            S.op("dve", lambda: nc.vector.tensor_tensor(out=dst[:], in0=mag[:], in1=sn[:],
                                                                        op=ALU.mult), reads=[rP], writes=[rP])

                        sin_of(PWim, 0.0)
                        sin_of(PWre, PI / 2.0)
                        for r_ in range(2):
                            S.op("dve", lambda: nc.vector.tensor_copy(A1[:, :, r_, :], PWre[:, :, 24, :]),
                                 reads=[rP], writes=[rA12])
                        S.op("dve", lambda: nc.vector.tensor_single_scalar(A2[:, :, 0, :], PWim[:, :, 24, :], -1.0,
                                                                           ALU.mult), reads=[rP], writes=[rA12])
                        S.op("dve", lambda: nc.vector.tensor_copy(A2[:, :, 1, :], PWim[:, :, 24, :]), reads=[rP],
                             writes=[rA12])
                        SG = [64, 2, GH]
                        nr = sbp("nr", SG, F32); den = sbp("den", SG, F32); t1 = sbp("t1", SG, F32)
                        fre = sbp("fre", SG, F32); fim = sbp("fim", SG, F32)
                        P1r = PWre[:, :, 25, :]; P1i = PWim[:, :, 25, :]
                        tt = lambda o, a_, b_, op_: S.op("dve", lambda: nc.vector.tensor_tensor(out=o, in0=a_, in1=b_,
                                                                                                op=op_),
                                                         reads=[rP], writes=[rP])
                        S.op("dve", lambda: nc.vector.tensor_single_scalar(nr[:], P1r, -1.0, ALU.add), reads=[rP],
                             writes=[rP])
                        tt(den[:], Are[:], Are[:], ALU.mult)
                        tt(t1[:], Aim[:], Aim[:], ALU.mult)
                        tt(den[:], den[:], t1[:], ALU.add)
                        S.op("dve", lambda: nc.vector.reciprocal(den[:], den[:]), reads=[rP], writes=[rP])
                        tt(fre[:], nr[:], Are[:], ALU.mult)
                        tt(t1[:], P1i, Aim[:], ALU.mult)
                        tt(fre[:], fre[:], t1[:], ALU.add)
                        tt(fre[:], fre[:], den[:], ALU.mult)
                        tt(fim[:], P1i, Are[:], ALU.mult)
                        tt(t1[:], nr[:], Aim[:], ALU.mult)
                        tt(fim[:], fim[:], t1[:], ALU.subtract)
                        tt(fim[:], fim[:], den[:], ALU.mult)
                        SB = [64, 2, GH, 16]
                        Bre = sbp("Bre", SB, F32); Bim = sbp("Bim", SB, F32)
                        Bbr = sbp("Bbr", SB, F32); Bbi = sbp("Bbi", SB, F32); tb_ = sbp("tb_", SB, F32)
                        for d_ in range(2):
                            S.dma("sp", Bre[:, d_, :, :], W["s5_b_re"][l, d_, g0:g0 + GH].rearrange("g p c -> p g c"),
                                  writes=[rP])
                            S.dma("sp", Bim[:, d_, :, :], W["s5_b_im"][l, d_, g0:g0 + GH].rearrange("g p c -> p g c"),
                                  writes=[rP])
                        bc = lambda a_: a_[:].unsqueeze(3).to_broadcast(SB)
                        tt(Bbr[:], bc(fre), Bre[:], ALU.mult)
                        tt(tb_[:], bc(fim), Bim[:], ALU.mult)
                        tt(Bbr[:], Bbr[:], tb_[:], ALU.subtract)
                        tt(Bbi[:], bc(fre), Bim[:], ALU.mult)
                        tt(tb_[:], bc(fim), Bre[:], ALU.mult)
                        tt(Bbi[:], Bbi[:], tb_[:], ALU.add)
                        Cre = sbp("Cre", SB, F32); Cim = sbp("Cim", SB, F32)
                        cin = Rot([(sbp("cin%d" % i_, [128, 64], F32), Res()) for i_ in range(2)])
                        psC = psp("psC", [64, 512], F32); rpsC = Res()
                        for (src, dstc) in (("s5_c_re", Cre), ("s5_c_im", Cim)):
                            for d_ in range(2):
                                for k4 in range(GH // 8):
                                    ci, rci = cin.next()
                                    S.dma("sp", ci[:], W[src][l, d_, g0 + k4 * 8:g0 + k4 * 8 + 8].rearrange(
                                        "g c p -> (g c) p"), writes=[rci])
                                    S.op("pe", lambda: nc.tensor.transpose(psC[:, 0:128], ci[:], ident[:]),
                                         reads=[rci, rconst], writes=[rpsC])
                                    S.op("act", lambda: nc.scalar.copy(
                                        dstc[:, d_, k4 * 8:(k4 + 1) * 8, :].rearrange("p g c -> p (g c)"),
                                        psC[:, 0:128]), reads=[rpsC], writes=[rP])
                        GB2 = 16
                        SE = [64, GB2, 8, 16]
                        Er = sbp("Er", SE, F32); Ei = sbp("Ei", SE, F32)
                        Gr = sbp("Gr", SE, F32); Gi = sbp("Gi", SE, F32)
                        tA = sbp("tA", SE, F32); tB = sbp("tB", SE, F32)
                        psT = psp("psT", [128, 512], F32); rpsT = Res()
                        psE = psp("psE", [128, 512], F32); rpsE = Res()
                        for d_ in range(2):
                          for gb2 in range(GH // GB2):
                            gsl = slice(gb2 * GB2, (gb2 + 1) * GB2)
                            pw = lambda P_, s0: P_[:, d_, s0:s0 + 8, gsl].rearrange("p s g -> p g s").unsqueeze(
                                3).to_broadcast(SE)
                            bb = lambda Q_: Q_[:, d_, gsl, :].unsqueeze(2).to_broadcast(SE)

                            def cmul(outr, outi_neg, s0, Xr, Xi, negate_im):
                                tt(tA[:], pw(PWre, s0), bb(Xr), ALU.mult)
                                tt(tB[:], pw(PWim, s0), bb(Xi), ALU.mult)
                                tt(outr, tA[:], tB[:], ALU.subtract)
                                tt(tA[:], pw(PWre, s0), bb(Xi), ALU.mult)
                                tt(tB[:], pw(PWim, s0), bb(Xr), ALU.mult)
                                if negate_im:
                                    S.op("dve", lambda: nc.vector.scalar_tensor_tensor(
                                        out=outi_neg, in0=tA[:], scalar=-1.0, in1=tB[:], op0=ALU.mult,
                                        op1=ALU.subtract), reads=[rP], writes=[rP])
                                else:
                                    tt(outi_neg, tA[:], tB[:], ALU.add)

                            cmul(Er[:], Ei[:], 0, Bbr, Bbi, False)
                            cmul(Gr[:], Gi[:], 16, Cre, Cim, True)
                            for gl in range(GB2):
                                g_ = gb2 * GB2 + gl
                                e_r = Er[:, gl, :, :].rearrange("p s c -> p (s c)")
                                e_i = Ei[:, gl, :, :].rearrange("p s c -> p (s c)")
                                g_r = Gr[:, gl, :, :].rearrange("p s c -> p (s c)")
                                g_i = Gi[:, gl, :, :].rearrange("p s c -> p (s c)")
                                S.op("pe", lambda: nc.tensor.matmul(psT[:, 0:128], e_r, g_r, start=True, stop=False),
                                     reads=[rP], writes=[rpsT], inc=False)
                                S.op("pe", lambda: nc.tensor.matmul(psT[:, 0:128], e_i, g_i, start=False, stop=True),
                                     reads=[rP], writes=[rpsT])
                                S.op("dve", lambda: nc.vector.tensor_tensor(out=Tmat[:, d_, g_, :], in0=psT[:, 0:128],
                                                                            in1=smask[:, d_, :], op=ALU.mult),
                                     reads=[rpsT, rsm], writes=[rTmat])
                                S.op("pe", lambda: nc.tensor.transpose(psE[:, 0:64], e_r, ident[0:64, 0:64]),
                                     reads=[rP, rconst], writes=[rpsE], inc=False)
                                S.op("pe", lambda: nc.tensor.transpose(psE[:, 64:128], e_i, ident[0:64, 0:64]),
                                     reads=[rP, rconst], writes=[rpsE])
                                S.op("act", lambda: nc.scalar.copy(EmatT[:, d_, g_, :], psE[:, 0:128]),
                                     reads=[rpsE], writes=[rEm])
                            tt(tA[:], pw(PWre, 8), bb(Cre), ALU.mult)
                            tt(tB[:], pw(PWim, 8), bb(Cim), ALU.mult)
                            S.op("dve", lambda: nc.vector.tensor_tensor(
                                out=Fmat[:, d_, 0, gsl, :].rearrange("p g (s c) -> p g s c", c=16), in0=tA[:],
                                in1=tB[:], op=ALU.subtract), reads=[rP], writes=[rFm])
                            tt(tA[:], pw(PWre, 8), bb(Cim), ALU.mult)
                            tt(tB[:], pw(PWim, 8), bb(Cre), ALU.mult)
                            S.op("dve", lambda: nc.vector.scalar_tensor_tensor(
                                out=Fmat[:, d_, 1, gsl, :].rearrange("p g (s c) -> p g s c", c=16), in0=tA[:],
                                scalar=-1.0, in1=tB[:], op0=ALU.mult, op1=ALU.subtract), reads=[rP], writes=[rFm])
                        S.barrier()
                    with contextlib.ExitStack() as mm:
                        sbm = lambda n, s_, d: mm.enter_context(nc.sbuf_tensor(uname(n), s_, d))
                        psm = lambda n, s_, d: mm.enter_context(nc.psum_tensor(uname(n), s_, d))
                        Xtok = Rot([(sbm("Xtok%d" % i_, [32, 8, 256], BF16), Res()) for i_ in range(3)])
                        Xperm = Rot([(sbm("Xperm%d" % i_, [32, 16, 128], BF16), Res()) for i_ in range(2)])
                        U2 = sbm("U2", [128, 3, GH, 32], BF16)
                        rU2 = [Res() for _ in range(3)]
                        Zs = sbm("Zs", [64, 2, 2, GH, 32], F32); rZs = Res()
                        Xh = sbm("Xh", [64, 2, 2, GH, 32], BF16); rXh = Res()
                        Xs = sbm("Xs", [64, 2, 3, GH], F32); rXs = Res()
                        XH = sbm("XH", [64, 2, 3, GH, 33], F32); rXH = Res()
                        T1 = sbm("T1", [64, 2, 2, GH], F32); rT1 = Res()
                        T2 = sbm("T2", [64, 2, 2, GH], F32); rT2 = Res()
                        Ysb = Rot([(sbm("Ysb%d" % i_, [128, 256], F32), Res()) for i_ in range(2)])
                        Ytok = Rot([(sbm("Ytok%d" % i_, [32, 8, 256], F32), Res()) for i_ in range(2)])
                        psU = psm("psU", [128, 512], F32); rpsU = Res()
                        psZ = Rot([(psm("psZ%d" % i_, [64, 512], F32), Res()) for i_ in range(2)])
                        psY = psm("psY", [128, 512], F32); rpsY = Res()
                        psYT = psm("psYT", [32, 1024], F32); rpsYT = Res()
                        S.op("dve", lambda: nc.vector.memset(Xs[:], 0.0), writes=[rXs])
                        for j in range(NS):
                            tiles = (j, NS - 1 - j)
                            for ui in range(3):
                                d_ = 0 if ui == 0 else 1
                                tok0 = tiles[d_] * 256
                                perm = anti_b[0:32, 96:128] if ui == 1 else ident_b[0:32, 0:32]
                                for gb in range(GH // 16):
                                    xt, rxt = Xtok.next()
                                    c0 = (g0 + gb * 16) * 16
                                    S.dma("pool", xt[:], su_d[tok0:tok0 + 256, c0:c0 + 256].rearrange(
                                        "(n s) c -> n s c", s=8), writes=[rxt])
                                    xp, rxp = Xperm.next()
                                    S.op("act", lambda: nc.scalar.copy(
                                        xp[:].rearrange("n g (s c) -> n g s c", c=16),
                                        xt[:].rearrange("n s (g c) -> n g s c", c=16)), reads=[rxt], writes=[rxp])
                                    for g16 in range(16):
                                        S.op("pe", lambda: nc.tensor.matmul(
                                            psU[:, g16 * 32:(g16 + 1) * 32], xp[:, g16, :],
                                            perm, start=True, stop=True),
                                            reads=[rxp, rconst], writes=[rpsU], inc=(g16 == 15))
                                    S.op("act", lambda: nc.scalar.copy(
                                        U2[:, ui, gb * 16:(gb + 1) * 16, :].rearrange("p g n -> p (g n)"), psU[:]),
                                        reads=[rpsU], writes=[rU2[ui]])
                            for d_ in range(2):
                                for gq in range(GH // 8):
                                    pz, rpz = psZ.next()
                                    for g8 in range(8):
                                        g_ = gq * 8 + g8
                                        for ri in range(2):
                                            S.op("pe", lambda: nc.tensor.matmul(
                                                pz[:, (ri * 8 + g8) * 32:(ri * 8 + g8 + 1) * 32],
                                                EmatT[:, d_, g_, ri * 64:(ri + 1) * 64], U2[:, d_, g_, :],
                                                start=True, stop=True),
                                                reads=[rEm, rU2[d_]], writes=[rpz], inc=(g8 == 7 and ri == 1))
                                    S.op("act", lambda: nc.scalar.copy(
                                        Zs[:, d_, :, gq * 8:(gq + 1) * 8, :],
                                        pz[:].rearrange("p (r g n) -> p r g n", r=2, g=8)),
                                        reads=[rpz], writes=[rZs])
                            if j * 256 == HALF_T:
                                S.op("dve", lambda: nc.vector.tensor_scalar(Xs[:], Xs[:], link64, None, ALU.mult),
                                     reads=[rXs, rconst], writes=[rXs])
                            S.op("dve", lambda: nc.vector.tensor_copy(XH[:, :, :, :, 0], Xs[:]), reads=[rXs, rXH],
                                 writes=[rXH])
                            for i32 in range(32):
                                xc = XH[:, :, :, :, i32]
                                xn = XH[:, :, :, :, i32 + 1]
                                S.op("dve", lambda: nc.vector.tensor_tensor(out=T1[:], in0=A1[:], in1=xc[:, :, 0:2, :],
                                                                            op=ALU.mult),
                                     reads=[rA12, rXH], writes=[rT1])
                                S.op("dve", lambda: nc.vector.tensor_tensor(out=T2[:], in0=A2[:], in1=xc[:, :, 1:3, :],
                                                                            op=ALU.mult),
                                     reads=[rA12, rXH], writes=[rT2])
                                S.op("dve", lambda: nc.vector.tensor_tensor(out=T1[:], in0=T1[:], in1=T2[:],
                                                                            op=ALU.add),
                                     reads=[rT1, rT2], writes=[rT1])
                                S.op("dve", lambda: nc.vector.tensor_tensor(out=xn[:, :, 0:2, :], in0=T1[:],
                                                                            in1=Zs[:, :, :, :, i32], op=ALU.add),
                                     reads=[rT1, rZs], writes=[rXH])
                                S.op("dve", lambda: nc.vector.tensor_tensor(out=xn[:, :, 2, :], in0=T1[:, :, 0, :],
                                                                            in1=Zs[:, :, 0, :, i32], op=ALU.add),
                                     reads=[rT1, rZs], writes=[rXH])
                            S.op("dve", lambda: nc.vector.tensor_copy(Xs[:], XH[:, :, :, :, 32]), reads=[rXH],
                                 writes=[rXs])
                            S.op("act", lambda: nc.scalar.copy(Xh[:, 0, :, :, :], XH[:, 0, 0:2, :, 0:32]),
                                 reads=[rXH], writes=[rXh])
                            for i32 in range(32):
                                pass
                            for i32 in range(32):
                                S.op("act", lambda: nc.scalar.copy(Xh[:, 1, :, :, 31 - i32], XH[:, 1, 0:2, :, i32]),
                                     reads=[rXH], writes=[rXh])
                            for d_ in range(2):
                                un = 0 if d_ == 0 else 2
                                tok0 = tiles[d_] * 256
                                for gb in range(GH // 16):
                                    yt, ryt = Ytok.next()
                                    for gq2 in range(2):
                                        for g8 in range(8):
                                            g_ = gb * 16 + gq2 * 8 + g8
                                            o_ = psY[:, g8 * 32:(g8 + 1) * 32]
                                            S.op("pe", lambda: nc.tensor.matmul(o_, Tmat[:, d_, g_, :],
                                                                                U2[:, un, g_, :], start=True,
                                                                                stop=False),
                                                 reads=[rTmat, rU2[un]], writes=[rpsY], inc=False)
                                            S.op("pe", lambda: nc.tensor.matmul(o_, Fmat[:, d_, 0, g_, :],
                                                                                Xh[:, d_, 0, g_, :], start=False,
                                                                                stop=False),
                                                 reads=[rFm, rXh], writes=[rpsY], inc=False)
                                            S.op("pe", lambda: nc.tensor.matmul(o_, Fmat[:, d_, 1, g_, :],
                                                                                Xh[:, d_, 1, g_, :], start=False,
                                                                                stop=True),
                                                 reads=[rFm, rXh], writes=[rpsY], inc=(g8 == 7))
                                        ys, rys = Ysb.next()
                                        S.op("act", lambda: nc.scalar.copy(ys[:], psY[:, 0:256]), reads=[rpsY],
                                             writes=[rys])
                                        for g8 in range(8):
                                            S.op("pe", lambda: nc.tensor.transpose(
                                                psYT[:, g8 * 128:(g8 + 1) * 128], ys[:, g8 * 32:(g8 + 1) * 32],
                                                ident[:]), reads=[rys, rconst], writes=[rpsYT], inc=(g8 == 7))
                                        S.op("dve", lambda: nc.vector.tensor_copy(
                                            yt[:, :, gq2 * 128:(gq2 + 1) * 128].rearrange("n t (g c) -> n g t c", c=16),
                                            psYT[:].rearrange("n (g t c) -> n g t c", g=8, t=8)),
                                            reads=[rpsYT], writes=[ryt])
                                    c0 = (g0 + gb * 16) * 16
                                    S.dma("sp", yfb_d[d_, tok0:tok0 + 256, c0:c0 + 256].rearrange(
                                        "(n s) c -> n s c", s=8), yt[:], reads=[ryt])
                        S.barrier()
                S.barrier()

        def phase3b(l):
            GC = float(np.sqrt(2.0 / np.pi))
            with contextlib.ExitStack() as ph:
                sbt = lambda n, s_, d: ph.enter_context(nc.sbuf_tensor(uname(n), s_, d))
                pst = lambda n, s_, d: ph.enter_context(nc.psum_tensor(uname(n), s_, d))
                Wg = sbt("Wg", [128, 8, 1024], BF16); rW = Res()
                S.dma("pool", Wg[:], W["s5_w_glu"][l].rearrange("(k p) f -> p k f", p=128), writes=[rW])
                dvec = sbt("dvec", [128, 1024], F32)
                S.dma("sp", dvec[:], W["s5_d"][l].rearrange("(o f) -> o f", o=1).partition_broadcast(128), writes=[rW])
                bg = sbt("bg", [128, 8], F32); og = sbt("og", [128, 8], F32)
                load_pc(bg[:], W["s5_b_glu"][l], rW)
                load_pc(og[:], W["s5_out_norm"][l], rW)
                S.op("dve", lambda: nc.vector.tensor_single_scalar(bg[:], bg[:], -1.0, ALU.mult), reads=[rW],
                     writes=[rW])
                ld = Rot([(sbt("ld3_%d" % i_, [128, 3, 1024], F32), Res()) for i_ in range(2)])
                ya = sbt("ya", [128, 1024], F32); rya = Res()
                yb_ = sbt("yb_", [128, 1024], F32); ryb = Res()
                glb = Rot([(sbt("glb%d" % i_, [128, 1024], BF16), Res()) for i_ in range(2)])
                glT = sbt("glT", [128, 8, TT], BF16); rglT = Res()
                cT = sbt("cT", [128, 8, TT], F32); rcT = [Res() for _ in range(8)]
                et = Rot([(sbt("et%d" % i_, [128, TT], F32), Res()) for i_ in range(2)])
                sq = Rot([(sbt("sq3_%d" % i_, [128, TT], BF16), Res()) for i_ in range(2)])
                rstd = sbt("rstd3", [128, TT], F32); rrstd = Res()
                cst = Rot([(sbt("cst%d" % i_, [128, TT], BF16), Res()) for i_ in range(2)])
                psG = Rot([(pst("psG%d" % i_, [128, 1024], BF16), Res()) for i_ in range(2)])
                psM = Rot([(pst("psM%d" % i_, [128, TT], F32), Res()) for i_ in range(2)])
                pstat = pst("pstat3", [128, TT], F32); rpstat = Res()
                for it in range(NT):
                    t0 = it * TT
                    for sub in range(4):
                        tk = t0 + sub * 128
                        lt, rlt = ld.next()
                        S.dma("sp", lt[:, 0, :], yfb_d[0, tk:tk + 128, :], writes=[rlt])
                        S.dma("sp", lt[:, 1, :], yfb_d[1, tk:tk + 128, :], writes=[rlt])
                        S.dma("sp", lt[:, 2, :], su_d[tk:tk + 128, :], writes=[rlt])
                        S.op("dve", lambda: nc.vector.tensor_tensor(out=ya[:], in0=lt[:, 2, :], in1=dvec[:],
                                                                    op=ALU.mult), reads=[rlt, rW], writes=[rya])
                        S.op("dve", lambda: nc.vector.tensor_tensor(out=ya[:], in0=ya[:], in1=lt[:, 0, :], op=ALU.add),
                             reads=[rya, rlt], writes=[rya])
                        S.op("dve", lambda: nc.vector.tensor_tensor(out=ya[:], in0=ya[:], in1=lt[:, 1, :], op=ALU.add),
                             reads=[rya, rlt], writes=[rya])
                        S.op("dve", lambda: nc.vector.tensor_tensor(out=yb_[:], in0=ya[:], in1=ya[:], op=ALU.mult),
                             reads=[rya], writes=[ryb])
                        S.op("dve", lambda: nc.vector.tensor_scalar(yb_[:], yb_[:], 0.044715, 1.0, ALU.mult, ALU.add),
                             reads=[ryb], writes=[ryb])
                        S.op("dve", lambda: nc.vector.tensor_tensor(out=yb_[:], in0=yb_[:], in1=ya[:], op=ALU.mult),
                             reads=[ryb, rya], writes=[ryb])
                        S.op("dve", lambda: nc.vector.tensor_single_scalar(yb_[:], yb_[:], -30.0, ALU.max),
                             reads=[ryb], writes=[ryb])
                        S.op("act", lambda: nc.scalar.activation(out=yb_[:], in_=yb_[:], func=AF.Exp,
                                                                 scale=-2.0 * GC), reads=[ryb], writes=[ryb])
                        S.op("dve", lambda: nc.vector.tensor_single_scalar(yb_[:], yb_[:], 1.0, ALU.add),
                             reads=[ryb], writes=[ryb])
                        S.op("dve", lambda: nc.vector.reciprocal(yb_[:], yb_[:]), reads=[ryb], writes=[ryb])
                        gl, rgl = glb.next()
                        S.op("dve", lambda: nc.vector.tensor_tensor(out=gl[:], in0=ya[:], in1=yb_[:], op=ALU.mult),
                             reads=[rya, ryb], writes=[rgl])
                        pg, rpg = psG.next()
                        for k in range(8):
                            S.op("pe", lambda: nc.tensor.transpose(pg[:, k * 128:(k + 1) * 128],
                                                                   gl[:, k * 128:(k + 1) * 128], ident_b[:]),
                                 reads=[rgl, rconst], writes=[rpg], inc=(k == 7))
                        S.op("act", lambda: nc.scalar.copy(glT[:, :, sub * 128:(sub + 1) * 128],
                                                           pg[:].rearrange("p (k t) -> p k t", k=8)),
                             reads=[rpg], writes=[rglT])
                    for oc in range(8):
                        pm, rpm = psM.next()
                        for k in range(8):
                            S.op("pe", lambda: nc.tensor.matmul(pm[:], Wg[:, k, oc * 128:(oc + 1) * 128], glT[:, k, :],
                                                                start=(k == 0), stop=(k == 7)),
                                 reads=[rW, rglT], writes=[rpm], inc=(k == 7))
                        e_, re_ = et.next()
                        S.op("act", lambda: nc.scalar.activation(out=e_[:], in_=pm[:], func=AF.Exp, scale=-1.0,
                                                                 bias=bg[:, oc:oc + 1]), reads=[rpm, rW], writes=[re_])
                        S.op("dve", lambda: nc.vector.tensor_single_scalar(e_[:], e_[:], 1.0, ALU.add), reads=[re_],
                             writes=[re_])
                        S.op("dve", lambda: nc.vector.reciprocal(e_[:], e_[:]), reads=[re_], writes=[re_])
                        S.op("dve", lambda: nc.vector.tensor_tensor(out=cT[:, oc, :], in0=glT[:, oc, :], in1=e_[:],
                                                                    op=ALU.mult), reads=[rglT, re_], writes=[rcT[oc]])
                        sq_, rsq_ = sq.next()
                        S.op("act", lambda: nc.scalar.activation(out=sq_[:], in_=cT[:, oc, :], func=AF.Square),
                             reads=[rcT[oc]], writes=[rsq_])
                        S.op("pe", lambda: nc.tensor.matmul(pstat[:], ones_b[:], sq_[:], start=(oc == 0),
                                                            stop=(oc == 7)), reads=[rsq_, rconst], writes=[rpstat])
                    S.op("act", lambda: nc.scalar.activation(out=rstd[:], in_=pstat[:], func=AF.Ln, bias=epsb[:],
                                                             scale=1.0 / 1024), reads=[rpstat, rconst], writes=[rrstd])
                    S.op("act", lambda: nc.scalar.activation(out=rstd[:], in_=rstd[:], func=AF.Exp, scale=-0.5),
                         reads=[rrstd], writes=[rrstd])
                    for oc in range(8):
                        cs_, rcs_ = cst.next()
                        S.op("dve", lambda: nc.vector.scalar_tensor_tensor(
                            out=cs_[:], in0=cT[:, oc, :], scalar=og[:, oc:oc + 1], in1=rstd[:], op0=ALU.mult,
                            op1=ALU.mult), reads=[rcT[oc], rW, rrstd], writes=[rcs_])
                        S.dma("sp", mixT_d[1024 + oc * 128:1024 + (oc + 1) * 128, t0:t0 + TT], cs_[:], reads=[rcs_])
                S.barrier()

        def phase4a(l):
            with contextlib.ExitStack() as ph:
                sbt = lambda n, s_, d: ph.enter_context(nc.sbuf_tensor(uname(n), s_, d))
                pst = lambda n, s_, d: ph.enter_context(nc.psum_tensor(uname(n), s_, d))
                NTB = T // 128
                HR = NROW // 2
                qT = sbt("qTs", [128, 4, T], BF16)
                kT = sbt("kTs", [128, 4, T], BF16)
                va = sbt("va", [128, NTB, 512], BF16)
                vb = sbt("vb", [128, NTB - 1, 512], BF16)
                rin = Res()
                for c in range(4):
                    S.dma("sp", qT[:, c, :], qT_d[c * 128:(c + 1) * 128, :], writes=[rin])
                    S.dma("sp", kT[:, c, :], kT_d[c * 128:(c + 1) * 128, :], writes=[rin])
                S.dma("sp", va[:], v_d.rearrange("(n p) c -> p n c", p=128), writes=[rin])
                S.dma("sp", vb[:], v_d[64:T - 64, :].rearrange("(n p) c -> p n c", p=128), writes=[rin])
                biasE = Rot([(sbt("biasE%d" % i_, [128, 4096], F32), Res()) for i_ in range(2)])
                biasB = Rot([(sbt("biasB%d" % i_, [128, 4096], BF16), Res()) for i_ in range(2)])
                bias4 = sbt("bias4", [128, 4096], BF16)
                rb4 = Res()
                b4f, rb4f = biasE.next()
                S.dma("sp", b4f[0:64, :], C["nabias"][l, 4], writes=[rb4f])
                S.dma("sp", b4f[64:128, :], C["nabias"][l, 4], writes=[rb4f])
                S.op("act", lambda: nc.scalar.copy(bias4[:], b4f[:]), reads=[rb4f], writes=[rb4])
                gainb = sbt("gainb", [64, 512], F32)
                S.dma("sp", gainb[:], W["attn_out_norm"][l].rearrange("(o f) -> o f", o=1).partition_broadcast(64),
                      writes=[rin])
                Pm = Rot([(sbt("Pm%d" % i_, [64, 512], BF16), Res()) for i_ in range(2)])
                PT = Rot([(sbt("PT%d" % i_, [128, 4, 64], BF16), Res()) for i_ in range(2)])
                stat = Rot([(sbt("nst%d" % i_, [64, 4], F32), Res()) for i_ in range(4)])
                araw = [(sbt("araw%d" % i_, [64, 512], F32), Res()) for i_ in range(2)]
                anb = sbt("anb", [64, 512], BF16); ranb = Res()
                junk = sbt("junk", [64, 512], BF16); rjunk = Res()
                nst2 = sbt("nst2", [64, 2], F32); rnst2 = Res()
                aT = Rot([(sbt("aT%d" % i_, [128, 4, TT], BF16), Res()) for i_ in range(2)])
                ps_S = Rot([(pst("psS%d" % i_, [64, 512], F32), Res()) for i_ in range(2)])
                ps_T = Rot([(pst("psT%d" % i_, [128, 1024], BF16), Res()) for i_ in range(2)])
                ps_O = [(pst("psO%d" % i_, [64, 512], F32), Res()) for i_ in range(2)]
                ps_A = pst("psA", [128, 1024], BF16); rps_A = Res()

                def attend(r, rs, vi):
                    dl = r - rs
                    if dl == 4:
                        bt, rbt = bias4, rb4
                    else:
                        bf_, rbf_ = biasE.next()
                        S.dma("sp", bf_[0:64, :], C["nabias"][l, dl], writes=[rbf_])
                        S.dma("sp", bf_[64:128, :], C["nabias"][l, dl], writes=[rbf_])
                        bt, rbt = biasB.next()
                        S.op("act", lambda: nc.scalar.copy(bt[:], bf_[:]), reads=[rbf_], writes=[rbt])
                    po, rpo = ps_O[vi]
                    ar, rar = araw[vi]
                    for hh in range(8):
                        c = hh // 2
                        pb0 = (hh % 2) * 64
                        pS, rpS = ps_S.next()
                        S.op("pe", lambda: nc.tensor.matmul(pS[:], qT[pb0:pb0 + 64, c, r * 64:(r + 1) * 64],
                                                            kT[pb0:pb0 + 64, c, rs * 64:(rs + 8) * 64], start=True,
                                                            stop=False),
                             reads=[rin], writes=[rpS], inc=False)
                        S.op("pe", lambda: nc.tensor.matmul(pS[:], ident_b[pb0:pb0 + 64, pb0:pb0 + 64],
                                                            bt[pb0:pb0 + 64, hh * 512:(hh + 1) * 512], start=False,
                                                            stop=True),
                             reads=[rbt, rconst], writes=[rpS])
                        st_, rst_ = stat.next()
                        pm, rpm = Pm.next()
                        S.op("act", lambda: nc.scalar.activation(out=pm[:], in_=pS[:], func=AF.Exp,
                                                                 accum_out=st_[:, 2:3]),
                             reads=[rpS], writes=[rpm, rst_])
                        pT, rpT = ps_T.next()
                        for j in range(4):
                            S.op("pe", lambda: nc.tensor.transpose(pT[:, j * 64:(j + 1) * 64],
                                                                   pm[:, j * 128:(j + 1) * 128], ident_b[0:64, 0:64]),
                                 reads=[rpm, rconst], writes=[rpT], inc=(j == 3))
                        pt_, rpt_ = PT.next()
                        S.op("act", lambda: nc.scalar.copy(pt_[:].rearrange("p a b -> p (a b)"), pT[:, 0:256]),
                             reads=[rpT], writes=[rpt_])
                        for j in range(4):
                            if rs % 2 == 0:
                                vsrc = va[:, rs // 2 + j, hh * 64:(hh + 1) * 64]
                            else:
                                vsrc = vb[:, (rs - 1) // 2 + j, hh * 64:(hh + 1) * 64]
                            S.op("pe", lambda: nc.tensor.matmul(po[:, hh * 64:(hh + 1) * 64], pt_[:, j, :], vsrc,
                                                                start=(j == 0), stop=(j == 3)),
                                 reads=[rpt_, rin], writes=[rpo], inc=(j == 3))
                        S.op("dve", lambda: nc.vector.reciprocal(st_[:, 3:4], st_[:, 2:3]), reads=[rst_],
                             writes=[rst_])
                        S.op("dve", lambda: nc.vector.tensor_scalar(ar[:, hh * 64:(hh + 1) * 64],
                                                                    po[:, hh * 64:(hh + 1) * 64], st_[:, 3:4], None,
                                                                    ALU.mult),
                             reads=[rpo, rst_], writes=[rar])

                for it in range(NT):
                    t0 = it * TT
                    at, rat = aT.next()
                    for r8 in range(8):
                        r = it * 8 + r8
                        rs_s = min(max(r - 4, 0), NROW - 8)
                        base = 0 if r < HR else HR
                        rs_p = base + min(max(r - base - 4, 0), HR - 8)
                        attend(r, rs_s, 0)
                        ar, rar = araw[0]
                        if rs_p != rs_s:
                            attend(r, rs_p, 1)
                            ap_, rap = araw[1]
                            S.op("dve", lambda: nc.vector.tensor_tensor(out=ar[:], in0=ar[:], in1=ap_[:],
                                                                        op=ALU.subtract),
                                 reads=[rar, rap], writes=[rar])
                            S.op("dve", lambda: nc.vector.scalar_tensor_tensor(
                                out=ar[:], in0=ar[:], scalar=linkt[0:64, 0:1], in1=ap_[:], op0=ALU.mult,
                                op1=ALU.add), reads=[rar, rap, rconst], writes=[rar])
                        S.op("act", lambda: nc.scalar.activation(out=junk[:], in_=ar[:], func=AF.Square,
                                                                 accum_out=nst2[:, 0:1]),
                             reads=[rar], writes=[rjunk, rnst2])
                        S.op("act", lambda: nc.scalar.activation(out=nst2[:, 1:2], in_=nst2[:, 0:1], func=AF.Ln,
                                                                 bias=epsb[0:64, :], scale=1.0 / 512),
                             reads=[rnst2, rconst], writes=[rnst2])
                        S.op("act", lambda: nc.scalar.activation(out=nst2[:, 1:2], in_=nst2[:, 1:2], func=AF.Exp,
                                                                 scale=-0.5),
                             reads=[rnst2], writes=[rnst2])
                        S.op("dve", lambda: nc.vector.scalar_tensor_tensor(
                            out=anb[:], in0=ar[:], scalar=nst2[:, 1:2], in1=gainb[:], op0=ALU.mult, op1=ALU.mult),
                            reads=[rar, rnst2, rin], writes=[ranb])
                        for c in range(4):
                            S.op("pe", lambda: nc.tensor.transpose(ps_A[:, c * 64:(c + 1) * 64],
                                                                   anb[:, c * 128:(c + 1) * 128],
                                                                   ident_b[0:64, 0:64]),
                                 reads=[ranb, rconst], writes=[rps_A], inc=(c == 3))
                        S.op("act", lambda: nc.scalar.copy(at[:, :, r8 * 64:(r8 + 1) * 64],
                                                           ps_A[:, 0:256].rearrange("p (c t) -> p c t", c=4)),
                             reads=[rps_A], writes=[rat])
                    S.dma("sp", mixT_d[0:512, t0:t0 + TT].rearrange("(c p) t -> p c t", p=128), at[:], reads=[rat])
                S.barrier()

        def phase4b(l):
            with contextlib.ExitStack() as ph:
                tl = TL(ph)
                sbt = tl.sbt
                x, rx, h, rh = tl.x, tl.rx, tl.h, tl.rh
                load_pc(tl.gains[:, 0, :], W["ffn2_norm"][l], tl.rgains)
                load_pc(tl.gains[:, 1, :], W["final_norm"][l], tl.rgains)
                last = (l == L - 1)
                if last:
                    yt_rot = Rot([(sbt("yt%d" % i, [128, D], F32), Res()) for i in range(2)])
                wov = W["w_out"][l].rearrange("(k p) f -> p k f", p=128)
                for it in range(NT):
                    t0 = it * TT
                    tl.load_x(it)
                    S.dma("sp", h[:], mixT_d[:, t0:t0 + TT].rearrange("(c p) t -> p c t", p=128), writes=[rh])
                    NB = D // 256
                    nxt = tl.wgu_rot.next()
                    S.dma("pool", nxt[0][:], wov[:, :, 0:256], writes=[nxt[1]])
                    for b in range(NB):
                        wb, rwb = nxt
                        if b + 1 < NB:
                            nxt = tl.wgu_rot.next()
                            S.dma("pool", nxt[0][:], wov[:, :, (b + 1) * 256:(b + 2) * 256], writes=[nxt[1]])
                        for jj in range(2):
                            i = 2 * b + jj
                            ps, rps = tl.pb_rot.next()
                            for k in range(NKC):
                                S.op("pe", lambda: nc.tensor.matmul(ps[:], wb[:, k, jj * 128:(jj + 1) * 128],
                                                                    h[:, k, :], start=(k == 0), stop=(k == NKC - 1)),
                                     reads=[rwb, rh], writes=[rps], inc=(k == NKC - 1))
                            S.op("dve", lambda: nc.vector.tensor_tensor(out=x[:, i, :], in0=ps[:], in1=x[:, i, :],
                                                                        op=ALU.add),
                                 reads=[rps, rx[i]], writes=[rx[i]])
                    tl.rmsnorm(0)
                    tl.ffn(W["ffn2_w_gate"][l], W["ffn2_w_up"][l], W["ffn2_w_down"][l])
                    tl.rmsnorm(1, inplace=True)
                    if not last:
                        tl.store_x(it)
                    else:
                        for tb in range(4):
                            yt, ryt = yt_rot.next()
                            for cq in range(4):
                                pb, rpb = tl.pb_rot.next()
                                for i4 in range(4):
                                    c = cq * 4 + i4
                                    S.op("pe", lambda: nc.tensor.transpose(pb[:, i4 * 128:(i4 + 1) * 128],
                                                                           x[:, c, tb * 128:(tb + 1) * 128],
                                                                           ident[:]),
                                         reads=[rx[c], rconst], writes=[rpb], inc=(i4 == 3))
                                if cq % 2 == 0:
                                    S.op("act", lambda: nc.scalar.copy(yt[:, cq * 512:(cq + 1) * 512], pb[:]),
                                         reads=[rpb], writes=[ryt])
                                else:
                                    S.op("dve", lambda: nc.vector.tensor_copy(yt[:, cq * 512:(cq + 1) * 512], pb[:]),
                                         reads=[rpb], writes=[ryt])
                            S.dma("sp", y_out[t0 + tb * 128:t0 + (tb + 1) * 128, :], yt[:], reads=[ryt])
                S.barrier()

        for l in range(L):
            if "p1" in phases:
                phase1(l)
            if "p2" in phases:
                phase2(l)
            if "p3a" in phases:
                phase3a(l)
            if "p3b" in phases:
                phase3b(l)
            if "p4a" in phases:
                phase4a(l)
            if "p4b" in phases:
                phase4b(l)
    return nc


T_CORE = 4096
DEPTH = 4
ALL_PHASES = ("p1", "p2", "p3a", "p3b", "p4a", "p4b")


def kernel(**inputs):
    xp = np.ascontiguousarray(np.asarray(inputs["x_prompt"], dtype=np.float32))
    xs = np.ascontiguousarray(np.asarray(inputs["x_sample"], dtype=np.float32))
    wts = {n: np.ascontiguousarray(np.asarray(inputs[n], dtype=np.float32)) for n, _ in WSPECS}
    rel_bias = np.asarray(inputs["rel_bias"], dtype=np.float32)
    nc = build(T_CORE, DEPTH, dbg=False, phases=ALL_PHASES)
    consts = [host_consts(T_CORE, DEPTH, rel_bias, 0), host_consts(T_CORE, DEPTH, rel_bias, 1)]
    in_maps = []
    for core in range(8):
        if core < 4:
            x = xp[2 * core:2 * core + 2].reshape(T_CORE, D)
            cst = consts[0]
        else:
            x = xs[core - 4].reshape(T_CORE, D)
            cst = consts[1]
        m = {"x": x}
        m.update(wts)
        m.update(cst)
        in_maps.append(m)
    res = run_bass_kernel_spmd(nc, in_maps, core_ids=list(range(8)))
    outs = [np.asarray(r["y"], dtype=np.float32) for r in res.results]
    y_prompt = np.concatenate([o.reshape(2, 2048, D) for o in outs[:4]], axis=0)
    y_sample = np.stack([o.reshape(4096, D) for o in outs[4:]], axis=0)
    return (y_prompt, y_sample)
```

```python
import contextlib
import numpy as np
import concourse.bass as bass
import concourse.mybir as mybir
from concourse.bass_utils import run_bass_kernel_spmd

F32 = mybir.dt.float32
BF16 = mybir.dt.bfloat16
I32 = mybir.dt.int32
AF = mybir.ActivationFunctionType
ALU = mybir.AluOpType
AX = mybir.AxisListType

D = 2048
FF = 5632
NKC = 16
NFC = 44
TT = 512
INC = 5120
EPS = 1e-6
NEG = -30000.0
STOP = None
NBLIM = None
PI = float(np.pi)


class Res:
    __slots__ = ("w", "r")

    def __init__(self):
        self.w = None
        self.r = {}


class Sched:
    def __init__(self, nc, es, n_dma=14):
        self.nc = nc
        self.eng = {"pe": nc.tensor, "act": nc.scalar, "dve": nc.vector, "pool": nc.gpsimd, "sp": nc.sync}
        self.sem = {}
        self.cnt = {}
        for e in ("pe", "act", "dve", "pool"):
            self.sem[e] = es.enter_context(nc.semaphore("s_" + e))
            self.cnt[e] = 0
        self.dq = {"sp": [], "pool": []}
        self.dq_next = {"sp": 0, "pool": 0}
        for q in ("sp", "pool"):
            for i in range(n_dma):
                pid = "d_%s_%d" % (q, i)
                self.sem[pid] = es.enter_context(nc.semaphore(pid))
                self.cnt[pid] = 0
                self.dq[q].append(pid)
        self.seen = {e: {} for e in self.eng}
        self.n_ins = 0

    def _wait(self, e, pid, val):
        if val > 0 and self.seen[e].get(pid, 0) < val:
            self.eng[e].wait_ge(self.sem[pid], val)
            self.seen[e][pid] = val

    def _deps(self, e, reads, writes):
        deps = {}
        for b in reads:
            if b.w is not None:
                p, v = b.w
                if deps.get(p, 0) < v:
                    deps[p] = v
        for b in writes:
            if b.w is not None:
                p, v = b.w
                if deps.get(p, 0) < v:
                    deps[p] = v
            for p, v in b.r.items():
                if deps.get(p, 0) < v:
                    deps[p] = v
        for p, v in deps.items():
            if p == e and e == "pe":
                continue
            self._wait(e, p, v)

    def _record(self, pid, val, reads, writes):
        for b in reads:
            if b.r.get(pid, 0) < val:
                b.r[pid] = val
        for b in writes:
            b.w = (pid, val)
            b.r = {}

    def op(self, e, fn, reads=(), writes=(), inc=True):
        self._deps(e, reads, writes)
        ins = fn()
        tick = self.cnt[e] + 1
        if inc:
            ins.then_inc(self.sem[e], 1)
            self.cnt[e] = tick
        self._record(e, tick, reads, writes)
        self.n_ins += 1
        return ins

    def dma(self, q, out, in_, reads=(), writes=(), **kw):
        self._deps(q, reads, writes)
        i = self.dq_next[q]
        self.dq_next[q] = (i + 1) % len(self.dq[q])
        pid = self.dq[q][i]
        prev = self.cnt[pid]
        self._wait(q, pid, prev)
        ins = self.eng[q].dma_start(out=out, in_=in_, **kw)
        ins.then_inc(self.sem[pid], 16)
        self.cnt[pid] = prev + 16
        self._record(pid, prev + 16, reads, writes)
        self.n_ins += 1
        return ins

    def barrier(self):
        for e in self.eng:
            for pid, v in self.cnt.items():
                if pid == e and e == "pe":
                    continue
                self._wait(e, pid, v)


class Rot:
    def __init__(self, items):
        self.items = items
        self.i = 0

    def next(self):
        it = self.items[self.i]
        self.i = (self.i + 1) % len(self.items)
        return it


WSPECS = [
    ("ffn1_norm", (D,)), ("ffn1_w_gate", (D, FF)), ("ffn1_w_up", (D, FF)), ("ffn1_w_down", (FF, D)),
    ("mix_norm", (D,)), ("w_in", (D, INC)), ("q_norm", (64,)), ("k_norm", (64,)),
    ("attn_out_norm", (512,)), ("hg_lb_logits", (2, 512)), ("hg_out_norm", (512,)),
    ("s5_a_re", (2, 64, 64)), ("s5_a_im", (2, 64, 64)), ("s5_log_dt", (2, 64)),
    ("s5_b_re", (2, 64, 64, 16)), ("s5_b_im", (2, 64, 64, 16)), ("s5_c_re", (2, 64, 16, 64)),
    ("s5_c_im", (2, 64, 16, 64)), ("s5_d", (1024,)), ("s5_w_glu", (1024, 1024)), ("s5_b_glu", (1024,)),
    ("s5_out_norm", (1024,)), ("w_out", (D, D)), ("ffn2_norm", (D,)), ("ffn2_w_gate", (D, FF)),
    ("ffn2_w_up", (D, FF)), ("ffn2_w_down", (FF, D)), ("final_norm", (D,)),
]


def host_consts(T, L, rel_bias, link):
    c = {}
    c["c_ident"] = np.eye(128, dtype=np.float32)
    c["c_anti"] = np.eye(128, dtype=np.float32)[::-1].copy()
    bo = np.zeros((128, 128), np.float32)
    bo[:64, :64] = 1.0
    bo[64:, 64:] = 1.0
    c["c_blockones"] = bo
    s = np.arange(128)
    same = (s[:, None] // 16) == (s[None, :] // 16)
    hm = np.zeros((2, 128, 128), np.float32)
    hm[0] = (same & (s[:, None] <= s[None, :])).astype(np.float32)
    hm[1] = (same & (s[:, None] >= s[None, :])).astype(np.float32)
    c["c_hgmask"] = hm
    c["c_chunkmask"] = ((s[:, None] // 16) == np.arange(8)[None, :]).astype(np.float32)
    rm = np.ones((128, TT), np.float32)
    rm[:, ::16] = 0.0
    c["c_resetmask"] = rm
    sp = s // 16
    sm = np.zeros((2, 128, 128), np.float32)
    sm[0] = (sp[None, :] >= sp[:, None]).astype(np.float32)
    sm[1] = (sp[:, None] >= sp[None, :]).astype(np.float32)
    c["c_s5mask"] = sm
    c["link"] = np.full((128, 1), float(link), np.float32)
    ex = np.zeros((2, 26), np.float32)
    k8 = np.arange(8)
    ex[0, 0:8] = 7 - k8; ex[1, 0:8] = k8
    ex[0, 8:16] = k8 + 1; ex[1, 8:16] = 8 - k8
    ex[0, 16:24] = k8 - 7; ex[1, 16:24] = -k8
    ex[:, 24] = 8; ex[:, 25] = 1
    c["c_s5exp"] = ex
    q = np.arange(64)
    cs = np.clip(q - 8, 0, 48)
    j = np.arange(64)
    inwin = (j[None, :] >= cs[:, None]) & (j[None, :] < cs[:, None] + 16)
    jj = np.clip(j[None, :] - q[:, None] + 15, 0, 30)
    nb = np.full((L, 8, 64, 8, 8, 64), NEG, np.float32)
    for dl in range(8):
        for a in range(8):
            rb = rel_bias[:L, :, a - dl + 7, :]
            g = rb[:, :, jj]
            g = np.where(inwin[None, None], g, np.float32(NEG))
            nb[:, dl, :, :, a, :] = np.transpose(g, (0, 2, 1, 3))
    c["nabias"] = nb.reshape(L, 8, 64, 8 * 512)
    return c


CONST_SHAPES = lambda L: {
    "c_ident": (128, 128), "c_anti": (128, 128), "c_blockones": (128, 128), "c_hgmask": (2, 128, 128),
    "c_chunkmask": (128, 8), "c_resetmask": (128, TT), "c_s5mask": (2, 128, 128), "link": (128, 1),
    "c_s5exp": (2, 26),
    "nabias": (L, 8, 64, 8 * 512),
}


def build(T, L, dbg=False, phases=("p1", "p2", "p3", "p4")):
    NT = T // TT
    NROW = T // 64
    HALF_T = T // 2
    nc = bass.Bass("TRN2", target_bir_lowering=False)

    def din(name, shape, dt=F32):
        return nc.dram_tensor(name, list(shape), dt, kind="ExternalInput").ap()

    def dscr(name, shape, dt):
        return nc.dram_tensor(name, list(shape), dt, kind=("ExternalOutput" if dbg else "Internal")).ap()

    x_in = din("x", [T, D])
    W = {n: din(n, (L,) + tuple(s)) for n, s in WSPECS}
    C = {n: din(n, s) for n, s in CONST_SHAPES(L).items()}
    y_out = nc.dram_tensor("y", [T, D], F32, kind="ExternalOutput").ap()

    xbuf = dscr("xbuf", [D, T], F32)
    qT_d = dscr("qT", [512, T], BF16)
    kT_d = dscr("kT", [512, T], BF16)
    v_d = dscr("vtok", [T, 512], BF16)
    hqT_d = dscr("hqT", [512, T], BF16)
    zf_d = dscr("zf", [2, 512, T], F32)
    vh_d = dscr("vh", [T, 512], BF16)
    hgT_d = dscr("hgT", [512, T], F32)
    su_d = dscr("sutok", [T, 1024], F32)
    ohf_d = dscr("ohf", [512, T], F32)
    yfb_d = dscr("yfb", [2, T, 1024], F32)
    mixT_d = dscr("mixT", [D, T], BF16)

    es = contextlib.ExitStack()
    with es:
        S = Sched(nc, es)

        _uid = [0]

        def uname(n):
            _uid[0] += 1
            return "t%d_%s" % (_uid[0], n)

        def gsb(name, shape, dt):
            return es.enter_context(nc.sbuf_tensor(uname(name), shape, dt))

        ident = gsb("ident", [128, 128], F32)
        ident_b = gsb("ident_b", [128, 128], BF16)
        anti_b = gsb("anti_b", [128, 128], BF16)
        ones_b = gsb("ones_b", [128, 128], BF16)
        bones_b = gsb("bones_b", [128, 128], BF16)
        epsb = gsb("epsb", [128, 1], F32)
        linkt = gsb("linkt", [128, 1], F32)
        onest = gsb("onest", [128, 1], F32)
        rconst = Res()
        tmpc = gsb("tmpc", [128, 128], F32)
        rtmpc = Res()
        S.dma("sp", ident[:], C["c_ident"], writes=[rconst])
        S.op("dve", lambda: nc.vector.tensor_copy(ident_b[:], ident[:]), reads=[rconst], writes=[rconst])
        S.dma("sp", tmpc[:], C["c_anti"], writes=[rtmpc])
        S.op("dve", lambda: nc.vector.tensor_copy(anti_b[:], tmpc[:]), reads=[rtmpc], writes=[rconst])
        S.dma("sp", tmpc[:], C["c_blockones"], reads=[], writes=[rtmpc])
        S.op("dve", lambda: nc.vector.tensor_copy(bones_b[:], tmpc[:]), reads=[rtmpc], writes=[rconst])
        S.op("dve", lambda: nc.vector.memset(ones_b[:], 1.0), writes=[rconst])
        S.op("dve", lambda: nc.vector.memset(epsb[:], EPS), writes=[rconst])
        S.op("dve", lambda: nc.vector.memset(onest[:], 1.0), writes=[rconst])
        S.dma("sp", linkt[:], C["link"], writes=[rconst])
        S.barrier()

        def load_pc(dst, src1d, res):
            S.dma("sp", dst, src1d.rearrange("(c p) -> p c", p=128), writes=[res],
                  allow_slow_non_contiguous=True)

        class TL:
            def __init__(self, ph, with_ffn=True):
                sbt = lambda n, s, d: ph.enter_context(nc.sbuf_tensor(uname(n), s, d))
                pst = lambda n, s, d: ph.enter_context(nc.psum_tensor(uname(n), s, d))
                self.sbt, self.pst = sbt, pst
                self.x = sbt("x", [128, NKC, TT], F32)
                self.rx = [Res() for _ in range(NKC)]
                self.h = sbt("h", [128, NKC, TT], BF16)
                self.rh = Res()
                self.sq = Rot([(sbt("sq%d" % i, [128, TT], BF16), Res()) for i in range(2)])
                self.rstd = sbt("rstd", [128, TT], F32)
                self.rrstd = Res()
                self.tmp = Rot([(sbt("tmp%d" % i, [128, TT], F32), Res()) for i in range(2)])
                self.wgu = [(sbt("wgu%d" % i, [128, NKC, 256], BF16), Res()) for i in range(4)]
                self.wgu_rot = Rot(self.wgu)
                self.pbank = [(pst("pb%d" % i, [128, TT], F32), Res()) for i in range(4)]
                self.pb_rot = Rot(self.pbank)
                self.pstat = pst("pstat", [128, TT], F32)
                self.rpstat = Res()
                if with_ffn:
                    self.g = sbt("g", [128, NFC, TT], BF16)
                    self.rg = [Res() for _ in range(NFC)]
                    self.wd = Rot([(sbt("wd%d" % i, [128, NFC // 2, 256], BF16), Res()) for i in range(3)])
                    self.pd = Rot([(pst("pd%d" % i, [128, TT], F32), Res()) for i in range(2)])
                self.gains = sbt("gains", [128, 4, NKC], F32)
                self.rgains = Res()

            def rmsnorm(self, gi, inplace=False):
                x, rx = self.x, self.rx
                for c in range(NKC):
                    sq, rsq = self.sq.next()
                    S.op("act", lambda: nc.scalar.activation(out=sq[:], in_=x[:, c, :], func=AF.Square),
                         reads=[rx[c]], writes=[rsq])
                    S.op("pe", lambda: nc.tensor.matmul(self.pstat[:], ones_b[:], sq[:], start=(c == 0),
                                                        stop=(c == NKC - 1)),
                         reads=[rsq, rconst], writes=[self.rpstat])
                S.op("act", lambda: nc.scalar.activation(out=self.rstd[:], in_=self.pstat[:], func=AF.Ln,
                                                         bias=epsb[:], scale=1.0 / D),
                     reads=[self.rpstat, rconst], writes=[self.rrstd])
                S.op("act", lambda: nc.scalar.activation(out=self.rstd[:], in_=self.rstd[:], func=AF.Exp,
                                                         scale=-0.5),
                     reads=[self.rrstd], writes=[self.rrstd])
                for c in range(NKC):
                    if inplace:
                        S.op("dve", lambda: nc.vector.scalar_tensor_tensor(
                            out=x[:, c, :], in0=x[:, c, :], scalar=self.gains[:, gi, c:c + 1], in1=self.rstd[:],
                            op0=ALU.mult, op1=ALU.mult),
                            reads=[rx[c], self.rrstd, self.rgains], writes=[rx[c]])
                    else:
                        S.op("dve", lambda: nc.vector.scalar_tensor_tensor(
                            out=self.h[:, c, :], in0=x[:, c, :], scalar=self.gains[:, gi, c:c + 1],
                            in1=self.rstd[:], op0=ALU.mult, op1=ALU.mult),
                            reads=[rx[c], self.rrstd, self.rgains], writes=[self.rh])

            def ffn(self, wg_ap, wu_ap, wd_ap):
                wgv = wg_ap.rearrange("(k p) f -> p k f", p=128)
                wuv = wu_ap.rearrange("(k p) f -> p k f", p=128)
                wdv = wd_ap.rearrange("(j p) d -> p j d", p=128)
                x, rx, h, rh, g, rg = self.x, self.rx, self.h, self.rh, self.g, self.rg

                def load_gu(jb):
                    tg, rtg = self.wgu_rot.next()
                    tu, rtu = self.wgu_rot.next()
                    S.dma("pool", tg[:], wgv[:, :, jb * 256:(jb + 1) * 256], writes=[rtg])
                    S.dma("pool", tu[:], wuv[:, :, jb * 256:(jb + 1) * 256], writes=[rtu])
                    return tg, rtg, tu, rtu

                def load_d(q):
                    db, jh = q // 2, q % 2
                    td, rtd = self.wd.next()
                    S.dma("pool", td[:], wdv[:, jh * 22:(jh + 1) * 22, db * 256:(db + 1) * 256], writes=[rtd])
                    return td, rtd

                NJB = FF // 256
                nxt = load_gu(0)
                dq = [load_d(0)]
                for jb in range(NJB):
                    tg, rtg, tu, rtu = nxt
                    if jb + 1 < NJB:
                        nxt = load_gu(jb + 1)
                    else:
                        dq.append(load_d(1))
                    for jj in range(2):
                        j = 2 * jb + jj
                        pg, rpg = self.pb_rot.next()
                        pu, rpu = self.pb_rot.next()
                        for k in range(NKC):
                            S.op("pe", lambda: nc.tensor.matmul(pg[:], tg[:, k, jj * 128:(jj + 1) * 128], h[:, k, :],
                                                                start=(k == 0), stop=(k == NKC - 1)),
                                 reads=[rtg, rh], writes=[rpg], inc=(k == NKC - 1))
                        for k in range(NKC):
                            S.op("pe", lambda: nc.tensor.matmul(pu[:], tu[:, k, jj * 128:(jj + 1) * 128], h[:, k, :],
                                                                start=(k == 0), stop=(k == NKC - 1)),
                                 reads=[rtu, rh], writes=[rpu], inc=(k == NKC - 1))
                        tmp, rtmp = self.tmp.next()
                        S.op("act", lambda: nc.scalar.activation(out=tmp[:], in_=pg[:], func=AF.Silu),
                             reads=[rpg], writes=[rtmp])
                        S.op("dve", lambda: nc.vector.tensor_tensor(out=g[:, j, :], in0=tmp[:], in1=pu[:],
                                                                    op=ALU.mult),
                             reads=[rtmp, rpu], writes=[rg[j]])
                NDB = D // 256
                NQ = NDB * 2
                for db in range(NDB):
                    pds = [self.pd.next() for _ in range(2)]
                    for jh in range(2):
                        q = db * 2 + jh
                        td, rtd = dq.pop(0)
                        if q + 2 < NQ:
                            dq.append(load_d(q + 2))
                        for dd in range(2):
                            pd, rpd = pds[dd]
                            for j2 in range(22):
                                j = jh * 22 + j2
                                S.op("pe", lambda: nc.tensor.matmul(pd[:], td[:, j2, dd * 128:(dd + 1) * 128],
                                                                    g[:, j, :], start=(j == 0), stop=(j == NFC - 1)),
                                     reads=[rtd, rg[j]], writes=[rpd], inc=(j2 == 21))
                    for dd in range(2):
                        i = 2 * db + dd
                        pd, rpd = pds[dd]
                        S.op("dve", lambda: nc.vector.scalar_tensor_tensor(
                            out=x[:, i, :], in0=pd[:], scalar=0.5, in1=x[:, i, :], op0=ALU.mult, op1=ALU.add),
                            reads=[rpd, rx[i]], writes=[rx[i]])

            def load_x(self, it):
                S.dma("sp", self.x[:], xbuf[:, it * TT:(it + 1) * TT].rearrange("(c p) t -> p c t", p=128),
                      writes=self.rx)

            def store_x(self, it):
                S.dma("sp", xbuf[:, it * TT:(it + 1) * TT].rearrange("(c p) t -> p c t", p=128), self.x[:],
                      reads=self.rx)

        def phase1(l):
            with contextlib.ExitStack() as ph:
                tl = TL(ph)
                sbt, pst = tl.sbt, tl.pst
                x, rx, h, rh = tl.x, tl.rx, tl.h, tl.rh
                load_pc(tl.gains[:, 0, :], W["ffn1_norm"][l], tl.rgains)
                load_pc(tl.gains[:, 1, :], W["mix_norm"][l], tl.rgains)
                qkg = sbt("qkg", [128, 2], F32)
                rqkg = Res()
                for half in range(2):
                    S.dma("sp", qkg[half * 64:(half + 1) * 64, 0:1],
                          W["q_norm"][l].rearrange("(p o) -> p o", o=1), writes=[rqkg])
                    S.dma("sp", qkg[half * 64:(half + 1) * 64, 1:2],
                          W["k_norm"][l].rearrange("(p o) -> p o", o=1), writes=[rqkg])
                S.op("dve", lambda: nc.vector.tensor_single_scalar(qkg[:, 0:1], qkg[:, 0:1], 0.125, ALU.mult),
                     reads=[rqkg], writes=[rqkg])
                stg_b = Rot([(sbt("stgb%d" % i, [128, TT], BF16), Res()) for i in range(2)])
                stg_f = Rot([(sbt("stgf%d" % i, [128, TT], F32), Res()) for i in range(2)])
                stk_b = Rot([(sbt("stkb%d" % i, [128, 2, 256], BF16), Res()) for i in range(2)])
                stk_f = Rot([(sbt("stkf%d" % i, [128, 2, 256], F32), Res()) for i in range(2)])
                if l == 0:
                    xin_rot = Rot([(sbt("xin%d" % i, [128, D], F32), Res()) for i in range(2)])
                winv = W["w_in"][l].rearrange("(k p) f -> p k f", p=128)
                for it in range(NT):
                    t0 = it * TT
                    if l == 0:
                        for tb in range(4):
                            xin, rxin = xin_rot.next()
                            S.dma("sp", xin[:], x_in[t0 + tb * 128:t0 + (tb + 1) * 128, :], writes=[rxin])
                            for cq in range(4):
                                pb, rpb = tl.pb_rot.next()
                                for i4 in range(4):
                                    c = cq * 4 + i4
                                    S.op("pe", lambda: nc.tensor.transpose(pb[:, i4 * 128:(i4 + 1) * 128],
                                                                           xin[:, c * 128:(c + 1) * 128], ident[:]),
                                         reads=[rxin, rconst], writes=[rpb], inc=(i4 == 3))
                                dsts = x[:, cq * 4:(cq + 1) * 4, tb * 128:(tb + 1) * 128]
                                srcs = pb[:].rearrange("p (c t) -> p c t", c=4)
                                wr = [rx[cq * 4 + i4] for i4 in range(4)]
                                if cq % 2 == 0:
                                    S.op("act", lambda: nc.scalar.copy(dsts, srcs), reads=[rpb], writes=wr)
                                else:
                                    S.op("dve", lambda: nc.vector.tensor_copy(dsts, srcs), reads=[rpb], writes=wr)
                    else:
                        tl.load_x(it)
                    if STOP == "x":
                        tl.store_x(it)
                        continue
                    tl.rmsnorm(0)
                    if STOP == "n":
                        tl.store_x(it)
                        continue
                    tl.ffn(W["ffn1_w_gate"][l], W["ffn1_w_up"][l], W["ffn1_w_down"][l])
                    tl.store_x(it)
                    if STOP == "f":
                        continue
                    tl.rmsnorm(1)
                    NB = INC // 256 if NBLIM is None else NBLIM
                    nxt = tl.wgu_rot.next()
                    S.dma("pool", nxt[0][:], winv[:, :, 0:256], writes=[nxt[1]])
                    for b in range(NB):
                        wb, rwb = nxt
                        if b + 1 < NB:
                            nxt = tl.wgu_rot.next()
                            S.dma("pool", nxt[0][:], winv[:, :, (b + 1) * 256:(b + 2) * 256], writes=[nxt[1]])
                        tokmajor = b in (4, 5, 12, 13, 16, 17, 18, 19)
                        if not tokmajor:
                            for jj in range(2):
                                ps, rps = tl.pb_rot.next()
                                for k in range(NKC):
                                    S.op("pe", lambda: nc.tensor.matmul(ps[:], wb[:, k, jj * 128:(jj + 1) * 128],
                                                                        h[:, k, :], start=(k == 0),
                                                                        stop=(k == NKC - 1)),
                                         reads=[rwb, rh], writes=[rps], inc=(k == NKC - 1))
                                if b < 4:
                                    c4 = 2 * (b % 2) + jj
                                    gi = 0 if b < 2 else 1
                                    dst = qT_d if b < 2 else kT_d
                                    sq, rsq = tl.sq.next()
                                    S.op("act", lambda: nc.scalar.activation(out=sq[:], in_=ps[:], func=AF.Square),
                                         reads=[rps], writes=[rsq])
                                    S.op("pe", lambda: nc.tensor.matmul(tl.pstat[:], bones_b[:], sq[:], start=True,
                                                                        stop=True),
                                         reads=[rsq, rconst], writes=[tl.rpstat])
                                    S.op("act", lambda: nc.scalar.activation(out=tl.rstd[:], in_=tl.pstat[:],
                                                                             func=AF.Ln, bias=epsb[:],
                                                                             scale=1.0 / 64),
                                         reads=[tl.rpstat, rconst], writes=[tl.rrstd])
                                    S.op("act", lambda: nc.scalar.activation(out=tl.rstd[:], in_=tl.rstd[:],
                                                                             func=AF.Exp, scale=-0.5),
                                         reads=[tl.rrstd], writes=[tl.rrstd])
                                    st, rst = stg_b.next()
                                    S.op("dve", lambda: nc.vector.scalar_tensor_tensor(
                                        out=st[:], in0=ps[:], scalar=qkg[:, gi:gi + 1], in1=tl.rstd[:],
                                        op0=ALU.mult, op1=ALU.mult),
                                        reads=[rps, tl.rrstd, rqkg], writes=[rst])
                                    S.dma("sp", dst[c4 * 128:(c4 + 1) * 128, t0:t0 + TT], st[:], reads=[rst])
                                elif b in (6, 7):
                                    c4 = 2 * (b - 6) + jj
                                    st, rst = stg_b.next()
                                    S.op("act", lambda: nc.scalar.activation(out=st[:], in_=ps[:], func=AF.Silu),
                                         reads=[rps], writes=[rst])
                                    S.dma("sp", hqT_d[c4 * 128:(c4 + 1) * 128, t0:t0 + TT], st[:], reads=[rst])
                                elif b in (8, 9, 10, 11):
                                    dr = 0 if b < 10 else 1
                                    c4 = 2 * ((b - 8) % 2) + jj
                                    st, rst = stg_f.next()
                                    S.op("act", lambda: nc.scalar.copy(st[:], ps[:]), reads=[rps], writes=[rst])
                                    S.dma("sp", zf_d[dr, c4 * 128:(c4 + 1) * 128, t0:t0 + TT], st[:], reads=[rst])
                                else:
                                    c4 = 2 * (b - 14) + jj
                                    st, rst = stg_f.next()
                                    S.op("act", lambda: nc.scalar.activation(out=st[:], in_=ps[:], func=AF.Silu),
                                         reads=[rps], writes=[rst])
                                    S.dma("sp", hgT_d[c4 * 128:(c4 + 1) * 128, t0:t0 + TT], st[:], reads=[rst])
                        else:
                            if b in (4, 5):
                                dst, c0, isf = v_d, (b - 4) * 256, False
                            elif b in (12, 13):
                                dst, c0, isf = vh_d, (b - 12) * 256, False
                            else:
                                dst, c0, isf = su_d, (b - 16) * 256, True
                            for sp2 in range(2):
                                ps, rps = tl.pb_rot.next()
                                for s2 in range(2):
                                    s = sp2 * 2 + s2
                                    for k in range(NKC):
                                        S.op("pe", lambda: nc.tensor.matmul(
                                            ps[:, s2 * 256:(s2 + 1) * 256], h[:, k, s * 128:(s + 1) * 128],
                                            wb[:, k, :], start=(k == 0), stop=(k == NKC - 1)),
                                            reads=[rwb, rh], writes=[rps], inc=(s2 == 1 and k == NKC - 1))
                                st, rst = (stk_f if isf else stk_b).next()
                                S.op("act", lambda: nc.scalar.copy(st[:], ps[:].rearrange("p (s c) -> p s c", s=2)),
                                     reads=[rps], writes=[rst])
                                S.dma("sp", dst[t0 + sp2 * 256:t0 + (sp2 + 1) * 256, c0:c0 + 256].rearrange(
                                    "(s p) c -> p s c", p=128), st[:], reads=[rst])
                S.barrier()

        def phase2(l):
            with contextlib.ExitStack() as ph:
                sbt = lambda n, s_, d: ph.enter_context(nc.sbuf_tensor(uname(n), s_, d))
                pst = lambda n, s_, d: ph.enter_context(nc.psum_tensor(uname(n), s_, d))
                NL = L
                lg = sbt("lg", [128, NL, 8], F32)
                rlg = Res()
                S.dma("sp", lg[:].rearrange("p l (d h) -> p l d h", d=2),
                      W["hg_lb_logits"].rearrange("l d (h p) -> p l d h", p=128), writes=[rlg],
                      allow_slow_non_contiguous=True)
                mx = sbt("mx", [128, 8], F32)
                lbt = sbt("lbt", [128, 8], F32)
                omlt = sbt("omlt", [128, 8], F32)
                ssum = sbt("ssum", [128, 8], F32)
                rlb = Res()
                S.op("dve", lambda: nc.vector.tensor_copy(mx[:], lg[:, 0, :]), reads=[rlg], writes=[rlb])
                for ll in range(1, NL):
                    S.op("dve", lambda: nc.vector.tensor_tensor(out=mx[:], in0=mx[:], in1=lg[:, ll, :], op=ALU.max),
                         reads=[rlg, rlb], writes=[rlb])
                for ll in range(NL):
                    S.op("dve", lambda: nc.vector.tensor_tensor(out=lg[:, ll, :], in0=lg[:, ll, :], in1=mx[:],
                                                                op=ALU.subtract), reads=[rlg, rlb], writes=[rlg])
                S.op("act", lambda: nc.scalar.activation(out=lg[:], in_=lg[:], func=AF.Exp), reads=[rlg],
                     writes=[rlg])
                S.op("dve", lambda: nc.vector.tensor_copy(ssum[:], lg[:, 0, :]), reads=[rlg], writes=[rlb])
                S.op("dve", lambda: nc.vector.memset(lbt[:], 0.0), writes=[rlb])
                for ll in range(1, NL):
                    S.op("dve", lambda: nc.vector.tensor_tensor(out=ssum[:], in0=ssum[:], in1=lg[:, ll, :],
                                                                op=ALU.add), reads=[rlg, rlb], writes=[rlb])
                    if ll <= l:
                        S.op("dve", lambda: nc.vector.tensor_tensor(out=lbt[:], in0=lbt[:], in1=lg[:, ll, :],
                                                                    op=ALU.add), reads=[rlg, rlb], writes=[rlb])
                S.op("dve", lambda: nc.vector.reciprocal(ssum[:], ssum[:]), reads=[rlb], writes=[rlb])
                S.op("dve", lambda: nc.vector.tensor_tensor(out=lbt[:], in0=lbt[:], in1=ssum[:], op=ALU.mult),
                     reads=[rlb], writes=[rlb])
                S.op("dve", lambda: nc.vector.tensor_scalar(omlt[:], lbt[:], -1.0, 1.0, ALU.mult, ALU.add),
                     reads=[rlb], writes=[rlb])
                hgain = sbt("hgain", [128, 4], F32)
                load_pc(hgain[:], W["hg_out_norm"][l], rlb)
                cf = sbt("cf", [128, 128], F32)
                rcf = Res()
                hmask = sbt("hmask", [128, 2, 128], F32)
                rhm = Res()
                for d_ in range(2):
                    S.dma("sp", hmask[:, d_, :], C["c_hgmask"][d_], writes=[rhm])
                cmask = sbt("cmask", [128, 8], F32)
                S.dma("sp", cmask[:], C["c_chunkmask"], writes=[rhm])
                rmask = sbt("rmask", [128, TT], F32)
                S.dma("sp", rmask[:], C["c_resetmask"], writes=[rhm])
                ldq = Rot([(sbt("ldq%d" % i, [128, TT], BF16), Res()) for i in range(2)])
                ldz = Rot([(sbt("ldz%d" % i, [128, TT], F32), Res()) for i in range(2)])
                ldv = Rot([(sbt("ldv%d" % i, [128, 4, 128], BF16), Res()) for i in range(2)])
                ldo = Rot([(sbt("ldo%d" % i, [128, TT], F32), Res()) for i in range(2)])
                ldg = Rot([(sbt("ldg%d" % i, [128, TT], F32), Res()) for i in range(2)])
                A = sbt("A", [128, TT], F32); rA = Res()
                B = sbt("B", [128, TT], F32); rB = Res()
                B2 = sbt("B2", [128, TT], F32); rB2 = Res()
                KK = sbt("KK", [128, TT], F32); rKK = Res()
                LF = sbt("LF", [128, TT], F32); rLF = Res()
                CUM = sbt("CUM", [128, TT], F32); rCUM = Res()
                CU2 = sbt("CU2", [128, TT], F32); rCU2 = Res()
                Qt = sbt("Qt", [128, TT], BF16); rQt = Res()
                Kt = sbt("Kt", [128, TT], BF16); rKt = Res()
                Kh = sbt("Kh", [128, TT], BF16); rKh = Res()
                dec = sbt("dec", [128, 32], F32); rdec = Res()
                AT = sbt("AT", [128, 128], BF16); rAT = Res()
                KhT = sbt("KhT", [128, 128], BF16); rKhT = Res()
                Vb = sbt("Vb", [128, 8, 128], BF16); rVb = Res()
                Sbf = sbt("Sbf", [128, 8, 128], BF16); rSbf = Res()
                osum = sbt("osum", [128, TT], F32); rosum = Res()
                sqb = sbt("sqb", [128, TT], BF16); rsqb = Res()
                rstd = sbt("rstd2", [128, TT], F32); rrstd = Res()
                bst = Rot([(sbt("bst%d" % i, [128, TT], BF16), Res()) for i in range(2)])
                hist = [[(sbt("hist%d_%d" % (h_, i), [128, 9, 128], F32), Res()) for i in range(2)]
                        for h_ in range(4)]
                ps_s = pst("ps_s", [128, TT], F32); rps_s = Res()
                ps_t = pst("ps_t", [128, 1024], BF16); rps_t = Res()
                ps_kv = [(pst("ps_kv%d" % i, [128, TT], F32), Res()) for i in range(2)]
                ps_o = pst("ps_o", [128, TT], F32); rps_o = Res()
                pstat = pst("pstat2", [128, TT], F32); rpstat = Res()
                v3 = lambda t_: t_[:].rearrange("p (n c) -> p n c", c=16)

                for dr in range(2):
                    cur = [0, 0, 0, 0]
                    for h_ in range(4):
                        S.op("dve", lambda: nc.vector.memset(hist[h_][1][0][:, 8, :], 0.0), writes=[hist[h_][1][1]])
                    tiles = list(range(NT)) if dr == 0 else list(range(NT - 1, -1, -1))
                    subs = list(range(4)) if dr == 0 else [3, 2, 1, 0]
                    for it in tiles:
                        t0 = it * TT
                        for h_ in range(4):
                            fs = slice(h_ * 128, (h_ + 1) * 128)
                            q, rq = ldq.next()
                            z, rz = ldz.next()
                            vt, rvt = ldv.next()
                            S.dma("sp", q[:], hqT_d[fs, t0:t0 + TT], writes=[rq])
                            S.dma("sp", z[:], zf_d[dr, fs, t0:t0 + TT], writes=[rz])
                            S.dma("sp", vt[:], vh_d[t0:t0 + TT, fs].rearrange("(s p) c -> p s c", p=128),
                                  writes=[rvt])
                            if dr == 1:
                                of, rof = ldo.next()
                                hg, rhg = ldg.next()
                                S.dma("sp", of[:], ohf_d[fs, t0:t0 + TT], writes=[rof])
                                S.dma("sp", hg[:], hgT_d[fs, t0:t0 + TT], writes=[rhg])
                            lbs = lbt[:, dr * 4 + h_:dr * 4 + h_ + 1]
                            oms = omlt[:, dr * 4 + h_:dr * 4 + h_ + 1]
                            S.op("act", lambda: nc.scalar.activation(out=A[:], in_=z[:], func=AF.Exp, scale=-1.0),
                                 reads=[rz], writes=[rA])
                            S.op("dve", lambda: nc.vector.tensor_single_scalar(B[:], A[:], 1.0, ALU.add),
                                 reads=[rA], writes=[rB])
                            S.op("dve", lambda: nc.vector.reciprocal(B[:], B[:]), reads=[rB], writes=[rB])
                            S.op("dve", lambda: nc.vector.scalar_tensor_tensor(
                                out=KK[:], in0=A[:], scalar=oms, in1=B[:], op0=ALU.mult, op1=ALU.mult),
                                reads=[rA, rB, rlb], writes=[rKK])
                            S.op("act", lambda: nc.scalar.activation(out=LF[:], in_=KK[:], func=AF.Ln, scale=-1.0,
                                                                     bias=onest[:]),
                                 reads=[rKK, rconst], writes=[rLF])
                            S.op("dve", lambda: nc.vector.tensor_tensor_scan(CUM[:], rmask[:], LF[:], 0.0, ALU.mult,
                                                                             ALU.add),
                                 reads=[rhm, rLF], writes=[rCUM])
                            totb = v3(CUM)[:, :, 15:16].to_broadcast([128, 32, 16])
                            if dr == 0:
                                cum, rcum = CUM, rCUM
                            else:
                                S.op("dve", lambda: nc.vector.tensor_tensor(out=v3(CU2), in0=totb, in1=v3(CUM),
                                                                            op=ALU.subtract),
                                     reads=[rCUM], writes=[rCU2])
                                S.op("dve", lambda: nc.vector.tensor_tensor(out=CU2[:], in0=CU2[:], in1=LF[:],
                                                                            op=ALU.add),
                                     reads=[rCU2, rLF], writes=[rCU2])
                                cum, rcum = CU2, rCU2
                            S.op("act", lambda: nc.scalar.activation(out=A[:], in_=cum[:], func=AF.Exp),
                                 reads=[rcum], writes=[rA])
                            S.op("dve", lambda: nc.vector.tensor_tensor(out=Qt[:], in0=q[:], in1=A[:], op=ALU.mult),
                                 reads=[rq, rA], writes=[rQt])
                            S.op("act", lambda: nc.scalar.activation(out=B[:], in_=cum[:], func=AF.Exp, scale=-1.0),
                                 reads=[rcum], writes=[rB])
                            S.op("dve", lambda: nc.vector.tensor_tensor(out=Kt[:], in0=KK[:], in1=B[:], op=ALU.mult),
                                 reads=[rKK, rB], writes=[rKt])
                            S.op("dve", lambda: nc.vector.tensor_tensor(out=v3(B2), in0=totb, in1=v3(cum),
                                                                        op=ALU.subtract),
                                 reads=[rCUM, rcum], writes=[rB2])
                            S.op("act", lambda: nc.scalar.activation(out=B2[:], in_=B2[:], func=AF.Exp),
                                 reads=[rB2], writes=[rB2])
                            S.op("dve", lambda: nc.vector.tensor_tensor(out=Kh[:], in0=KK[:], in1=B2[:],
                                                                        op=ALU.mult),
                                 reads=[rKK, rB2], writes=[rKh])
                            S.op("act", lambda: nc.scalar.activation(out=dec[:], in_=v3(CUM)[:, :, 15], func=AF.Exp),
                                 reads=[rCUM], writes=[rdec])
                            for sub in subs:
                                cs = slice(sub * 128, (sub + 1) * 128)
                                S.op("pe", lambda: nc.tensor.matmul(ps_s[:, 0:128], Kt[:, cs], Qt[:, cs], start=True,
                                                                    stop=True),
                                     reads=[rKt, rQt], writes=[rps_s])
                                S.op("dve", lambda: nc.vector.tensor_tensor(out=AT[:], in0=ps_s[:, 0:128],
                                                                            in1=hmask[:, dr, :], op=ALU.mult),
                                     reads=[rps_s, rhm], writes=[rAT])
                                S.op("pe", lambda: nc.tensor.transpose(ps_t[:, 0:128], Kh[:, cs], ident_b[:]),
                                     reads=[rKh, rconst], writes=[rps_t])
                                S.op("act", lambda: nc.scalar.copy(KhT[:], ps_t[:, 0:128]), reads=[rps_t],
                                     writes=[rKhT])
                                S.op("dve", lambda: nc.vector.tensor_tensor(
                                    out=Vb[:], in0=vt[:, sub, :].unsqueeze(1).to_broadcast([128, 8, 128]),
                                    in1=cmask[:].unsqueeze(2).to_broadcast([128, 8, 128]), op=ALU.mult),
                                    reads=[rvt, rhm], writes=[rVb])
                                for hf in range(2):
                                    S.op("pe", lambda: nc.tensor.matmul(
                                        ps_kv[hf][0][:], KhT[:],
                                        Vb[:, hf * 4:(hf + 1) * 4, :].rearrange("p a b -> p (a b)"),
                                        start=True, stop=True),
                                        reads=[rKhT, rVb], writes=[ps_kv[hf][1]])
                                hc, rhc = hist[h_][cur[h_]]
                                hp, rhp = hist[h_][1 - cur[h_]]
                                cur[h_] = 1 - cur[h_]
                                tok_lo = t0 + sub * 128
                                boundary = (dr == 0 and tok_lo == HALF_T) or (dr == 1 and tok_lo + 128 == HALF_T)
                                lk = linkt if boundary else onest
                                S.op("dve", lambda: nc.vector.tensor_scalar(hc[:, 0, :], hp[:, 8, :], lk[:, 0:1], None,
                                                                            ALU.mult),
                                     reads=[rhp, rconst], writes=[rhc])
                                for i8 in range(8):
                                    n = i8 if dr == 0 else 7 - i8
                                    kvb, rkvb = ps_kv[n // 4]
                                    S.op("dve", lambda: nc.vector.scalar_tensor_tensor(
                                        out=hc[:, i8 + 1, :], in0=hc[:, i8, :],
                                        scalar=dec[:, sub * 8 + n:sub * 8 + n + 1],
                                        in1=kvb[:, (n % 4) * 128:(n % 4 + 1) * 128], op0=ALU.mult, op1=ALU.add),
                                        reads=[rhc, rdec, rkvb], writes=[rhc])
                                S.op("act", lambda: nc.scalar.copy(Sbf[:], hc[:, 0:8, :]), reads=[rhc],
                                     writes=[rSbf])
                                S.op("pe", lambda: nc.tensor.matmul(ps_o[:, cs], vt[:, sub, :], AT[:], start=True,
                                                                    stop=False),
                                     reads=[rvt, rAT], writes=[rps_o], inc=False)
                                for i8 in range(8):
                                    n = i8 if dr == 0 else 7 - i8
                                    c0 = sub * 128 + n * 16
                                    S.op("pe", lambda: nc.tensor.matmul(ps_o[:, c0:c0 + 16], Sbf[:, i8, :],
                                                                        Qt[:, c0:c0 + 16], start=False,
                                                                        stop=(i8 == 7)),
                                         reads=[rSbf, rQt], writes=[rps_o], inc=(i8 == 7))
                            if dr == 0:
                                S.op("act", lambda: nc.scalar.copy(osum[:], ps_o[:]), reads=[rps_o], writes=[rosum])
                                S.dma("sp", ohf_d[fs, t0:t0 + TT], osum[:], reads=[rosum])
                            else:
                                S.op("dve", lambda: nc.vector.tensor_tensor(out=osum[:], in0=ps_o[:], in1=of[:],
                                                                            op=ALU.add),
                                     reads=[rps_o, rof], writes=[rosum])
                                S.op("act", lambda: nc.scalar.activation(out=sqb[:], in_=osum[:], func=AF.Square),
                                     reads=[rosum], writes=[rsqb])
                                S.op("pe", lambda: nc.tensor.matmul(pstat[:], ones_b[:], sqb[:], start=True,
                                                                    stop=True),
                                     reads=[rsqb, rconst], writes=[rpstat])
                                S.op("act", lambda: nc.scalar.activation(out=rstd[:], in_=pstat[:], func=AF.Ln,
                                                                         bias=epsb[:], scale=1.0 / 128),
                                     reads=[rpstat, rconst], writes=[rrstd])
                                S.op("act", lambda: nc.scalar.activation(out=rstd[:], in_=rstd[:], func=AF.Exp,
                                                                         scale=-0.5),
                                     reads=[rrstd], writes=[rrstd])
                                S.op("dve", lambda: nc.vector.scalar_tensor_tensor(
                                    out=hg[:], in0=hg[:], scalar=hgain[:, h_:h_ + 1], in1=rstd[:], op0=ALU.mult,
                                    op1=ALU.mult), reads=[rhg, rlb, rrstd], writes=[rhg])
                                bo, rbo = bst.next()
                                S.op("dve", lambda: nc.vector.tensor_tensor(out=bo[:], in0=osum[:], in1=hg[:],
                                                                            op=ALU.mult),
                                     reads=[rosum, rhg], writes=[rbo])
                                S.dma("sp", mixT_d[512 + h_ * 128:512 + (h_ + 1) * 128, t0:t0 + TT], bo[:],
                                      reads=[rbo])
                S.barrier()

        def phase3a(l):
            GH = 32
            NS = T // 256
            TWO_PI = 2.0 * PI
            with contextlib.ExitStack() as ph:
                sbt = lambda n, s_, d: ph.enter_context(nc.sbuf_tensor(uname(n), s_, d))
                pst = lambda n, s_, d: ph.enter_context(nc.psum_tensor(uname(n), s_, d))
                Tmat = sbt("Tmat", [128, 2, GH, 128], BF16); rTmat = Res()
                EmatT = sbt("EmatT", [128, 2, GH, 128], BF16); rEm = Res()
                Fmat = sbt("Fmat", [64, 2, 2, GH, 128], BF16); rFm = Res()
                A1 = sbt("A1", [64, 2, 2, GH], F32)
                A2 = sbt("A2", [64, 2, 2, GH], F32)
                rA12 = Res()
                link64 = linkt[0:64, 0:1]
                smask = sbt("smask", [128, 2, 128], F32); rsm = Res()
                for d_ in range(2):
                    S.dma("sp", smask[:, d_, :], C["c_s5mask"][d_], writes=[rsm])
                for gh in range(64 // GH):
                    g0 = gh * GH
                    with contextlib.ExitStack() as pp:
                        sbp = lambda n, s_, d: pp.enter_context(nc.sbuf_tensor(uname(n), s_, d))
                        psp = lambda n, s_, d: pp.enter_context(nc.psum_tensor(uname(n), s_, d))
                        Are = sbp("Are", [64, 2, GH], F32); Aim = sbp("Aim", [64, 2, GH], F32)
                        Ldt = sbp("Ldt", [64, 2, GH], F32); mt = sbp("mt", [64, 2, 26], F32)
                        rP = Res()
                        for d_ in range(2):
                            S.dma("sp", Are[:, d_, :], W["s5_a_re"][l, d_, g0:g0 + GH, :].rearrange("g p -> p g"),
                                  writes=[rP], allow_slow_non_contiguous=True)
                            S.dma("sp", Aim[:, d_, :], W["s5_a_im"][l, d_, g0:g0 + GH, :].rearrange("g p -> p g"),
                                  writes=[rP], allow_slow_non_contiguous=True)
                            S.dma("sp", Ldt[:, d_, :], W["s5_log_dt"][l, d_:d_ + 1, g0:g0 + GH].partition_broadcast(64),
                                  writes=[rP])
                        S.dma("sp", mt[:].rearrange("p d m -> p (d m)"),
                              C["c_s5exp"].rearrange("(o d) m -> o (d m)", o=1).partition_broadcast(64), writes=[rP])
                        dta = sbp("dta", [64, 2, GH], F32); ang = sbp("ang", [64, 2, GH], F32)
                        S.op("act", lambda: nc.scalar.activation(out=Ldt[:], in_=Ldt[:], func=AF.Exp), reads=[rP],
                             writes=[rP])
                        S.op("dve", lambda: nc.vector.tensor_tensor(out=dta[:], in0=Ldt[:], in1=Are[:], op=ALU.mult),
                             reads=[rP], writes=[rP])
                        S.op("dve", lambda: nc.vector.tensor_tensor(out=ang[:], in0=Ldt[:], in1=Aim[:], op=ALU.mult),
                             reads=[rP], writes=[rP])
                        SH = [64, 2, 26, GH]
                        mag = sbp("mag", SH, F32); am = sbp("am", SH, F32); tq = sbp("tq", SH, F32)
                        ti = sbp("ti", SH, I32); sn = sbp("sn", SH, F32)
                        PWre = sbp("PWre", SH, F32); PWim = sbp("PWim", SH, F32)
                        bg = lambda a_: a_[:].unsqueeze(2).to_broadcast(SH)
                        bm = lambda a_: a_[:].unsqueeze(3).to_broadcast(SH)
                        S.op("dve", lambda: nc.vector.tensor_tensor(out=mag[:], in0=bg(dta), in1=bm(mt), op=ALU.mult),
                             reads=[rP], writes=[rP])
                        S.op("act", lambda: nc.scalar.activation(out=mag[:], in_=mag[:], func=AF.Exp), reads=[rP],
                             writes=[rP])
                        S.op("dve", lambda: nc.vector.tensor_tensor(out=am[:], in0=bg(ang), in1=bm(mt), op=ALU.mult),
                             reads=[rP], writes=[rP])

                        def sin_of(dst, shift):
                            S.op("dve", lambda: nc.vector.tensor_scalar(tq[:], am[:], shift, 1.0 / TWO_PI, ALU.add,
                                                                        ALU.mult), reads=[rP], writes=[rP])
                            S.op("dve", lambda: nc.vector.tensor_copy(ti[:], tq[:]), reads=[rP], writes=[rP])
                            S.op("dve", lambda: nc.vector.tensor_copy(tq[:], ti[:]), reads=[rP], writes=[rP])
                            S.op("dve", lambda: nc.vector.scalar_tensor_tensor(
                                out=sn[:], in0=tq[:], scalar=-TWO_PI, in1=am[:], op0=ALU.mult, op1=ALU.add),
                                reads=[rP], writes=[rP])
                            if shift != 0.0:
                                S.op("dve", lambda: nc.vector.tensor_single_scalar(sn[:], sn[:], shift, ALU.add),
                                     reads=[rP], writes=[rP])
                            S.op("dve", lambda: nc.vector.tensor_single_scalar(tq[:], sn[:], PI, ALU.is_gt),
                                 reads=[rP], writes=[rP])
                            S.op("dve", lambda: nc.vector.scalar_tensor_tensor(
                                out=sn[:], in0=tq[:], scalar=-TWO_PI, in1=sn[:], op0=ALU.mult, op1=ALU.add),
                                reads=[rP], writes=[rP])
                            S.op("dve", lambda: nc.vector.tensor_single_scalar(tq[:], sn[:], -PI, ALU.is_lt),
                                 reads=[rP], writes=[rP])
                            S.op("dve", lambda: nc.vector.scalar_tensor_tensor(
                                out=sn[:], in0=tq[:], scalar=TWO_PI, in1=sn[:], op0=ALU.mult, op1=ALU.add),
                                reads=[rP], writes=[rP])
                            S.op("act", lambda: nc.scalar.activation(out=sn[:], in_=sn[:], func=AF.Sin), reads=[rP],
                                 writes=[rP])
                            S.op("dve", lambda: nc.vector.tensor_tensor(out=dst[:], in0=mag[:], in1=sn[:],
                                                                        op=ALU.mult), reads=[rP], writes=[rP])

                        sin_of(PWim, 0.0)
                        sin_of(PWre, PI / 2.0)
                        for r_ in range(2):
                            S.op("dve", lambda: nc.vector.tensor_copy(A1[:, :, r_, :], PWre[:, :, 24, :]),
                                 reads=[rP], writes=[rA12])
                        S.op("dve", lambda: nc.vector.tensor_single_scalar(A2[:, :, 0, :], PWim[:, :, 24, :], -1.0,
                                                                           ALU.mult), reads=[rP], writes=[rA12])
                        S.op("dve", lambda: nc.vector.tensor_copy(A2[:, :, 1, :], PWim[:, :, 24, :]), reads=[rP],
                             writes=[rA12])
                        SG = [64, 2, GH]
                        nr = sbp("nr", SG, F32); den = sbp("den", SG, F32); t1 = sbp("t1", SG, F32)
                        fre = sbp("fre", SG, F32); fim = sbp("fim", SG, F32)
                        P1r = PWre[:, :, 25, :]; P1i = PWim[:, :, 25, :]
                        tt = lambda o, a_, b_, op_: S.op("dve", lambda: nc.vector.tensor_tensor(out=o, in0=a_, in1=b_,
                                                                                                op=op_),
                                                         reads=[rP], writes=[rP])
                        S.op("dve", lambda: nc.vector.tensor_single_scalar(nr[:], P1r, -1.0, ALU.add), reads=[rP],
                             writes=[rP])
                        tt(den[:], Are[:], Are[:], ALU.mult)
                        tt(t1[:], Aim[:], Aim[:], ALU.mult)
                        tt(den[:], den[:], t1[:], ALU.add)
                        S.op("dve", lambda: nc.vector.reciprocal(den[:], den[:]), reads=[rP], writes=[rP])
                        tt(fre[:], nr[:], Are[:], ALU.mult)
                        tt(t1[:], P1i, Aim[:], ALU.mult)
                        tt(fre[:], fre[:], t1[:], ALU.add)
                        tt(fre[:], fre[:], den[:], ALU.mult)
                        tt(fim[:], P1i, Are[:], ALU.mult)
                        tt(t1[:], nr[:], Aim[:], ALU.mult)
                        tt(fim[:], fim[:], t1[:], ALU.subtract)
                        tt(fim[:], fim[:], den[:], ALU.mult)
                        SB = [64, 2, GH, 16]
                        Bre = sbp("Bre", SB, F32); Bim = sbp("Bim", SB, F32)
                        Bbr = sbp("Bbr", SB, F32); Bbi = sbp("Bbi", SB, F32); tb_ = sbp("tb_", SB, F32)
                        for d_ in range(2):
                            S.dma("sp", Bre[:, d_, :, :], W["s5_b_re"][l, d_, g0:g0 + GH].rearrange("g p c -> p g c"),
                                  writes=[rP])
                            S.dma("sp", Bim[:, d_, :, :], W["s5_b_im"][l, d_, g0:g0 + GH].rearrange("g p c -> p g c"),
                                  writes=[rP])
                        bc = lambda a_: a_[:].unsqueeze(3).to_broadcast(SB)
                        tt(Bbr[:], bc(fre), Bre[:], ALU.mult)
                        tt(tb_[:], bc(fim), Bim[:], ALU.mult)
                        tt(Bbr[:], Bbr[:], tb_[:], ALU.subtract)
                        tt(Bbi[:], bc(fre), Bim[:], ALU.mult)
                        tt(tb_[:], bc(fim), Bre[:], ALU.mult)
                        tt(Bbi[:], Bbi[:], tb_[:], ALU.add)
                        Cre = sbp("Cre", SB, F32); Cim = sbp("Cim", SB, F32)
                        cin = Rot([(sbp("cin%d" % i_, [128, 64], F32), Res()) for i_ in range(2)])
                        psC = psp("psC", [64, 512], F32); rpsC = Res()
                        for (src, dstc) in (("s5_c_re", Cre), ("s5_c_im", Cim)):
                            for d_ in range(2):
                                for k4 in range(GH // 8):
                                    ci, rci = cin.next()
                                    S.dma("sp", ci[:], W[src][l, d_, g0 + k4 * 8:g0 + k4 * 8 + 8].rearrange(
                                        "g c p -> (g c) p"), writes=[rci])
                                    S.op("pe", lambda: nc.tensor.transpose(psC[:, 0:128], ci[:], ident[:]),
                                         reads=[rci, rconst], writes=[rpsC])
                                    S.op("act", lambda: nc.scalar.copy(
                                        dstc[:, d_, k4 * 8:(k4 + 1) * 8, :].rearrange("p g c -> p (g c)"),
                                        psC[:, 0:128]), reads=[rpsC], writes=[rP])
                        GB2 = 16
                        SE = [64, GB2, 8, 16]
                        Er = sbp("Er", SE, F32); Ei = sbp("Ei", SE, F32)
                        Gr = sbp("Gr", SE, F32); Gi = sbp("Gi", SE, F32)
                        tA = sbp("tA", SE, F32); tB = sbp("tB", SE, F32)
                        psT = psp("psT", [128, 512], F32); rpsT = Res()
                        psE = psp("psE", [128, 512], F32); rpsE = Res()
                        for d_ in range(2):
                          for gb2 in range(GH // GB2):
                            gsl = slice(gb2 * GB2, (gb2 + 1) * GB2)
                            pw = lambda P_, s0: P_[:, d_, s0:s0 + 8, gsl].rearrange("p s g -> p g s").unsqueeze(
                                3).to_broadcast(SE)
                            bb = lambda Q_: Q_[:, d_, gsl, :].unsqueeze(2).to_broadcast(SE)

                            def cmul(outr, outi_neg, s0, Xr, Xi, negate_im):
                                tt(tA[:], pw(PWre, s0), bb(Xr), ALU.mult)
                                tt(tB[:], pw(PWim, s0), bb(Xi), ALU.mult)
                                tt(outr, tA[:], tB[:], ALU.subtract)
                                tt(tA[:], pw(PWre, s0), bb(Xi), ALU.mult)
                                tt(tB[:], pw(PWim, s0), bb(Xr), ALU.mult)
                                if negate_im:
                                    S.op("dve", lambda: nc.vector.scalar_tensor_tensor(
                                        out=outi_neg, in0=tA[:], scalar=-1.0, in1=tB[:], op0=ALU.mult,
                                        op1=ALU.subtract), reads=[rP], writes=[rP])
                                else:
                                    tt(outi_neg, tA[:], tB[:], ALU.add)

                            cmul(Er[:], Ei[:], 0, Bbr, Bbi, False)
                            cmul(Gr[:], Gi[:], 16, Cre, Cim, True)
                            for gl in range(GB2):
                                g_ = gb2 * GB2 + gl
                                e_r = Er[:, gl, :, :].rearrange("p s c -> p (s c)")
                                e_i = Ei[:, gl, :, :].rearrange("p s c -> p (s c)")
                                g_r = Gr[:, gl, :, :].rearrange("p s c -> p (s c)")
                                g_i = Gi[:, gl, :, :].rearrange("p s c -> p (s c)")
                                S.op("pe", lambda: nc.tensor.matmul(psT[:, 0:128], e_r, g_r, start=True, stop=False),
                                     reads=[rP], writes=[rpsT], inc=False)
                                S.op("pe", lambda: nc.tensor.matmul(psT[:, 0:128], e_i, g_i, start=False, stop=True),
                                     reads=[rP], writes=[rpsT])
                                S.op("dve", lambda: nc.vector.tensor_tensor(out=Tmat[:, d_, g_, :], in0=psT[:, 0:128],
                                                                            in1=smask[:, d_, :], op=ALU.mult),
                                     reads=[rpsT, rsm], writes=[rTmat])
                                S.op("pe", lambda: nc.tensor.transpose(psE[:, 0:64], e_r, ident[0:64, 0:64]),
                                     reads=[rP, rconst], writes=[rpsE], inc=False)
                                S.op("pe", lambda: nc.tensor.transpose(psE[:, 64:128], e_i, ident[0:64, 0:64]),
                                     reads=[rP, rconst], writes=[rpsE])
                                S.op("act", lambda: nc.scalar.copy(EmatT[:, d_, g_, :], psE[:, 0:128]),
                                     reads=[rpsE], writes=[rEm])
                            tt(tA[:], pw(PWre, 8), bb(Cre), ALU.mult)
                            tt(tB[:], pw(PWim, 8), bb(Cim), ALU.mult)
                            S.op("dve", lambda: nc.vector.tensor_tensor(
                                out=Fmat[:, d_, 0, gsl, :].rearrange("p g (s c) -> p g s c", c=16), in0=tA[:],
                                in1=tB[:], op=ALU.subtract), reads=[rP], writes=[rFm])
                            tt(tA[:], pw(PWre, 8), bb(Cim), ALU.mult)
                            tt(tB[:], pw(PWim, 8), bb(Cre), ALU.mult)
                            S.op("dve", lambda: nc.vector.scalar_tensor_tensor(
                                out=Fmat[:, d_, 1, gsl, :].rearrange("p g (s c) -> p g s c", c=16), in0=tA[:],
                                scalar=-1.0, in1=tB[:], op0=ALU.mult, op1=ALU.subtract), reads=[rP], writes=[rFm])
                        S.barrier()
                    with contextlib.ExitStack() as mm:
                        sbm = lambda n, s_, d: mm.enter_context(nc.sbuf_tensor(uname(n), s_, d))
                        psm = lambda n, s_, d: mm.enter_context(nc.psum_tensor(uname(n), s_, d))
                        Xtok = Rot([(sbm("Xtok%d" % i_, [32, 8, 256], BF16), Res()) for i_ in range(3)])
                        Xperm = Rot([(sbm("Xperm%d" % i_, [32, 16, 128], BF16), Res()) for i_ in range(2)])
                        U2 = sbm("U2", [128, 2, 3, GH, 32], BF16)
                        rU2 = [[Res() for _ in range(3)] for _ in range(2)]
                        Zs = sbm("Zs", [64, 2, 2, 2, GH, 32], F32); rZs = [Res(), Res()]
                        Xh = sbm("Xh", [64, 2, 2, 2, GH, 32], BF16); rXh = [Res(), Res()]
                        Xs = sbm("Xs", [64, 2, 3, GH], F32); rXs = Res()
                        XH = sbm("XH", [64, 2, 3, GH, 33], F32); rXH = Res()
                        T1 = sbm("T1", [64, 2, 2, GH], F32); rT1 = Res()
                        T2 = sbm("T2", [64, 2, 2, GH], F32); rT2 = Res()
                        Ysb = Rot([(sbm("Ysb%d" % i_, [128, 256], F32), Res()) for i_ in range(2)])
                        Ytok = Rot([(sbm("Ytok%d" % i_, [32, 8, 256], F32), Res()) for i_ in range(2)])
                        psU = psm("psU", [128, 512], F32); rpsU = Res()
                        psZ = Rot([(psm("psZ%d" % i_, [64, 512], F32), Res()) for i_ in range(2)])
                        psY = psm("psY", [128, 512], F32); rpsY = Res()
                        psYT = psm("psYT", [32, 1024], F32); rpsYT = Res()
                        S.op("dve", lambda: nc.vector.memset(Xs[:], 0.0), writes=[rXs])

                        def step_ab(j):
                            jb = j % 2
                            tiles = (j, NS - 1 - j)
                            for ui in range(3):
                                d_ = 0 if ui == 0 else 1
                                tok0 = tiles[d_] * 256
                                perm = anti_b[0:32, 96:128] if ui == 1 else ident_b[0:32, 0:32]
                                for gb in range(GH // 16):
                                    xt, rxt = Xtok.next()
                                    c0 = (g0 + gb * 16) * 16
                                    S.dma("pool", xt[:], su_d[tok0:tok0 + 256, c0:c0 + 256].rearrange(
                                        "(n s) c -> n s c", s=8), writes=[rxt])
                                    xp, rxp = Xperm.next()
                                    S.op("act", lambda: nc.scalar.copy(
                                        xp[:].rearrange("n g (s c) -> n g s c", c=16),
                                        xt[:].rearrange("n s (g c) -> n g s c", c=16)), reads=[rxt], writes=[rxp])
                                    for g16 in range(16):
                                        S.op("pe", lambda: nc.tensor.matmul(
                                            psU[:, g16 * 32:(g16 + 1) * 32], xp[:, g16, :],
                                            perm, start=True, stop=True),
                                            reads=[rxp, rconst], writes=[rpsU], inc=(g16 == 15))
                                    S.op("act", lambda: nc.scalar.copy(
                                        U2[:, jb, ui, gb * 16:(gb + 1) * 16, :].rearrange("p g n -> p (g n)"),
                                        psU[:]), reads=[rpsU], writes=[rU2[jb][ui]])
                            for d_ in range(2):
                                for gq in range(GH // 8):
                                    pz, rpz = psZ.next()
                                    for g8 in range(8):
                                        g_ = gq * 8 + g8
                                        for ri in range(2):
                                            S.op("pe", lambda: nc.tensor.matmul(
                                                pz[:, (ri * 8 + g8) * 32:(ri * 8 + g8 + 1) * 32],
                                                EmatT[:, d_, g_, ri * 64:(ri + 1) * 64], U2[:, jb, d_, g_, :],
                                                start=True, stop=True),
                                                reads=[rEm, rU2[jb][d_]], writes=[rpz],
                                                inc=(g8 == 7 and ri == 1))
                                    S.op("act", lambda: nc.scalar.copy(
                                        Zs[:, jb, d_, :, gq * 8:(gq + 1) * 8, :],
                                        pz[:].rearrange("p (r g n) -> p r g n", r=2, g=8)),
                                        reads=[rpz], writes=[rZs[jb]])

                        def step_c(j):
                            jb = j % 2
                            if j * 256 == HALF_T:
                                S.op("dve", lambda: nc.vector.tensor_scalar(Xs[:], Xs[:], link64, None, ALU.mult),
                                     reads=[rXs, rconst], writes=[rXs])
                            S.op("dve", lambda: nc.vector.tensor_copy(XH[:, :, :, :, 0], Xs[:]), reads=[rXs, rXH],
                                 writes=[rXH])
                            for i32 in range(32):
                                xc = XH[:, :, :, :, i32]
                                xn = XH[:, :, :, :, i32 + 1]
                                S.op("dve", lambda: nc.vector.tensor_tensor(out=T1[:], in0=A1[:], in1=xc[:, :, 0:2, :],
                                                                            op=ALU.mult),
                                     reads=[rA12, rXH], writes=[rT1])
                                S.op("dve", lambda: nc.vector.tensor_tensor(out=T2[:], in0=A2[:], in1=xc[:, :, 1:3, :],
                                                                            op=ALU.mult),
                                     reads=[rA12, rXH], writes=[rT2])
                                S.op("dve", lambda: nc.vector.tensor_tensor(out=T1[:], in0=T1[:], in1=T2[:],
                                                                            op=ALU.add),
                                     reads=[rT1, rT2], writes=[rT1])
                                S.op("dve", lambda: nc.vector.tensor_tensor(out=xn[:, :, 0:2, :], in0=T1[:],
                                                                            in1=Zs[:, jb, :, :, :, i32], op=ALU.add),
                                     reads=[rT1, rZs[jb]], writes=[rXH])
                                S.op("dve", lambda: nc.vector.tensor_tensor(out=xn[:, :, 2, :], in0=T1[:, :, 0, :],
                                                                            in1=Zs[:, jb, :, 0, :, i32], op=ALU.add),
                                     reads=[rT1, rZs[jb]], writes=[rXH])
                            S.op("dve", lambda: nc.vector.tensor_copy(Xs[:], XH[:, :, :, :, 32]), reads=[rXH],
                                 writes=[rXs])

                        def step_d(j):
                            jb = j % 2
                            S.op("act", lambda: nc.scalar.copy(Xh[:, jb, 0, :, :, :], XH[:, 0, 0:2, :, 0:32]),
                                 reads=[rXH], writes=[rXh[jb]])
                            for i32 in range(32):
                                S.op("act", lambda: nc.scalar.copy(Xh[:, jb, 1, :, :, 31 - i32],
                                                                   XH[:, 1, 0:2, :, i32]),
                                     reads=[rXH], writes=[rXh[jb]])

                        def step_e(j):
                            jb = j % 2
                            tiles = (j, NS - 1 - j)
                            for d_ in range(2):
                                un = 0 if d_ == 0 else 2
                                tok0 = tiles[d_] * 256
                                for gb in range(GH // 16):
                                    yt, ryt = Ytok.next()
                                    for gq2 in range(2):
                                        for g8 in range(8):
                                            g_ = gb * 16 + gq2 * 8 + g8
                                            o_ = psY[:, g8 * 32:(g8 + 1) * 32]
                                            S.op("pe", lambda: nc.tensor.matmul(o_, Tmat[:, d_, g_, :],
                                                                                U2[:, jb, un, g_, :], start=True,
                                                                                stop=False),
                                                 reads=[rTmat, rU2[jb][un]], writes=[rpsY], inc=False)
                                            S.op("pe", lambda: nc.tensor.matmul(o_, Fmat[:, d_, 0, g_, :],
                                                                                Xh[:, jb, d_, 0, g_, :], start=False,
                                                                                stop=False),
                                                 reads=[rFm, rXh[jb]], writes=[rpsY], inc=False)
                                            S.op("pe", lambda: nc.tensor.matmul(o_, Fmat[:, d_, 1, g_, :],
                                                                                Xh[:, jb, d_, 1, g_, :], start=False,
                                                                                stop=True),
                                                 reads=[rFm, rXh[jb]], writes=[rpsY], inc=(g8 == 7))
                                        ys, rys = Ysb.next()
                                        S.op("act", lambda: nc.scalar.copy(ys[:], psY[:, 0:256]), reads=[rpsY],
                                             writes=[rys])
                                        for g8 in range(8):
                                            S.op("pe", lambda: nc.tensor.transpose(
                                                psYT[:, g8 * 128:(g8 + 1) * 128], ys[:, g8 * 32:(g8 + 1) * 32],
                                                ident[:]), reads=[rys, rconst], writes=[rpsYT], inc=(g8 == 7))
                                        S.op("act", lambda: nc.scalar.copy(
                                            yt[:, :, gq2 * 128:(gq2 + 1) * 128].rearrange("n t (g c) -> n g t c", c=16),
                                            psYT[:].rearrange("n (g t c) -> n g t c", g=8, t=8)),
                                            reads=[rpsYT], writes=[ryt])
                                    c0 = (g0 + gb * 16) * 16
                                    S.dma("sp", yfb_d[d_, tok0:tok0 + 256, c0:c0 + 256].rearrange(
                                        "(n s) c -> n s c", s=8), yt[:], reads=[ryt])

                        step_ab(0)
                        for j in range(NS):
                            if j + 1 < NS:
                                step_ab(j + 1)
                            step_c(j)
                            step_d(j)
                            step_e(j)
                        S.barrier()
                S.barrier()

        def phase3b(l):
            GC = float(np.sqrt(2.0 / np.pi))
            with contextlib.ExitStack() as ph:
                sbt = lambda n, s_, d: ph.enter_context(nc.sbuf_tensor(uname(n), s_, d))
                pst = lambda n, s_, d: ph.enter_context(nc.psum_tensor(uname(n), s_, d))
                Wg = sbt("Wg", [128, 8, 1024], BF16); rW = Res()
                S.dma("pool", Wg[:], W["s5_w_glu"][l].rearrange("(k p) f -> p k f", p=128), writes=[rW])
                dvec = sbt("dvec", [128, 1024], F32)
                S.dma("sp", dvec[:], W["s5_d"][l].rearrange("(o f) -> o f", o=1).partition_broadcast(128), writes=[rW])
                bg = sbt("bg", [128, 8], F32); og = sbt("og", [128, 8], F32)
                load_pc(bg[:], W["s5_b_glu"][l], rW)
                load_pc(og[:], W["s5_out_norm"][l], rW)
                S.op("dve", lambda: nc.vector.tensor_single_scalar(bg[:], bg[:], -1.0, ALU.mult), reads=[rW],
                     writes=[rW])
                ld = Rot([(sbt("ld3_%d" % i_, [128, 3, 1024], F32), Res()) for i_ in range(2)])
                ya = sbt("ya", [128, 1024], F32); rya = Res()
                yb_ = sbt("yb_", [128, 1024], F32); ryb = Res()
                glb = Rot([(sbt("glb%d" % i_, [128, 1024], BF16), Res()) for i_ in range(2)])
                glT = sbt("glT", [128, 8, TT], BF16); rglT = Res()
                cT = sbt("cT", [128, 8, TT], F32); rcT = [Res() for _ in range(8)]
                et = Rot([(sbt("et%d" % i_, [128, TT], F32), Res()) for i_ in range(2)])
                sq = Rot([(sbt("sq3_%d" % i_, [128, TT], BF16), Res()) for i_ in range(2)])
                rstd = sbt("rstd3", [128, TT], F32); rrstd = Res()
                cst = Rot([(sbt("cst%d" % i_, [128, TT], BF16), Res()) for i_ in range(2)])
                psG = Rot([(pst("psG%d" % i_, [128, 1024], BF16), Res()) for i_ in range(2)])
                psM = Rot([(pst("psM%d" % i_, [128, TT], F32), Res()) for i_ in range(2)])
                pstat = pst("pstat3", [128, TT], F32); rpstat = Res()
                for it in range(NT):
                    t0 = it * TT
                    for sub in range(4):
                        tk = t0 + sub * 128
                        lt, rlt = ld.next()
                        S.dma("sp", lt[:, 0, :], yfb_d[0, tk:tk + 128, :], writes=[rlt])
                        S.dma("sp", lt[:, 1, :], yfb_d[1, tk:tk + 128, :], writes=[rlt])
                        S.dma("sp", lt[:, 2, :], su_d[tk:tk + 128, :], writes=[rlt])
                        S.op("dve", lambda: nc.vector.tensor_tensor(out=ya[:], in0=lt[:, 2, :], in1=dvec[:],
                                                                    op=ALU.mult), reads=[rlt, rW], writes=[rya])
                        S.op("dve", lambda: nc.vector.tensor_tensor(out=ya[:], in0=ya[:], in1=lt[:, 0, :], op=ALU.add),
                             reads=[rya, rlt], writes=[rya])
                        S.op("dve", lambda: nc.vector.tensor_tensor(out=ya[:], in0=ya[:], in1=lt[:, 1, :], op=ALU.add),
                             reads=[rya, rlt], writes=[rya])
                        S.op("dve", lambda: nc.vector.tensor_tensor(out=yb_[:], in0=ya[:], in1=ya[:], op=ALU.mult),
                             reads=[rya], writes=[ryb])
                        S.op("dve", lambda: nc.vector.tensor_scalar(yb_[:], yb_[:], 0.044715, 1.0, ALU.mult, ALU.add),
                             reads=[ryb], writes=[ryb])
                        S.op("dve", lambda: nc.vector.tensor_tensor(out=yb_[:], in0=yb_[:], in1=ya[:], op=ALU.mult),
                             reads=[ryb, rya], writes=[ryb])
                        S.op("dve", lambda: nc.vector.tensor_single_scalar(yb_[:], yb_[:], -30.0, ALU.max),
                             reads=[ryb], writes=[ryb])
                        S.op("act", lambda: nc.scalar.activation(out=yb_[:], in_=yb_[:], func=AF.Exp,
                                                                 scale=-2.0 * GC), reads=[ryb], writes=[ryb])
                        S.op("dve", lambda: nc.vector.tensor_single_scalar(yb_[:], yb_[:], 1.0, ALU.add),
                             reads=[ryb], writes=[ryb])
                        S.op("dve", lambda: nc.vector.reciprocal(yb_[:], yb_[:]), reads=[ryb], writes=[ryb])
                        gl, rgl = glb.next()
                        S.op("dve", lambda: nc.vector.tensor_tensor(out=gl[:], in0=ya[:], in1=yb_[:], op=ALU.mult),
                             reads=[rya, ryb], writes=[rgl])
                        pg, rpg = psG.next()
                        for k in range(8):
                            S.op("pe", lambda: nc.tensor.transpose(pg[:, k * 128:(k + 1) * 128],
                                                                   gl[:, k * 128:(k + 1) * 128], ident_b[:]),
                                 reads=[rgl, rconst], writes=[rpg], inc=(k == 7))
                        S.op("act", lambda: nc.scalar.copy(glT[:, :, sub * 128:(sub + 1) * 128],
                                                           pg[:].rearrange("p (k t) -> p k t", k=8)),
                             reads=[rpg], writes=[rglT])
                    for oc in range(8):
                        pm, rpm = psM.next()
                        for k in range(8):
                            S.op("pe", lambda: nc.tensor.matmul(pm[:], Wg[:, k, oc * 128:(oc + 1) * 128], glT[:, k, :],
                                                                start=(k == 0), stop=(k == 7)),
                                 reads=[rW, rglT], writes=[rpm], inc=(k == 7))
                        e_, re_ = et.next()
                        S.op("act", lambda: nc.scalar.activation(out=e_[:], in_=pm[:], func=AF.Exp, scale=-1.0,
                                                                 bias=bg[:, oc:oc + 1]), reads=[rpm, rW], writes=[re_])
                        S.op("dve", lambda: nc.vector.tensor_single_scalar(e_[:], e_[:], 1.0, ALU.add), reads=[re_],
                             writes=[re_])
                        S.op("dve", lambda: nc.vector.reciprocal(e_[:], e_[:]), reads=[re_], writes=[re_])
                        S.op("dve", lambda: nc.vector.tensor_tensor(out=cT[:, oc, :], in0=glT[:, oc, :], in1=e_[:],
                                                                    op=ALU.mult), reads=[rglT, re_], writes=[rcT[oc]])
                        sq_, rsq_ = sq.next()
                        S.op("act", lambda: nc.scalar.activation(out=sq_[:], in_=cT[:, oc, :], func=AF.Square),
                             reads=[rcT[oc]], writes=[rsq_])
                        S.op("pe", lambda: nc.tensor.matmul(pstat[:], ones_b[:], sq_[:], start=(oc == 0),
                                                            stop=(oc == 7)), reads=[rsq_, rconst], writes=[rpstat])
                    S.op("act", lambda: nc.scalar.activation(out=rstd[:], in_=pstat[:], func=AF.Ln, bias=epsb[:],
                                                             scale=1.0 / 1024), reads=[rpstat, rconst], writes=[rrstd])
                    S.op("act", lambda: nc.scalar.activation(out=rstd[:], in_=rstd[:], func=AF.Exp, scale=-0.5),
                         reads=[rrstd], writes=[rrstd])
                    for oc in range(8):
                        cs_, rcs_ = cst.next()
                        S.op("dve", lambda: nc.vector.scalar_tensor_tensor(
                            out=cs_[:], in0=cT[:, oc, :], scalar=og[:, oc:oc + 1], in1=rstd[:], op0=ALU.mult,
                            op1=ALU.mult), reads=[rcT[oc], rW, rrstd], writes=[rcs_])
                        S.dma("sp", mixT_d[1024 + oc * 128:1024 + (oc + 1) * 128, t0:t0 + TT], cs_[:], reads=[rcs_])
                S.barrier()

        def phase4a(l):
            with contextlib.ExitStack() as ph:
                sbt = lambda n, s_, d: ph.enter_context(nc.sbuf_tensor(uname(n), s_, d))
                pst = lambda n, s_, d: ph.enter_context(nc.psum_tensor(uname(n), s_, d))
                NTB = T // 128
                HR = NROW // 2
                qT = sbt("qTs", [128, 4, T], BF16)
                kT = sbt("kTs", [128, 4, T], BF16)
                va = sbt("va", [128, NTB, 512], BF16)
                vb = sbt("vb", [128, NTB - 1, 512], BF16)
                rin = Res()
                for c in range(4):
                    S.dma("sp", qT[:, c, :], qT_d[c * 128:(c + 1) * 128, :], writes=[rin])
                    S.dma("sp", kT[:, c, :], kT_d[c * 128:(c + 1) * 128, :], writes=[rin])
                S.dma("sp", va[:], v_d.rearrange("(n p) c -> p n c", p=128), writes=[rin])
                S.dma("sp", vb[:], v_d[64:T - 64, :].rearrange("(n p) c -> p n c", p=128), writes=[rin])
                biasE = Rot([(sbt("biasE%d" % i_, [128, 4096], F32), Res()) for i_ in range(2)])
                biasB = Rot([(sbt("biasB%d" % i_, [128, 4096], BF16), Res()) for i_ in range(2)])
                bias4 = sbt("bias4", [128, 4096], BF16)
                rb4 = Res()
                b4f, rb4f = biasE.next()
                S.dma("sp", b4f[0:64, :], C["nabias"][l, 4], writes=[rb4f])
                S.dma("sp", b4f[64:128, :], C["nabias"][l, 4], writes=[rb4f])
                S.op("act", lambda: nc.scalar.copy(bias4[:], b4f[:]), reads=[rb4f], writes=[rb4])
                gainb = sbt("gainb", [64, 512], F32)
                S.dma("sp", gainb[:], W["attn_out_norm"][l].rearrange("(o f) -> o f", o=1).partition_broadcast(64),
                      writes=[rin])
                Pm = Rot([(sbt("Pm%d" % i_, [64, 512], BF16), Res()) for i_ in range(2)])
                PT = Rot([(sbt("PT%d" % i_, [128, 4, 64], BF16), Res()) for i_ in range(2)])
                stat = Rot([(sbt("nst%d" % i_, [64, 4], F32), Res()) for i_ in range(4)])
                araw = [(sbt("araw%d" % i_, [64, 512], F32), Res()) for i_ in range(2)]
                anb = sbt("anb", [64, 512], BF16); ranb = Res()
                junk = sbt("junk", [64, 512], BF16); rjunk = Res()
                nst2 = sbt("nst2", [64, 2], F32); rnst2 = Res()
                aT = Rot([(sbt("aT%d" % i_, [128, 4, TT], BF16), Res()) for i_ in range(2)])
                ps_S = Rot([(pst("psS%d" % i_, [64, 512], F32), Res()) for i_ in range(2)])
                ps_T = Rot([(pst("psT%d" % i_, [128, 1024], BF16), Res()) for i_ in range(2)])
                ps_O = [(pst("psO%d" % i_, [64, 512], F32), Res()) for i_ in range(2)]
                ps_A = pst("psA", [128, 1024], BF16); rps_A = Res()

                def attend(r, rs, vi):
                    dl = r - rs
                    if dl == 4:
                        bt, rbt = bias4, rb4
                    else:
                        bf_, rbf_ = biasE.next()
                        S.dma("sp", bf_[0:64, :], C["nabias"][l, dl], writes=[rbf_])
                        S.dma("sp", bf_[64:128, :], C["nabias"][l, dl], writes=[rbf_])
                        bt, rbt = biasB.next()
                        S.op("act", lambda: nc.scalar.copy(bt[:], bf_[:]), reads=[rbf_], writes=[rbt])
                    po, rpo = ps_O[vi]
                    ar, rar = araw[vi]
                    for hh in range(8):
                        c = hh // 2
                        pb0 = (hh % 2) * 64
                        pS, rpS = ps_S.next()
                        S.op("pe", lambda: nc.tensor.matmul(pS[:], qT[pb0:pb0 + 64, c, r * 64:(r + 1) * 64],
                                                            kT[pb0:pb0 + 64, c, rs * 64:(rs + 8) * 64], start=True,
                                                            stop=False),
                             reads=[rin], writes=[rpS], inc=False)
                        S.op("pe", lambda: nc.tensor.matmul(pS[:], ident_b[pb0:pb0 + 64, pb0:pb0 + 64],
                                                            bt[pb0:pb0 + 64, hh * 512:(hh + 1) * 512], start=False,
                                                            stop=True),
                             reads=[rbt, rconst], writes=[rpS])
                        st_, rst_ = stat.next()
                        pm, rpm = Pm.next()
                        S.op("act", lambda: nc.scalar.activation(out=pm[:], in_=pS[:], func=AF.Exp,
                                                                 accum_out=st_[:, 2:3]),
                             reads=[rpS], writes=[rpm, rst_])
                        pT, rpT = ps_T.next()
                        for j in range(4):
                            S.op("pe", lambda: nc.tensor.transpose(pT[:, j * 64:(j + 1) * 64],
                                                                   pm[:, j * 128:(j + 1) * 128], ident_b[0:64, 0:64]),
                                 reads=[rpm, rconst], writes=[rpT], inc=(j == 3))
                        pt_, rpt_ = PT.next()
                        S.op("act", lambda: nc.scalar.copy(pt_[:].rearrange("p a b -> p (a b)"), pT[:, 0:256]),
                             reads=[rpT], writes=[rpt_])
                        for j in range(4):
                            if rs % 2 == 0:
                                vsrc = va[:, rs // 2 + j, hh * 64:(hh + 1) * 64]
                            else:
                                vsrc = vb[:, (rs - 1) // 2 + j, hh * 64:(hh + 1) * 64]
                            S.op("pe", lambda: nc.tensor.matmul(po[:, hh * 64:(hh + 1) * 64], pt_[:, j, :], vsrc,
                                                                start=(j == 0), stop=(j == 3)),
                                 reads=[rpt_, rin], writes=[rpo], inc=(j == 3))
                        S.op("dve", lambda: nc.vector.reciprocal(st_[:, 3:4], st_[:, 2:3]), reads=[rst_],
                             writes=[rst_])
                        S.op("dve", lambda: nc.vector.tensor_scalar(ar[:, hh * 64:(hh + 1) * 64],
                                                                    po[:, hh * 64:(hh + 1) * 64], st_[:, 3:4], None,
                                                                    ALU.mult),
                             reads=[rpo, rst_], writes=[rar])

                for it in range(NT):
                    t0 = it * TT
                    at, rat = aT.next()
                    for r8 in range(8):
                        r = it * 8 + r8
                        rs_s = min(max(r - 4, 0), NROW - 8)
                        base = 0 if r < HR else HR
                        rs_p = base + min(max(r - base - 4, 0), HR - 8)
                        attend(r, rs_s, 0)
                        ar, rar = araw[0]
                        if rs_p != rs_s:
                            attend(r, rs_p, 1)
                            ap_, rap = araw[1]
                            S.op("dve", lambda: nc.vector.tensor_tensor(out=ar[:], in0=ar[:], in1=ap_[:],
                                                                        op=ALU.subtract),
                                 reads=[rar, rap], writes=[rar])
                            S.op("dve", lambda: nc.vector.scalar_tensor_tensor(
                                out=ar[:], in0=ar[:], scalar=linkt[0:64, 0:1], in1=ap_[:], op0=ALU.mult,
                                op1=ALU.add), reads=[rar, rap, rconst], writes=[rar])
                        S.op("act", lambda: nc.scalar.activation(out=junk[:], in_=ar[:], func=AF.Square,
                                                                 accum_out=nst2[:, 0:1]),
                             reads=[rar], writes=[rjunk, rnst2])
                        S.op("act", lambda: nc.scalar.activation(out=nst2[:, 1:2], in_=nst2[:, 0:1], func=AF.Ln,
                                                                 bias=epsb[0:64, :], scale=1.0 / 512),
                             reads=[rnst2, rconst], writes=[rnst2])
                        S.op("act", lambda: nc.scalar.activation(out=nst2[:, 1:2], in_=nst2[:, 1:2], func=AF.Exp,
                                                                 scale=-0.5),
                             reads=[rnst2], writes=[rnst2])
                        S.op("dve", lambda: nc.vector.scalar_tensor_tensor(
                            out=anb[:], in0=ar[:], scalar=nst2[:, 1:2], in1=gainb[:], op0=ALU.mult, op1=ALU.mult),
                            reads=[rar, rnst2, rin], writes=[ranb])
                        for c in range(4):
                            S.op("pe", lambda: nc.tensor.transpose(ps_A[:, c * 64:(c + 1) * 64],
                                                                   anb[:, c * 128:(c + 1) * 128],
                                                                   ident_b[0:64, 0:64]),
                                 reads=[ranb, rconst], writes=[rps_A], inc=(c == 3))
                        S.op("act", lambda: nc.scalar.copy(at[:, :, r8 * 64:(r8 + 1) * 64],
                                                           ps_A[:, 0:256].rearrange("p (c t) -> p c t", c=4)),
                             reads=[rps_A], writes=[rat])
                    S.dma("sp", mixT_d[0:512, t0:t0 + TT].rearrange("(c p) t -> p c t", p=128), at[:], reads=[rat])
                S.barrier()

        def phase4b(l):
            with contextlib.ExitStack() as ph:
                tl = TL(ph)
                sbt = tl.sbt
                x, rx, h, rh = tl.x, tl.rx, tl.h, tl.rh
                load_pc(tl.gains[:, 0, :], W["ffn2_norm"][l], tl.rgains)
                load_pc(tl.gains[:, 1, :], W["final_norm"][l], tl.rgains)
                last = (l == L - 1)
                if last:
                    yt_rot = Rot([(sbt("yt%d" % i, [128, D], F32), Res()) for i in range(2)])
                wov = W["w_out"][l].rearrange("(k p) f -> p k f", p=128)
                for it in range(NT):
                    t0 = it * TT
                    tl.load_x(it)
                    S.dma("sp", h[:], mixT_d[:, t0:t0 + TT].rearrange("(c p) t -> p c t", p=128), writes=[rh])
                    NB = D // 256
                    nxt = tl.wgu_rot.next()
                    S.dma("pool", nxt[0][:], wov[:, :, 0:256], writes=[nxt[1]])
                    for b in range(NB):
                        wb, rwb = nxt
                        if b + 1 < NB:
                            nxt = tl.wgu_rot.next()
                            S.dma("pool", nxt[0][:], wov[:, :, (b + 1) * 256:(b + 2) * 256], writes=[nxt[1]])
                        for jj in range(2):
                            i = 2 * b + jj
                            ps, rps = tl.pb_rot.next()
                            for k in range(NKC):
                                S.op("pe", lambda: nc.tensor.matmul(ps[:], wb[:, k, jj * 128:(jj + 1) * 128],
                                                                    h[:, k, :], start=(k == 0), stop=(k == NKC - 1)),
                                     reads=[rwb, rh], writes=[rps], inc=(k == NKC - 1))
                            S.op("dve", lambda: nc.vector.tensor_tensor(out=x[:, i, :], in0=ps[:], in1=x[:, i, :],
                                                                        op=ALU.add),
                                 reads=[rps, rx[i]], writes=[rx[i]])
                    tl.rmsnorm(0)
                    tl.ffn(W["ffn2_w_gate"][l], W["ffn2_w_up"][l], W["ffn2_w_down"][l])
                    tl.rmsnorm(1, inplace=True)
                    if not last:
                        tl.store_x(it)
                    else:
                        for tb in range(4):
                            yt, ryt = yt_rot.next()
                            for cq in range(4):
                                pb, rpb = tl.pb_rot.next()
                                for i4 in range(4):
                                    c = cq * 4 + i4
                                    S.op("pe", lambda: nc.tensor.transpose(pb[:, i4 * 128:(i4 + 1) * 128],
                                                                           x[:, c, tb * 128:(tb + 1) * 128],
                                                                           ident[:]),
                                         reads=[rx[c], rconst], writes=[rpb], inc=(i4 == 3))
                                if cq % 2 == 0:
                                    S.op("act", lambda: nc.scalar.copy(yt[:, cq * 512:(cq + 1) * 512], pb[:]),
                                         reads=[rpb], writes=[ryt])
                                else:
                                    S.op("dve", lambda: nc.vector.tensor_copy(yt[:, cq * 512:(cq + 1) * 512], pb[:]),
                                         reads=[rpb], writes=[ryt])
                            S.dma("sp", y_out[t0 + tb * 128:t0 + (tb + 1) * 128, :], yt[:], reads=[ryt])
                S.barrier()

        for l in range(L):
            if "p1" in phases:
                phase1(l)
            if "p2" in phases:
                phase2(l)
            if "p3a" in phases:
                phase3a(l)
            if "p3b" in phases:
                phase3b(l)
            if "p4a" in phases:
                phase4a(l)
            if "p4b" in phases:
                phase4b(l)
    return nc


T_CORE = 4096
DEPTH = 4
ALL_PHASES = ("p1", "p2", "p3a", "p3b", "p4a", "p4b")


def kernel(**inputs):
    xp = np.ascontiguousarray(np.asarray(inputs["x_prompt"], dtype=np.float32))
    xs = np.ascontiguousarray(np.asarray(inputs["x_sample"], dtype=np.float32))
    wts = {n: np.ascontiguousarray(np.asarray(inputs[n], dtype=np.float32)) for n, _ in WSPECS}
    rel_bias = np.asarray(inputs["rel_bias"], dtype=np.float32)
    nc = build(T_CORE, DEPTH, dbg=False, phases=ALL_PHASES)
    consts = [host_consts(T_CORE, DEPTH, rel_bias, 0), host_consts(T_CORE, DEPTH, rel_bias, 1)]
    in_maps = []
    for core in range(8):
        if core < 4:
            x = xp[2 * core:2 * core + 2].reshape(T_CORE, D)
            cst = consts[0]
        else:
            x = xs[core - 4].reshape(T_CORE, D)
            cst = consts[1]
        m = {"x": x}
        m.update(wts)
        m.update(cst)
        in_maps.append(m)
    res = run_bass_kernel_spmd(nc, in_maps, core_ids=list(range(8)))
    outs = [np.asarray(r["y"], dtype=np.float32) for r in res.results]
    y_prompt = np.concatenate([o.reshape(2, 2048, D) for o in outs[:4]], axis=0)
    y_sample = np.stack([o.reshape(4096, D) for o in outs[4:]], axis=0)
    return (y_prompt, y_sample)
```
